# Optimizing a Trainium2 kernel written in Bass

```python
import jax, jax.numpy as jnp
from jax import lax
import numpy as np

D_MODEL = 1024
BATCH = 4
SEQ = 4096
DEPTH = 4

GRID_W = 64
CTX_LEN = 256
EPS = 1e-6
N_MOD = 6
SGU_WIDTH = 1024
SGU_CHUNK = 128
SGU_GROUPS = 8
SGU_GROUP_DIM = SGU_WIDTH // SGU_GROUPS
GLA_HEADS = 4
GLA_KEY_DIM = D_MODEL // 2
GLA_VAL_DIM = D_MODEL
GLA_HEAD_K = GLA_KEY_DIM // GLA_HEADS
GLA_HEAD_V = GLA_VAL_DIM // GLA_HEADS
GLA_RANK = 16
GLA_GATE_NORMALIZER = 16.0
GLA_CHUNK = 64
D_FF = 2816
CONV_K = 3

IN_SPLITS = (SGU_WIDTH, SGU_WIDTH, GLA_KEY_DIM, GLA_VAL_DIM, D_MODEL, D_MODEL, GLA_KEY_DIM, GLA_VAL_DIM, 2 * GLA_RANK)
D_IN = sum(IN_SPLITS)
OFF_KVG = sum(IN_SPLITS[:6])

kernel_name = "hybrid_sgu_gla_convglu_prefix_dit"


def _split(z, sizes):
    offs = [int(o) for o in np.cumsum(sizes)[:-1]]
    return jnp.split(z, offs, axis=-1)


def _rms_norm(x, g):
    xf = x.astype(jnp.float32)
    y = xf * lax.rsqrt(jnp.mean(xf * xf, axis=-1, keepdims=True) + EPS)
    return (y * g.astype(jnp.float32)).astype(x.dtype)


def _layer_norm(x, g, b):
    xf = x.astype(jnp.float32)
    mu = jnp.mean(xf, axis=-1, keepdims=True)
    var = jnp.mean(jnp.square(xf - mu), axis=-1, keepdims=True)
    y = (xf - mu) * lax.rsqrt(var + EPS)
    return (y * g.astype(jnp.float32) + b.astype(jnp.float32)).astype(x.dtype)


def _modulate(h, shift, scale):
    return h * (1 + scale) + shift


def _spatial_gating(u, vs, ln_g, ln_b, w_s, b_s):
    bsz, length, _ = u.shape
    n = length // SGU_CHUNK
    vn = _layer_norm(vs, ln_g, ln_b).reshape(bsz, n, SGU_CHUNK, SGU_GROUPS, SGU_GROUP_DIM)
    s = jnp.einsum('gpq,bnqgc->bnpgc', w_s, vn) + b_s.T[:, :, None]
    return u * s.reshape(bsz, length, SGU_WIDTH)


def _heads(a, hd):
    return a.reshape(a.shape[0], a.shape[1], GLA_HEADS, hd).astype(jnp.float32)


def _log_decay(lr, w2, b2):
    z = (lr @ w2 + b2).astype(jnp.float32)
    la = jax.nn.log_sigmoid(z) / GLA_GATE_NORMALIZER
    return la.reshape(lr.shape[0], lr.shape[1], GLA_HEADS, GLA_HEAD_K)


def _gla_direction(q, k, v, log_a, s0):
    bsz, t, h, _ = k.shape
    n = t // GLA_CHUNK

    def chunks(a):
        return a.reshape(bsz, n, GLA_CHUNK, h, a.shape[-1])

    k, v, log_a = chunks(k), chunks(v), chunks(log_a)
    b = jnp.cumsum(log_a, axis=2)
    b_last = b[:, :, -1:]
    k_dec = k * jnp.exp(b_last - b)
    cm = lambda a: jnp.moveaxis(a, 1, 0)
    xs_state = (cm(k_dec), cm(v), cm(jnp.exp(b_last[:, :, 0])))

    if q is None:
        def step_state(s, xs):
            kd, vv, dec = xs
            return dec[..., None] * s + jnp.einsum('bchk,bchv->bhkv', kd, vv), None
        s_final, _ = lax.scan(step_state, s0, xs_state)
        return None, s_final

    q = chunks(q)
    q_dec = q * jnp.exp(b)
    k_inv = k * jnp.exp(-b)
    mask = jnp.tril(jnp.ones((GLA_CHUNK, GLA_CHUNK), dtype=bool))
    scores = jnp.where(mask, jnp.einsum('bnchk,bnshk->bnhcs', q_dec, k_inv), 0.0)
    o_intra = jnp.einsum('bnhcs,bnshv->bnchv', scores, v)

    def step(s, xs):
        qd, kd, vv, dec = xs
        o = jnp.einsum('bchk,bhkv->bchv', qd, s)
        return dec[..., None] * s + jnp.einsum('bchk,bchv->bhkv', kd, vv), o

    s_final, o_inter = lax.scan(step, s0, (cm(q_dec),) + xs_state)
    o = o_intra + jnp.moveaxis(o_inter, 0, 1)
    return o.reshape(bsz, t, h, -1), s_final


def _gla_bidir(q, k, v, lr, w2, b2, s0f, s0b):
    la_f = _log_decay(lr[..., :GLA_RANK], w2[0], b2[0])
    la_b = _log_decay(lr[..., GLA_RANK:], w2[1], b2[1])
    flip = lambda a: None if a is None else a[:, ::-1]
    o_f, s_f = _gla_direction(q, k, v, la_f, s0f)
    o_b, s_b = _gla_direction(flip(q), flip(k), flip(v), flip(la_b), s0b)
    o = None if q is None else o_f + flip(o_b)
    return o, s_f, s_b


def _token_mixer(h, w_in, sgu_ln_g, sgu_ln_b, sgu_w, sgu_b, gla_w2, gla_b2, gla_norm_g,
                 w_br_a, w_br_b, w_o, s0f, s0b):
    z = h @ w_in
    u, vs, q, r, ga, gb, k, v, lr = _split(z, IN_SPLITS)
    a = _spatial_gating(jax.nn.gelu(u, approximate=False), jax.nn.gelu(vs, approximate=False),
                        sgu_ln_g, sgu_ln_b, sgu_w, sgu_b)
    qh = _heads(q, GLA_HEAD_K) * (GLA_HEAD_K ** -0.5)
    o, s_f, s_b = _gla_bidir(qh, _heads(k, GLA_HEAD_K), _heads(v, GLA_HEAD_V), lr, gla_w2, gla_b2, s0f, s0b)
    o = o * lax.rsqrt(jnp.mean(o * o, axis=-1, keepdims=True) + EPS)
    o = o.reshape(h.shape[0], h.shape[1], GLA_VAL_DIM) * gla_norm_g.astype(jnp.float32)
    ob = o.astype(h.dtype) * jax.nn.silu(r)
    merged = jax.nn.sigmoid(ga) * (a @ w_br_a) + jax.nn.sigmoid(gb) * (ob @ w_br_b)
    return merged @ w_o, s_f, s_b


def _context_states(hc, w_in, gla_w2, gla_b2, s0):
    k, v, lr = _split(hc @ w_in[:, OFF_KVG:], IN_SPLITS[6:])
    _, s_f, s_b = _gla_bidir(None, _heads(k, GLA_HEAD_K), _heads(v, GLA_HEAD_V), lr, gla_w2, gla_b2, s0, s0)
    return s_f, s_b


def _conv_ffn(h, rows, w_up, conv_w, conv_b, w_down):
    bsz, length, _ = h.shape
    a, val = _split(h @ w_up, (D_FF, D_FF))
    img = a.reshape(bsz, rows, length // rows, D_FF)
    a = lax.conv_general_dilated(img, conv_w[:, :, None, :], (1, 1), 'SAME',
                                 dimension_numbers=('NHWC', 'HWIO', 'NHWC'),
                                 feature_group_count=D_FF) + conv_b
    return (jax.nn.gelu(a.reshape(bsz, length, D_FF), approximate=False) * val) @ w_down


def setup_inputs(seed: int = 0) -> dict:
    key = jax.random.key(seed)
    ks = jax.random.split(key, 32)

    def nrm(k, shape, scale):
        return jax.random.normal(k, shape, jnp.float32) * scale

    def gain(k, shape):
        return 1.0 + nrm(k, shape, 0.1)

    D = D_MODEL
    return {
        "x": nrm(ks[0], (BATCH, SEQ, D), 1.0),
        "c": nrm(ks[1], (BATCH, D), 1.0),
        "ctx": nrm(ks[2], (BATCH, CTX_LEN, D), 1.0),
        "c_ctx": nrm(ks[3], (D,), 1.0),
        "w_ada": nrm(ks[4], (DEPTH, D, N_MOD * D), 0.5 * D ** -0.5),
        "b_ada": nrm(ks[5], (DEPTH, N_MOD * D), 0.01),
        "norm1_g": gain(ks[6], (DEPTH, D)),
        "norm2_g": gain(ks[7], (DEPTH, D)),
        "w_in": nrm(ks[8], (DEPTH, D, D_IN), D ** -0.5),
        "sgu_ln_g": gain(ks[9], (DEPTH, SGU_WIDTH)),
        "sgu_ln_b": nrm(ks[10], (DEPTH, SGU_WIDTH), 0.01),
        "sgu_w": nrm(ks[11], (DEPTH, SGU_GROUPS, SGU_CHUNK, SGU_CHUNK), 0.5 * SGU_CHUNK ** -0.5),
        "sgu_b": gain(ks[12], (DEPTH, SGU_GROUPS, SGU_CHUNK)),
        "gla_w2": nrm(ks[13], (DEPTH, 2, GLA_RANK, GLA_KEY_DIM), GLA_RANK ** -0.5),
        "gla_b2": nrm(ks[14], (DEPTH, 2, GLA_KEY_DIM), 0.1),
        "gla_norm_g": gain(ks[15], (DEPTH, GLA_VAL_DIM)),
        "w_br_a": nrm(ks[16], (DEPTH, SGU_WIDTH, D), SGU_WIDTH ** -0.5),
        "w_br_b": nrm(ks[17], (DEPTH, GLA_VAL_DIM, D), GLA_VAL_DIM ** -0.5),
        "w_o": nrm(ks[18], (DEPTH, D, D), D ** -0.5),
        "ffn_w_up": nrm(ks[19], (DEPTH, D, 2 * D_FF), D ** -0.5),
        "ffn_conv_w": nrm(ks[20], (DEPTH, CONV_K, CONV_K, D_FF), 1.0 / CONV_K),
        "ffn_conv_b": nrm(ks[21], (DEPTH, D_FF), 0.01),
        "ffn_w_down": nrm(ks[22], (DEPTH, D_FF, D), D_FF ** -0.5),
        "final_norm_g": gain(ks[23], (D,)),
    }


def reference(x, c, ctx, c_ctx, w_ada, b_ada, norm1_g, norm2_g, w_in, sgu_ln_g, sgu_ln_b, sgu_w, sgu_b,
              gla_w2, gla_b2, gla_norm_g, w_br_a, w_br_b, w_o, ffn_w_up, ffn_conv_w, ffn_conv_b,
              ffn_w_down, final_norm_g):
    bsz, seq, _ = x.shape
    rows = seq // GRID_W
    s0 = jnp.zeros((bsz, GLA_HEADS, GLA_HEAD_K, GLA_HEAD_V), jnp.float32)
    for l in range(DEPTH):
        last = l == DEPTH - 1
        sh1x, sc1x, g1x, sh2x, sc2x, g2x = jnp.split((jax.nn.silu(c) @ w_ada[l] + b_ada[l])[:, None, :], N_MOD, axis=-1)
        sh1c, sc1c, g1c, sh2c, sc2c, g2c = jnp.split(jax.nn.silu(c_ctx) @ w_ada[l] + b_ada[l], N_MOD, axis=-1)
        tm = (w_in[l], sgu_ln_g[l], sgu_ln_b[l], sgu_w[l], sgu_b[l], gla_w2[l], gla_b2[l], gla_norm_g[l],
              w_br_a[l], w_br_b[l], w_o[l])
        hc = _modulate(_rms_norm(ctx, norm1_g[l]), sh1c, sc1c)
        if last:
            s_f, s_b = _context_states(hc, w_in[l], gla_w2[l], gla_b2[l], s0)
        else:
            y_c, s_f, s_b = _token_mixer(hc, *tm, s0, s0)
        hx = _modulate(_rms_norm(x, norm1_g[l]), sh1x, sc1x)
        y_x, _, _ = _token_mixer(hx, *tm, s_f, s_b)
        x = x + g1x * y_x
        hx2 = _modulate(_rms_norm(x, norm2_g[l]), sh2x, sc2x)
        x = x + g2x * _conv_ffn(hx2, rows, ffn_w_up[l], ffn_conv_w[l], ffn_conv_b[l], ffn_w_down[l])
        if not last:
            ctx = ctx + g1c * y_c
            hc2 = _modulate(_rms_norm(ctx, norm2_g[l]), sh2c, sc2c)
            ctx = ctx + g2c * _conv_ffn(hc2, 1, ffn_w_up[l], ffn_conv_w[l], ffn_conv_b[l], ffn_w_down[l])
    return _rms_norm(x, final_norm_g)
```

```python
import contextlib
import types
import numpy as np
import concourse.bass as bass
import concourse.mybir as mybir
from concourse.bass_utils import run_bass_kernel_spmd

F32 = mybir.dt.float32
BF16 = mybir.dt.bfloat16
AF = mybir.ActivationFunctionType
ALU = mybir.AluOpType

D = 1024
SEQ = 4096
CTX = 256
DEPTH = 4
DIN = 7200
DFF = 2816
NCC = DFF // 128
EPS = 1e-6
OFF_Q = 2048
OFF_R = 2560
OFF_GA = 3584
OFF_GB = 4608
OFF_K = 5632
OFF_V = 6144
OFF_LR = 7168
G = 512
T = CTX + SEQ
NLG = SEQ // G

VL = {}
_o = 0
for _n, _w in (("n1g", 8), ("n2g", 8), ("lng", 8), ("lnb", 8), ("gng", 8), ("cb", NCC), ("cw", NCC * 9), ("bada", 48)):
    VL[_n] = (_o, _w)
    _o += _w
VPL = _o
VG = {"fng": DEPTH * VPL, "c": DEPTH * VPL + 8, "cctx": DEPTH * VPL + 16}
NV = DEPTH * VPL + 24


def _freeze(fn):
    if fn.__closure__ is None:
        return fn
    cells = []
    for c in fn.__closure__:
        try:
            cells.append(types.CellType(c.cell_contents))
        except ValueError:
            cells.append(c)
    return types.FunctionType(fn.__code__, fn.__globals__, fn.__name__, fn.__defaults__, tuple(cells))


class Prog:
    ENGS = ("pe", "act", "dve", "pool", "sp")

    def __init__(self, nc, es, ndma=8):
        self.nc = nc
        self.ins = {e: [] for e in self.ENGS}
        self.cnt = {e: 0 for e in self.ENGS}
        self.waited = {e: {} for e in self.ENGS}
        self.state = {}
        self.sems = {}
        for e in ("pe", "act", "dve", "pool"):
            self.sems[e] = es.enter_context(nc.semaphore("c_" + e))
        self.ndma = ndma
        self.dma_n = {}
        self.dma_cnt = {}
        for q in ("sp", "act", "pool"):
            self.dma_n[q] = 0
            for i in range(ndma):
                k = ("dma", q, i)
                self.sems[k] = es.enter_context(nc.semaphore("d_%s_%d" % (q, i)))
                self.dma_cnt[k] = 0

    def _st(self, key):
        s = self.state.get(key)
        if s is None:
            s = {"w": None, "r": {}}
            self.state[key] = s
        return s

    def _deps(self, eng, reads, writes):
        deps = {}

        def add(d):
            if d is None:
                return
            k, v = d
            if k == eng and eng == "pe":
                return
            if deps.get(k, 0) < v:
                deps[k] = v

        for r in reads:
            add(self._st(r)["w"])
        for w in writes:
            s = self._st(w)
            add(s["w"])
            for d in s["r"].values():
                add(d)
        out = []
        wd = self.waited[eng]
        for k, v in deps.items():
            if wd.get(k, 0) >= v:
                continue
            wd[k] = v
            out.append((k, v))
        return out

    def _commit(self, mydep, tag, reads, writes):
        ws = set(writes)
        for w in ws:
            s = self._st(w)
            s["w"] = mydep
            s["r"] = {}
        for r in reads:
            if r in ws:
                continue
            self._st(r)["r"][tag] = mydep

    def op(self, eng, fn, reads=(), writes=()):
        waits = self._deps(eng, reads, writes)
        self.cnt[eng] += 1
        mydep = (eng, self.cnt[eng])
        self.ins[eng].append((waits, _freeze(fn), (eng, 1)))
        self._commit(mydep, eng, reads, writes)

    def dma(self, q, out, in_, reads=(), writes=()):
        i = self.dma_n[q] % self.ndma
        self.dma_n[q] += 1
        k = ("dma", q, i)
        waits = self._deps(q, reads, writes)
        wd = self.waited[q]
        prev = self.dma_cnt[k] * 16
        if prev > 0 and wd.get(k, 0) < prev:
            wd[k] = prev
            waits.append((k, prev))
        self.dma_cnt[k] += 1
        mydep = (k, self.dma_cnt[k] * 16)
        self.ins[q].append((waits, lambda e: e.dma_start(out=out, in_=in_), (k, 16)))
        self._commit(mydep, k, reads, writes)

    def final_wait_all(self, eng="sp"):
        waits = []
        for e in ("pe", "act", "dve", "pool"):
            if self.cnt[e] > 0:
                waits.append((e, self.cnt[e]))
        for k, c in self.dma_cnt.items():
            if c > 0:
                waits.append((k, c * 16))
        self.ins[eng].append((waits, None, None))

    def emit(self):
        nc = self.nc
        with nc.Block() as block:
            def run(e, name):
                for waits, fn, inc in self.ins[name]:
                    for k, v in waits:
                        e.wait_ge(self.sems[k], v)
                    if fn is not None:
                        r = fn(e)
                        r.then_inc(self.sems[inc[0]], inc[1])

            @block.tensor
            def _(e):
                run(e, "pe")

            @block.scalar
            def _(e):
                run(e, "act")

            @block.vector
            def _(e):
                run(e, "dve")

            @block.gpsimd
            def _(e):
                run(e, "pool")

            @block.sync
            def _(e):
                run(e, "sp")


class Buf:
    GR = 512

    def __init__(self, ap, off, nbytes, sub=None):
        self.ap = ap
        self.off = off
        self.nbytes = nbytes
        self.sub = sub

    def k(self, i=None, n=1):
        if i is None:
            lo, hi = self.off, self.off + self.nbytes
        else:
            lo = self.off + i * self.sub
            hi = lo + n * self.sub
        return ["g%d" % j for j in range(lo // self.GR, (hi - 1) // self.GR + 1)]

    def kb(self, lo, hi):
        lo += self.off
        hi += self.off
        return ["g%d" % j for j in range(lo // self.GR, (hi - 1) // self.GR + 1)]


def build_program(n_layers=DEPTH, dbg=False):
    nc = bass.Bass("TRN2", target_bir_lowering=False)
    dt_in = lambda name, shape: nc.dram_tensor(name, shape, F32, kind="ExternalInput").ap()
    x_in = dt_in("x_in", [SEQ, D])
    ctx_in = dt_in("ctx_in", [CTX, D])
    vecs_in = dt_in("vecs", [128, NV])
    w_ada = dt_in("w_ada", [DEPTH, D, 6 * D])
    w_in = dt_in("w_in", [DEPTH, D, DIN])
    wsT_in = dt_in("wsT", [DEPTH, 128, 8 * 128])
    sgub_in = dt_in("sgub", [DEPTH, 1, 8 * 128])
    w2a_in = dt_in("w2a", [DEPTH, 2, 17, 512])
    w_bra = dt_in("w_br_a", [DEPTH, D, D])
    w_brb = dt_in("w_br_b", [DEPTH, D, D])
    w_o = dt_in("w_o", [DEPTH, D, D])
    w_up = dt_in("ffn_w_up", [DEPTH, D, 2 * DFF])
    w_dn = dt_in("ffn_w_down", [DEPTH, DFF, D])
    out = nc.dram_tensor("out", [SEQ, D], F32, kind="ExternalOutput").ap()
    xT = nc.dram_tensor("xT_scr", [D, T], F32).ap()
    s1save = nc.dram_tensor("s1save", [NLG, 128, 1024], F32).ap()
    xTv = xT.rearrange("(kc p) t -> p kc t", p=128)
    NGR = 1 + NLG
    ksc = nc.dram_tensor("ksc", [NGR, 128, (G // 128) * 512], BF16).ap()
    vsc = nc.dram_tensor("vsc", [NGR, 128, (G // 128) * 1024], BF16).ap()
    lsc = nc.dram_tensor("lsc", [NGR, 64, G], BF16).ap()
    WSRC = {"ada": w_ada, "in": w_in, "bra": w_bra, "brb": w_brb, "o": w_o, "up": w_up, "dn": w_dn}
    WB = {n: nc.dram_tensor("wb_" + n, list(a.shape), BF16).ap() for n, a in WSRC.items()}

    with contextlib.ExitStack() as es:
        P = Prog(nc, es)
        arena = es.enter_context(nc.sbuf_tensor("arena", [128, 200 * 256], F32))
        cur = [0]

        def alloc(shape, dt, at=None):
            esz = 4 if dt == F32 else 2
            n = 1
            for s in shape[1:]:
                n *= s
            nbytes = n * esz
            if at is None:
                off = cur[0]
                cur[0] = (off + nbytes + Buf.GR - 1) // Buf.GR * Buf.GR
            else:
                off = at
            assert off + nbytes <= 200 * 1024, (off, nbytes)
            ap = arena[0:shape[0], off // 4:(off + nbytes) // 4]
            if dt == BF16:
                ap = ap.bitcast(BF16)
            if len(shape) == 3:
                ap = ap.rearrange("p (a b) -> p a b", b=shape[2])
            elif len(shape) == 4:
                ap = ap.rearrange("p (a b c) -> p a b c", b=shape[2], c=shape[3])
            sub = nbytes // shape[1] if len(shape) >= 3 else None
            return Buf(ap, off, nbytes, sub)

        psb = [es.enter_context(nc.psum_tensor("ps%d" % i, [128, 512], F32)) for i in range(7)]
        pst = es.enter_context(nc.psum_tensor("pst", [128, 1024], BF16))
        psn = [0]

        def psum():
            i = psn[0] % 7
            psn[0] += 1
            return psb[i], ["ps%d" % i]

        dumped = set()

        def dump(name, buf, once=True):
            if not dbg or (once and name in dumped):
                return
            dumped.add(name)
            ap = buf.ap
            shp = list(ap.shape)
            n = 1
            for s_ in shp[1:]:
                n *= s_
            dten = nc.dram_tensor("dbg_" + name, [shp[0], n], F32, kind="ExternalOutput").ap()
            src = ap
            if len(shp) == 3:
                src = ap.rearrange("p a b -> p (a b)")
            elif len(shp) == 4:
                src = ap.rearrange("p a b c -> p (a b c)")
            P.dma("pool", dten, src, reads=buf.k(), writes=["dbg_" + name])

        vecs = alloc([128, NV], F32)
        mods = alloc([128, DEPTH * 2, 48], F32)
        lv = alloc([128, 12, 8], F32)
        ident_f = alloc([128, 128], F32)
        ident_b = alloc([128, 128], BF16)
        ones_b = alloc([128, 128], BF16)
        onesrow = alloc([1, 128], BF16)
        tcm2 = alloc([128, 2, 128], BF16)
        tbx2 = alloc([128, 2, 128], BF16)
        mk2 = alloc([128, 2, 4, 128], BF16)
        wsT = alloc([128, 8, 128], BF16)
        bsrow = alloc([1, 1024], BF16)
        w2a = alloc([64, 512], BF16)
        T2 = alloc([128, 8, 128], F32)
        lngB = alloc([128, 8, 128], BF16)
        scb = alloc([128, 8, 2], BF16)
        sttA = [alloc([128, 8], F32) for _ in range(2)]
        sttB = [alloc([128, 8], F32) for _ in range(2)]
        S1w = alloc([128, 1024], F32)
        S1b = alloc([128, 1024], BF16)
        S2w = alloc([128, 1024], F32)
        S2b = alloc([128, 1024], BF16)
        a_save = alloc([128, NCC, 64], BF16)
        lrT = alloc([64, G], BF16)
        wbufs = [alloc([128, 8 * 512], BF16) for _ in range(3)]
        xg = [alloc([128, 8, G], F32) for _ in range(2)]
        rstd = alloc([128, G], F32)
        tmpf = [alloc([128, G], F32) for _ in range(2)]
        hT = alloc([128, 8, G], BF16)
        hF = alloc([128, 8, G + 64], BF16)
        mA = alloc([128, 8, G], BF16)
        obT = alloc([128, 8, G], BF16)
        sq = alloc([128, 8, G], BF16, at=obT.off)
        spx = alloc([128, 512], F32)
        spb = [alloc([128, 512], BF16) for _ in range(2)]
        E1 = alloc([128, 512], F32)
        kdec = [alloc([128, 512], BF16) for _ in range(2)]
        EqT = alloc([128, 4, 128], F32)
        EkT = alloc([128, 4, 128], F32)
        decb = [alloc([128, 4, 2], F32) for _ in range(2)]
        qdT = [alloc([128, 4, 128], BF16) for _ in range(2)]
        kiT = [alloc([128, 4, 128], BF16) for _ in range(2)]
        scT = [alloc([128, 4, 128], BF16) for _ in range(2)]
        base = cur[0]
        NT = G // 128
        tcm = alloc([128, 2, 64], F32, at=base)
        tbx = alloc([128, 2, 65], F32, at=base + 1024)
        mk = alloc([128, 2, 4, 64], F32, at=base + 2048)
        uT = alloc([128, 8, G], BF16, at=base)
        vh = alloc([128, NT, 1024], BF16, at=base + 8192)
        gvs = [alloc([128, 1024], F32, at=base + 16384 + 4096 * i) for i in range(2)]
        aT = alloc([128, 8, G], BF16, at=base + 24576)
        qT = alloc([128, 4, G], BF16, at=base)
        kT = alloc([128, 4, G], BF16, at=base + 4096)
        ktm = alloc([128, NT, 512], BF16, at=base + 8192)
        vtm = alloc([128, NT, 1024], BF16, at=base + 12288)
        o1 = alloc([128, NT, 1024], F32, at=base + 20480)
        rs = [alloc([128, 1024], BF16, at=base + 36864 + 2048 * i) for i in range(2)]
        ob = [alloc([128, 1024], BF16, at=base + 40960 + 2048 * i) for i in range(2)]
        mg = alloc([128, 8, G], BF16, at=base)
        ktm0, vtm0, lrT0 = ktm, vtm, lrT
        ktm2 = alloc([128, NT, 512], BF16, at=base + 20480)
        vtm2 = alloc([128, NT, 1024], BF16, at=base + 24576)
        lrT2 = alloc([64, G], BF16, at=base + 32768)
        gv = alloc([128, NCC, G], BF16, at=base)
        apad = [alloc([128, 10, 66], BF16, at=base + 22528 + 1536 * i) for i in range(2)]
        dg = [alloc([128, 9, 128], BF16, at=base + 25600 + 2560 * i) for i in range(2)]
        gact = [alloc([128, G], BF16, at=base + 30720 + 1024 * i) for i in range(2)]
        apc = [alloc([128, 256 + 4], BF16, at=base + 32768 + 512 * i) for i in range(2)]
        fin2 = [alloc([128, 1024], F32, at=base + 33792 + 4096 * i) for i in range(2)]
        cur[0] = base + 45056
        assert cur[0] <= 200 * 1024, cur[0]

        V = lambda l, name: vecs.ap[:, l * VPL + VL[name][0]: l * VPL + VL[name][0] + VL[name][1]]
        KV = vecs.k()

        P.dma("sp", vecs.ap, vecs_in, writes=KV)
        P.op("pool", lambda e: e.memset(ident_f.ap, 0.0), writes=ident_f.k())
        P.op("pool", lambda e: e.affine_select(out=ident_f.ap, in_=ident_f.ap, pattern=[[-1, 128]], compare_op=ALU.not_equal,
                                               fill=1.0, base=0, channel_multiplier=1), reads=ident_f.k(), writes=ident_f.k())
        P.op("dve", lambda e: e.tensor_copy(out=ident_b.ap, in_=ident_f.ap), reads=ident_f.k(), writes=ident_b.k())
        P.op("dve", lambda e: e.memset(ones_b.ap, 1.0 / 1024.0), writes=ones_b.k())
        P.op("dve", lambda e: e.memset(onesrow.ap, 1.0), writes=onesrow.k())
        P.op("dve", lambda e: e.memset(lrT.ap, 1.0), writes=lrT.k())
        P.op("dve", lambda e: e.memset(a_save.ap, 0.0), writes=a_save.k())
        NEG = -1.0 / 16.0
        hs = slice(0, 64)
        for d in range(2):
            P.op("pool", lambda e, d=d: e.memset(tcm.ap[hs, d, :], NEG), writes=tcm.k())
            P.op("pool", lambda e, d=d: e.memset(tbx.ap[hs, d, :], NEG), writes=tbx.k())
            P.op("pool", lambda e, d=d: e.memset(mk.ap[hs, d, :, :], 1.0), writes=mk.k())
        P.op("pool", lambda e: e.affine_select(out=tcm.ap[hs, 0, :], in_=tcm.ap[hs, 0, :], pattern=[[-1, 64]],
             compare_op=ALU.is_gt, fill=0.0, base=0, channel_multiplier=1), reads=tcm.k(), writes=tcm.k())
        P.op("pool", lambda e: e.affine_select(out=tcm.ap[hs, 1, :], in_=tcm.ap[hs, 1, :], pattern=[[1, 64]],
             compare_op=ALU.is_gt, fill=0.0, base=0, channel_multiplier=-1), reads=tcm.k(), writes=tcm.k())
        P.op("pool", lambda e: e.affine_select(out=tbx.ap[hs, 0, 0:64], in_=tbx.ap[hs, 0, 0:64], pattern=[[1, 64]],
             compare_op=ALU.is_ge, fill=0.0, base=0, channel_multiplier=-1), reads=tbx.k(), writes=tbx.k())
        P.op("pool", lambda e: e.affine_select(out=tbx.ap[hs, 1, 0:64], in_=tbx.ap[hs, 1, 0:64], pattern=[[-1, 64]],
             compare_op=ALU.is_ge, fill=0.0, base=0, channel_multiplier=1), reads=tbx.k(), writes=tbx.k())
        for h in range(4):
            P.op("pool", lambda e, h=h: e.affine_select(out=mk.ap[hs, 0, h, :], in_=mk.ap[hs, 0, h, :], pattern=[[1, 64]],
                 compare_op=ALU.is_ge, fill=0.0, base=0, channel_multiplier=-1), reads=mk.k(), writes=mk.k())
            P.op("pool", lambda e, h=h: e.affine_select(out=mk.ap[hs, 1, h, :], in_=mk.ap[hs, 1, h, :], pattern=[[-1, 64]],
                 compare_op=ALU.is_ge, fill=0.0, base=0, channel_multiplier=1), reads=mk.k(), writes=mk.k())
        for b_ in (tcm, tbx, mk):
            P.dma("sp", b_.ap[64:128], b_.ap[0:64], reads=b_.k(), writes=b_.k())
        for b_ in (tcm2, tbx2, mk2):
            P.op("dve", lambda e, b_=b_: e.memset(b_.ap, 0.0), writes=b_.k())
        for j_ in range(2):
            hs_ = slice(j_ * 64, j_ * 64 + 64)
            P.op("dve", lambda e, hs_=hs_, j_=j_: e.tensor_copy(out=tcm2.ap[hs_, :, j_ * 64:(j_ + 1) * 64], in_=tcm.ap[hs_, :, :]), reads=tcm.k() + tcm2.k(), writes=tcm2.k())
            P.op("dve", lambda e, hs_=hs_, j_=j_: e.tensor_copy(out=tbx2.ap[hs_, :, j_ * 64:(j_ + 1) * 64], in_=tbx.ap[hs_, :, 0:64]), reads=tbx.k() + tbx2.k(), writes=tbx2.k())
            P.op("dve", lambda e, hs_=hs_, j_=j_: e.tensor_copy(out=mk2.ap[hs_, :, :, j_ * 64:(j_ + 1) * 64], in_=mk.ap[hs_, :, :, :]), reads=mk.k() + mk2.k(), writes=mk2.k())

        wn = [0]

        def cast_weights(l, names):
            for n in names:
                src, dst = WSRC[n], WB[n]
                for rc in range(src.shape[1] // 128):
                    P.dma("pool", dst[l, rc * 128:(rc + 1) * 128, :], src[l, rc * 128:(rc + 1) * 128, :], writes=["wb_%s_%d_%d" % (n, l, rc)])

        def wload(src, nkc, ncols, slot=None):
            n, l, r0, nkc_, c0, ncl = src
            if slot is None:
                i = wn[0] % 3
                wn[0] += 1
            else:
                i = slot
            b = wbufs[i]
            ap = b.ap[:, 0:nkc * ncols].rearrange("p (a b) -> p a b", b=ncols)
            sap = WB[n][l, r0:r0 + nkc * 128, c0:c0 + ncols].rearrange("(kc p) n -> p kc n", p=128)
            rk = ["wb_%s_%d_%d" % (n, l, r0 // 128 + k) for k in range(nkc)]
            P.dma("sp", ap, sap, reads=rk, writes=b.k())
            return ap, b.k()

        WNAME = {id(w_ada): "ada", id(w_in): "in", id(w_bra): "bra", id(w_brb): "brb", id(w_o): "o", id(w_up): "up", id(w_dn): "dn"}

        def wsrc(w, l, r0, nkc, c0, ncols):
            return (WNAME[id(w)], l, r0, nkc, c0, ncols)

        def mm(outap, lhsT, rhs, start, stop, reads, writes):
            P.op("pe", lambda e: e.matmul(outap, lhsT=lhsT, rhs=rhs, start=start, stop=stop), reads=reads, writes=writes)

        def proj_fm(w, l, col0, nchunks, rhs, ntok, handler, nkc=8, r0=0):
            for b0 in range(0, nchunks, 4):
                nb = min(4, nchunks - b0)
                wap, wk = wload(wsrc(w, l, r0, nkc, col0 + b0 * 128, nb * 128), nkc, nb * 128)
                for j in range(nb):
                    ps, pk = psum()
                    for kc in range(nkc):
                        mm(ps[:, 0:ntok], wap[:, kc, j * 128:(j + 1) * 128], rhs.ap[:, kc, 0:ntok], kc == 0, kc == nkc - 1,
                           wk + rhs.k(kc), pk)
                    handler(b0 + j, ps, pk)

        def proj_tm(w, l, col0, ncols, lhs, ntiles, handler):
            for c0 in range(0, ncols, 512):
                ncl = min(512, ncols - c0)
                wap, wk = wload(wsrc(w, l, 0, 8, col0 + c0, ncl), 8, ncl)
                for t in range(ntiles):
                    ps, pk = psum()
                    for kc in range(8):
                        mm(ps[:, 0:ncl], lhs.ap[:, kc, t * 128:(t + 1) * 128], wap[:, kc, :], kc == 0, kc == 7, wk + lhs.k(kc), pk)
                    handler(t, c0, ncl, ps, pk)

        def make_h(xb, x0, ntok, Acol, Bcol, hb, h0):
            for kc in range(8):
                P.op("act", lambda e, kc=kc: e.activation(out=sq.ap[:, kc, 0:ntok], in_=xb.ap[:, kc, x0:x0 + ntok], func=AF.Square),
                     reads=xb.k(kc), writes=sq.k(kc))
            ps, pk = psum()
            for kc in range(8):
                mm(ps[:, 0:ntok], ones_b.ap, sq.ap[:, kc, 0:ntok], kc == 0, kc == 7, ones_b.k() + sq.k(kc), pk)
            P.op("act", lambda e: e.activation(out=rstd.ap[:, 0:ntok], in_=ps[:, 0:ntok], func=AF.Ln, bias=EPS, scale=1.0),
                 reads=pk, writes=rstd.k())
            P.op("act", lambda e: e.activation(out=rstd.ap[:, 0:ntok], in_=rstd.ap[:, 0:ntok], func=AF.Exp, scale=-0.5),
                 reads=rstd.k(), writes=rstd.k())
            for kc in range(8):
                tf = tmpf[kc % 2]
                P.op("dve", lambda e, kc=kc, tf=tf: e.scalar_tensor_tensor(out=tf.ap[:, 0:ntok], in0=xb.ap[:, kc, x0:x0 + ntok],
                     scalar=lv.ap[:, Acol, kc:kc + 1], in1=rstd.ap[:, 0:ntok], op0=ALU.mult, op1=ALU.mult),
                     reads=xb.k(kc) + lv.k() + rstd.k(), writes=tf.k())
                P.op("act", lambda e, kc=kc, tf=tf: e.activation(out=hb.ap[:, kc, h0:h0 + ntok], in_=tf.ap[:, 0:ntok], func=AF.Identity,
                     bias=lv.ap[:, Bcol, kc:kc + 1], scale=1.0), reads=tf.k() + lv.k(), writes=hb.k(kc))

        def load_tokens(src, nrows, tcol0):
            for t in range(nrows // 128):
                xb = xg[t % 2]
                stg = gvs[t % 2]
                P.dma("sp", stg.ap, src[t * 128:(t + 1) * 128, :], writes=stg.k())
                for hh in range(2):
                    ps, pk = psum()
                    for j in range(4):
                        kc = hh * 4 + j
                        P.op("pe", lambda e, ps=ps, j=j, kc=kc, stg=stg: e.transpose(ps[:, j * 128:(j + 1) * 128], stg.ap[:, kc * 128:(kc + 1) * 128], ident_f.ap),
                             reads=stg.k() + ident_f.k(), writes=pk)
                    P.op("dve", lambda e, ps=ps, hh=hh, xb=xb: e.tensor_copy(out=xb.ap[:, hh * 4:hh * 4 + 4, 0:128],
                         in_=ps[:, :].rearrange("p (a b) -> p a b", b=128)), reads=pk, writes=xb.k())
                P.dma("sp", xTv[:, :, tcol0 + t * 128: tcol0 + (t + 1) * 128], xb.ap[:, :, 0:128], reads=xb.k(), writes=["xT%d" % (0 if tcol0 == 0 else 1 + t // (G // 128))])

        load_tokens(ctx_in, CTX, 0)
        load_tokens(x_in, SEQ, CTX)

        cs = vecs.ap[:, VG["c"]:VG["c"] + 16].rearrange("p (two kc) -> p kc two", two=2)
        P.op("act", lambda e: e.activation(out=scb.ap, in_=cs, func=AF.Silu), reads=KV, writes=scb.k())
        def adaln(l):
                ps, pk = psum()
                for blk in range(12):
                    wap, wk = wload(wsrc(w_ada, l, 0, 8, blk * 512, 512), 8, 512)
                    for j in range(4):
                        col = (blk * 4 + j) * 2
                        for kc in range(8):
                            mm(ps[:, col:col + 2], wap[:, kc, j * 128:(j + 1) * 128], scb.ap[:, kc, :], kc == 0, kc == 7, wk + scb.k(), pk)
                psv = ps[:, 0:96].rearrange("p (c two) -> p two c", two=2)
                for two in range(2):
                    P.op("dve", lambda e, two=two, l=l, psv=psv: e.tensor_tensor(out=mods.ap[:, l * 2 + two, :], in0=psv[:, two, :], in1=V(l, "bada"), op=ALU.add),
                         reads=pk + KV, writes=mods.k(l * 2 + two))

        dump('mods', mods)
        groups = [(0, CTX, True)] + [(CTX + i * G, G, False) for i in range(NLG)]
        QS = 128.0 ** -0.5

        def layer_consts(l):
            for two in range(2):
                m = mods.ap[:, l * 2 + two, :]
                rk = mods.k(l * 2 + two) + KV
                P.op("dve", lambda e, m=m, two=two: e.scalar_tensor_tensor(out=lv.ap[:, 0 + 2 * two, :], in0=m[:, 8:16], scalar=1.0, in1=V(l, "n1g"),
                     op0=ALU.add, op1=ALU.mult), reads=rk, writes=lv.k())
                P.op("dve", lambda e, m=m, two=two: e.tensor_copy(out=lv.ap[:, 1 + 2 * two, :], in_=m[:, 0:8]), reads=rk, writes=lv.k())
                P.op("dve", lambda e, m=m, two=two: e.scalar_tensor_tensor(out=lv.ap[:, 4 + 2 * two, :], in0=m[:, 32:40], scalar=1.0, in1=V(l, "n2g"),
                     op0=ALU.add, op1=ALU.mult), reads=rk, writes=lv.k())
                P.op("dve", lambda e, m=m, two=two: e.tensor_copy(out=lv.ap[:, 5 + 2 * two, :], in_=m[:, 24:32]), reads=rk, writes=lv.k())
                P.op("dve", lambda e, m=m, two=two: e.tensor_copy(out=lv.ap[:, 8 + two, :], in_=m[:, 16:24]), reads=rk, writes=lv.k())
                P.op("dve", lambda e, m=m, two=two: e.tensor_copy(out=lv.ap[:, 10 + two, :], in_=m[:, 40:48]), reads=rk, writes=lv.k())
            P.dma("pool", wsT.ap.rearrange("p a b -> p (a b)"), wsT_in[l], writes=wsT.k())
            P.dma("pool", bsrow.ap, sgub_in[l], writes=bsrow.k())
            for d in range(2):
                P.dma("pool", w2a.ap[d * 32:d * 32 + 17, :], w2a_in[l, d], writes=w2a.k())
            for g in range(8):
                ps, pk = psum()
                mm(ps[:, 0:128], ones1.ap, wsT.ap[:, g, :], True, True, ones1.k() + wsT.k(), pk)
                mm(ps[:, 128:256], onesrow.ap[0:1, :], bsrow.ap[0:1, g * 128:(g + 1) * 128], True, True, onesrow.k() + bsrow.k(), pk)
                tf = tmpf[g % 2]
                P.op("act", lambda e, ps=ps, tf=tf: e.activation(out=tf.ap[:, 0:128], in_=ps[:, 128:256], func=AF.Copy), reads=pk, writes=tf.k())
                P.op("dve", lambda e, ps=ps, tf=tf, g=g: e.scalar_tensor_tensor(out=T2.ap[:, g, :], in0=ps[:, 0:128], scalar=V(l, "lnb")[:, g:g + 1],
                     in1=tf.ap[:, 0:128], op0=ALU.mult, op1=ALU.add), reads=pk + tf.k() + KV, writes=T2.k(g))
                P.op("dve", lambda e, g=g: e.tensor_scalar(out=lngB.ap[:, g, :], in0=ones1.ap, scalar1=V(l, "lng")[:, g:g + 1], scalar2=None, op0=ALU.mult),
                     reads=ones1.k() + KV, writes=lngB.k(g))

        ones1 = alloc([128, 128], BF16)
        P.op("dve", lambda e: e.memset(ones1.ap, 1.0), writes=ones1.k())

        def lr_proj(l, ntok, hb=None, lrT_=None):
            hb = hb or hT
            lrT_ = lrT_ or lrT
            wap, wk = wload(wsrc(w_in, l, 0, 8, OFF_LR, 32), 8, 32)
            ps, pk = psum()
            for d in range(2):
                for kc in range(8):
                    mm(ps[d * 32:d * 32 + 16, 0:ntok], wap[:, kc, d * 16:(d + 1) * 16], hb.ap[:, kc, 0:ntok], kc == 0, kc == 7, wk + hb.k(kc), pk)
            for d in range(2):
                P.op("act", lambda e, d=d, ps=ps: e.activation(out=lrT_.ap[d * 32:d * 32 + 16, 0:ntok], in_=ps[d * 32:d * 32 + 16, 0:ntok], func=AF.Copy),
                     reads=pk, writes=lrT_.k())

        def kv_thunks(l, ntok, hb=None, ktm_=None, vtm_=None):
            hb = hb or hT
            ktm_ = ktm_ or ktm
            vtm_ = vtm_ or vtm
            nt = ntok // 128
            th = []
            wst_ = {}
            for (off, dstk) in ((OFF_K, "k"), (OFF_V, "v0"), (OFF_V + 512, "v1")):
                for t in range(nt):
                    def f(off=off, dstk=dstk, t=t):
                        if t == 0:
                            wst_[dstk] = wload(wsrc(w_in, l, 0, 8, off, 512), 8, 512)
                        wap, wk = wst_[dstk]
                        ps, pk = psum()
                        for kc in range(8):
                            mm(ps[:, 0:512], hb.ap[:, kc, t * 128:(t + 1) * 128], wap[:, kc, :], kc == 0, kc == 7, wk + hb.k(kc), pk)
                        if dstk == "k":
                            P.op("act", lambda e: e.activation(out=ktm_.ap[:, t, :], in_=ps[:, 0:512], func=AF.Copy), reads=pk, writes=ktm_.k(t))
                        else:
                            c0 = 0 if dstk == "v0" else 512
                            P.op("dve", lambda e: e.tensor_copy(out=vtm_.ap[:, t, c0:c0 + 512], in_=ps[:, 0:512]), reads=pk, writes=vtm_.kb(t * 2048 + c0 * 2, t * 2048 + c0 * 2 + 1024))
                    th.append(f)
            return th

        def kv_proj(l, ntok):
            for f in kv_thunks(l, ntok):
                f()

        ucnt = [0]

        def gla_run(units, Smap, with_out, first_fn=None, bufs=None, fillers=()):
            U = len(units)
            ub = ucnt[0]
            ucnt[0] += U
            ktm, vtm, lrT = bufs if bufs is not None else (ktm0, vtm0, lrT0)
            fillers = list(fillers)
            st_ = [dict() for _ in units]

            def geo(u):
                d, t = units[u]
                return d, t, slice(t * 128, (t + 1) * 128), (ub + u) % 2

            def S0(u):
                d, t, tok, b = geo(u)
                sp_ = spb[b]
                ps, pk = psum()
                mm(ps[:, :], lrT.ap[d * 32:d * 32 + 17, tok], w2a.ap[d * 32:d * 32 + 17, :], True, True, lrT.k() + w2a.k(), pk)
                P.op("act", lambda e: e.activation(out=spx.ap, in_=ps[:, :], func=AF.Exp, scale=-1.0), reads=pk, writes=spx.k())
                P.op("act", lambda e: e.activation(out=sp_.ap, in_=spx.ap, func=AF.Ln, bias=1.0, scale=1.0), reads=spx.k(), writes=sp_.k())

            def S1(u):
                d, t, tok, b = geo(u)
                sp_, kd, dc = spb[b], kdec[b], decb[b]
                ps2, pk2 = psum()
                mm(ps2[:, :], tcm2.ap[:, d, :], sp_.ap, True, True, tcm2.k() + sp_.k(), pk2)
                P.op("act", lambda e: e.activation(out=E1.ap, in_=ps2[:, :], func=AF.Exp), reads=pk2, writes=E1.k())
                P.op("dve", lambda e: e.tensor_tensor(out=kd.ap, in0=ktm.ap[:, t, :], in1=E1.ap, op=ALU.mult), reads=ktm.k(t) + E1.k(), writes=kd.k())
                if with_out:
                    qd, ki = qdT[b], kiT[b]
                    ps3, pk3 = psum()
                    for h in range(4):
                        mm(ps3[:, h * 128:(h + 1) * 128], sp_.ap[:, h * 128:(h + 1) * 128], tbx2.ap[:, d, :], True, True, sp_.k() + tbx2.k(), pk3)
                    p3v = ps3[:, :].rearrange("p (h j c) -> p h j c", j=2, c=64)
                    lastc = 63 if d == 0 else 0
                    P.op("act", lambda e: e.activation(out=dc.ap, in_=p3v[:, :, :, lastc], func=AF.Exp), reads=pk3, writes=dc.k())
                    P.op("act", lambda e: e.activation(out=EqT.ap, in_=ps3[:, :].rearrange("p (h c) -> p h c", c=128), func=AF.Exp), reads=pk3, writes=EqT.k())
                    P.op("act", lambda e: e.activation(out=EkT.ap, in_=ps3[:, :].rearrange("p (h c) -> p h c", c=128), func=AF.Exp, scale=-1.0), reads=pk3, writes=EkT.k())
                    P.op("dve", lambda e: e.scalar_tensor_tensor(out=qd.ap, in0=qT.ap[:, :, tok], scalar=QS, in1=EqT.ap, op0=ALU.mult, op1=ALU.mult),
                         reads=qT.k() + EqT.k(), writes=qd.k())
                    P.op("dve", lambda e: e.tensor_tensor(out=ki.ap, in0=kT.ap[:, :, tok], in1=EkT.ap, op=ALU.mult), reads=kT.k() + EkT.k(), writes=ki.k())
                else:
                    ps3, pk3 = psum()
                    for h in range(4):
                        for j in range(2):
                            lc = j * 64 + (63 if d == 0 else 0)
                            mm(ps3[:, h * 2 + j:h * 2 + j + 1], sp_.ap[:, h * 128:(h + 1) * 128], tbx2.ap[:, d, lc:lc + 1], True, True, sp_.k() + tbx2.k(), pk3)
                    P.op("act", lambda e: e.activation(out=dc.ap, in_=ps3[:, 0:8].rearrange("p (h j) -> p h j", j=2), func=AF.Exp), reads=pk3, writes=dc.k())

            def S2(u):
                if not with_out:
                    return
                d, t, tok, b = geo(u)
                qd, ki, sc = qdT[b], kiT[b], scT[b]
                ps4, pk4 = psum()
                for h in range(4):
                    mm(ps4[:, h * 128:(h + 1) * 128], ki.ap[:, h, :], qd.ap[:, h, :], True, True, ki.k() + qd.k(), pk4)
                P.op("dve", lambda e: e.tensor_tensor(out=sc.ap, in0=ps4[:, :].rearrange("p (h c) -> p h c", c=128), in1=mk2.ap[:, d, :, :], op=ALU.mult),
                     reads=pk4 + mk2.k(), writes=sc.k())

            def S3(u):
                d, t, tok, b = geo(u)
                kd, dc = kdec[b], decb[b]
                Sw, Sb = Smap[d]
                if with_out:
                    qd, sc = qdT[b], scT[b]
                    first = first_fn(d, t)
                    for hh in range(2):
                        ps5, pk5 = psum()
                        for j in range(2):
                            h = hh * 2 + j
                            mm(ps5[:, j * 256:(j + 1) * 256], sc.ap[:, h, :], vtm.ap[:, t, h * 256:(h + 1) * 256], True, True, sc.k() + vtm.k(t), pk5)
                        osl = o1.ap[:, t, hh * 512:(hh + 1) * 512]
                        ok_ = o1.kb(t * 4096 + hh * 2048, t * 4096 + hh * 2048 + 2048)
                        if first:
                            P.op("act", lambda e, ps5=ps5, osl=osl: e.activation(out=osl, in_=ps5[:, :], func=AF.Copy), reads=pk5, writes=ok_)
                        else:
                            P.op("dve", lambda e, ps5=ps5, osl=osl: e.tensor_tensor(out=osl, in0=ps5[:, :], in1=osl, op=ALU.add), reads=pk5 + ok_, writes=ok_)
                for jc in ((0, 1) if d == 0 else (1, 0)):
                    par = slice(jc * 64, jc * 64 + 64)
                    if with_out:
                        for hh in range(2):
                            ps7, pk7 = psum()
                            for j in range(2):
                                h = hh * 2 + j
                                mm(ps7[par, j * 256:(j + 1) * 256], qd.ap[:, h, jc * 64:(jc + 1) * 64], Sb.ap[:, h * 256:(h + 1) * 256], True, True, qd.k() + Sb.k(), pk7)
                            osl = o1.ap[par, t, hh * 512:(hh + 1) * 512]
                            ok_ = o1.kb(t * 4096 + hh * 2048, t * 4096 + hh * 2048 + 2048)
                            P.op("dve", lambda e, ps7=ps7, osl=osl, par=par: e.tensor_tensor(out=osl, in0=ps7[par, :], in1=osl, op=ALU.add), reads=pk7 + ok_, writes=ok_)
                    for hh in range(2):
                        ps6, pk6 = psum()
                        for j in range(2):
                            h = hh * 2 + j
                            mm(ps6[:, j * 256:(j + 1) * 256], kd.ap[par, h * 128:(h + 1) * 128], vtm.ap[par, t, h * 256:(h + 1) * 256], True, True, kd.k() + vtm.k(t), pk6)
                        for j in range(2):
                            h = hh * 2 + j
                            hk = Sw.kb(h * 1024, h * 1024 + 1024)
                            P.op("dve", lambda e, h=h, j=j, ps6=ps6, jc=jc: e.scalar_tensor_tensor(out=Sw.ap[:, h * 256:(h + 1) * 256], in0=Sw.ap[:, h * 256:(h + 1) * 256], scalar=dc.ap[:, h, jc:jc + 1],
                                 in1=ps6[:, j * 256:(j + 1) * 256], op0=ALU.mult, op1=ALU.add), reads=hk + dc.k() + pk6, writes=hk)
                    if with_out:
                        P.op("act", lambda e: e.activation(out=Sb.ap, in_=Sw.ap, func=AF.Copy), reads=Sw.k(), writes=Sb.k())

            stages = [S0, S1, S2, S3]
            nsteps = U + 3
            perstep = (len(fillers) + nsteps - 1) // nsteps if fillers else 0
            for step in range(nsteps):
                for s_ in (3, 2, 1, 0):
                    u = step - s_
                    if 0 <= u < U:
                        stages[s_](u)
                for _ in range(perstep):
                    if fillers:
                        fillers.pop(0)()
            while fillers:
                fillers.pop(0)()

        def interleave(n):
            units = []
            for i in range(n):
                units.append((0, i))
                units.append((1, n - 1 - i))
            return units

        def zero_state(Sw, Sb):
            P.op("dve", lambda e: e.memset(Sw.ap, 0.0), writes=Sw.k())
            P.op("dve", lambda e: e.memset(Sb.ap, 0.0), writes=Sb.k())

        def load_x(gi, slot):
            c0, ntok, isc = groups[gi]
            P.dma("sp", xg[slot].ap[:, :, 0:ntok], xTv[:, :, c0:c0 + ntok], reads=["xT%d" % gi], writes=xg[slot].k())

        def store_x(gi, slot):
            c0, ntok, isc = groups[gi]
            P.dma("sp", xTv[:, :, c0:c0 + ntok], xg[slot].ap[:, :, 0:ntok], reads=xg[slot].k(), writes=["xT%d" % gi])

        def pass1(l):
            zero_state(S1w, S1b)
            zero_state(S2w, S2b)
            P.op("dve", lambda e: e.memset(lrT2.ap, 1.0), writes=lrT2.k())
            sets = [(hT, ktm, vtm, lrT), (hF, ktm2, vtm2, lrT2)]

            def prep_thunks(gi):
                c0, ntok, isc = groups[gi]
                slot = gi % 2
                hb, k_, v_, lr_ = sets[gi % 2]
                th = []

                def f0():
                    load_x(gi, slot)
                    make_h(xg[slot], 0, ntok, 2 if isc else 0, 3 if isc else 1, hb, 0)
                th.append(f0)
                th.extend(kv_thunks(l, ntok, hb, k_, v_))
                th.append(lambda: lr_proj(l, ntok, hb, lr_))

                def fst():
                    P.dma("sp", ksc[gi], k_.ap.rearrange("p a b -> p (a b)"), reads=k_.k(), writes=["ksc%d" % gi])
                    P.dma("sp", vsc[gi], v_.ap.rearrange("p a b -> p (a b)"), reads=v_.k(), writes=["vsc%d" % gi])
                    P.dma("sp", lsc[gi], lr_.ap, reads=lr_.k(), writes=["lsc%d" % gi])
                th.append(fst)
                return th

            for f in prep_thunks(0):
                f()
            for gi in range(len(groups)):
                c0, ntok, isc = groups[gi]
                hb, k_, v_, lr_ = sets[gi % 2]
                fl = prep_thunks(gi + 1) if gi + 1 < len(groups) else []
                if not isc:
                    P.dma("sp", s1save[gi - 1], S1w.ap, reads=S1w.k(), writes=["s1save%d" % (gi - 1)])
                if isc:
                    gla_run(interleave(ntok // 128), {0: (S1w, S1b), 1: (S2w, S2b)}, False, bufs=(k_, v_, lr_), fillers=fl)
                else:
                    gla_run([(0, t_) for t_ in range(ntok // 128)], {0: (S1w, S1b)}, False, bufs=(k_, v_, lr_), fillers=fl)

        def mixer(l, gi, slot):
            c0, ntok, isc = groups[gi]
            nt = ntok // 128
            xb = xg[slot]
            load_x(gi, slot)
            make_h(xb, 0, ntok, 2 if isc else 0, 3 if isc else 1, hT, 0)

            def hu(j, ps, pk):
                P.op("act", lambda e: e.activation(out=uT.ap[:, j, 0:ntok], in_=ps[:, 0:ntok], func=AF.Gelu), reads=pk, writes=uT.k(j))
            proj_fm(w_in, l, 0, 8, hT, ntok, hu)

            wblocks = []
            for cb_ in range(2):
                wblocks.append(wload(wsrc(w_in, l, 0, 8, 1024 + cb_ * 512, 512), 8, 512, slot=cb_))
            gst = {}

            def gate_proj(jj, off, dst):
                if jj % 4 == 0:
                    gst["w"] = wload(wsrc(w_in, l, 0, 8, off + jj * 128, 512), 8, 512, slot=2)
                wa, wka = gst["w"]
                j = jj % 4
                ps, pk = psum()
                for kc in range(8):
                    mm(ps[:, 0:ntok], wa[:, kc, j * 128:(j + 1) * 128], hT.ap[:, kc, 0:ntok], kc == 0, kc == 7, wka + hT.k(kc), pk)
                P.op("act", lambda e: e.activation(out=dst.ap[:, jj, 0:ntok], in_=ps[:, 0:ntok], func=AF.Sigmoid), reads=pk, writes=dst.k(jj))

            def vs_proj(t):
                gb = gvs[t % 2]
                st = sttA[t % 2]
                for cb_ in range(2):
                    wap, wk = wblocks[cb_]
                    ps, pk = psum()
                    for kc in range(8):
                        mm(ps[:, 0:512], hT.ap[:, kc, t * 128:(t + 1) * 128], wap[:, kc, :], kc == 0, kc == 7, wk + hT.k(kc), pk)
                    c0_ = cb_ * 512
                    gk = gb.kb(c0_ * 4, c0_ * 4 + 2048)
                    P.op("act", lambda e, ps=ps, c0_=c0_, cb_=cb_: e.activation(out=gb.ap[:, c0_:c0_ + 512], in_=ps[:, 0:512], func=AF.Gelu, accum_out=st.ap[:, 2 * cb_:2 * cb_ + 1]),
                         reads=pk, writes=gk + st.k())
                    P.op("act", lambda e, c0_=c0_, cb_=cb_: e.activation(out=tmpf[cb_].ap[:, 0:512], in_=gb.ap[:, c0_:c0_ + 512], func=AF.Square, accum_out=st.ap[:, 2 * cb_ + 1:2 * cb_ + 2]),
                         reads=gk, writes=tmpf[cb_].k() + st.k())

            def ln_gate(t):
                gb = gvs[t % 2]
                st = sttA[t % 2]
                P.op("dve", lambda e: e.tensor_tensor(out=st.ap[:, 4:6], in0=st.ap[:, 0:2], in1=st.ap[:, 2:4], op=ALU.add), reads=st.k(), writes=st.k())
                P.op("dve", lambda e: e.tensor_scalar(out=st.ap[:, 4:6], in0=st.ap[:, 4:6], scalar1=1.0 / 1024.0, scalar2=None, op0=ALU.mult), reads=st.k(), writes=st.k())
                P.op("dve", lambda e: e.scalar_tensor_tensor(out=st.ap[:, 6:7], in0=st.ap[:, 4:5], scalar=st.ap[:, 4:5], in1=st.ap[:, 5:6], op0=ALU.mult, op1=ALU.subtract),
                     reads=st.k(), writes=st.k())
                P.op("act", lambda e: e.activation(out=st.ap[:, 7:8], in_=st.ap[:, 6:7], func=AF.Ln, bias=EPS, scale=-1.0), reads=st.k(), writes=st.k())
                P.op("act", lambda e: e.activation(out=st.ap[:, 7:8], in_=st.ap[:, 7:8], func=AF.Exp, scale=-0.5), reads=st.k(), writes=st.k())
                P.op("dve", lambda e: e.tensor_scalar(out=vh.ap[:, t, :], in0=gb.ap, scalar1=st.ap[:, 4:5], scalar2=st.ap[:, 7:8], op0=ALU.subtract, op1=ALU.mult),
                     reads=gb.k() + st.k(), writes=vh.k(t))
                for hh in range(2):
                    ps, pk = psum()
                    for j in range(4):
                        g = hh * 4 + j
                        mm(ps[:, j * 128:(j + 1) * 128], vh.ap[:, t, g * 128:(g + 1) * 128], wsT.ap[:, g, :], True, True, vh.k(t) + wsT.k(), pk)
                    gs = slice(hh * 4, hh * 4 + 4)
                    tv = gb.ap[:, hh * 512:(hh + 1) * 512].rearrange("p (a b) -> p a b", b=128)
                    tk = gb.kb(hh * 2048, hh * 2048 + 2048)
                    P.op("dve", lambda e, ps=ps, gs=gs, tv=tv: e.tensor_tensor(out=tv, in0=ps[:, :].rearrange("p (a b) -> p a b", b=128), in1=lngB.ap[:, gs, :], op=ALU.mult),
                         reads=pk + lngB.k(), writes=tk)
                    P.op("dve", lambda e, gs=gs, tv=tv: e.tensor_tensor(out=tv, in0=tv, in1=T2.ap[:, gs, :], op=ALU.add), reads=tk + T2.k(), writes=tk)
                    P.op("dve", lambda e, gs=gs, tv=tv: e.tensor_tensor(out=aT.ap[:, gs, t * 128:(t + 1) * 128], in0=tv, in1=uT.ap[:, gs, t * 128:(t + 1) * 128], op=ALU.mult),
                         reads=tk + uT.k(hh * 4, 4), writes=aT.k(hh * 4, 4))

            vs_proj(0)
            gq = list(range(8))
            per = (8 + nt - 1) // nt
            for t in range(nt):
                if t + 1 < nt:
                    vs_proj(t + 1)
                for _ in range(per):
                    if gq:
                        gate_proj(gq.pop(0), OFF_GA, mA)
                ln_gate(t)
            wn[0] = 0
            for b0 in range(0, 8, 4):
                wb_, wkb = wload(wsrc(w_bra, l, 0, 8, b0 * 128, 512), 8, 512)
                for j in range(4):
                    jj = b0 + j
                    ps2, pk2 = psum()
                    for kc in range(8):
                        mm(ps2[:, 0:ntok], wb_[:, kc, j * 128:(j + 1) * 128], aT.ap[:, kc, 0:ntok], kc == 0, kc == 7, wkb + aT.k(kc), pk2)
                    P.op("dve", lambda e, ps2=ps2, jj=jj: e.tensor_tensor(out=mA.ap[:, jj, 0:ntok], in0=ps2[:, 0:ntok], in1=mA.ap[:, jj, 0:ntok], op=ALU.mult),
                         reads=pk2 + mA.k(jj), writes=mA.k(jj))
            dump('m_uT', uT); dump('m_vh', vh); dump('m_aT', aT); dump('m_mA', mA); dump('m_T2', T2)
            P.dma("pool", ktm.ap.rearrange("p a b -> p (a b)"), ksc[gi], reads=["ksc%d" % gi], writes=ktm.k())
            P.dma("pool", vtm.ap.rearrange("p a b -> p (a b)"), vsc[gi], reads=["vsc%d" % gi], writes=vtm.k())
            P.dma("pool", lrT.ap, lsc[gi], reads=["lsc%d" % gi], writes=lrT.k())

            def hq(j, ps, pk):
                P.op("act", lambda e: e.activation(out=qT.ap[:, j, 0:ntok], in_=ps[:, 0:ntok], func=AF.Copy), reads=pk, writes=qT.k(j))
            proj_fm(w_in, l, OFF_Q, 4, hT, ntok, hq)

            def hkT(j, ps, pk):
                P.op("dve", lambda e: e.tensor_copy(out=kT.ap[:, j, 0:ntok], in_=ps[:, 0:ntok]), reads=pk, writes=kT.k(j))
            proj_fm(w_in, l, OFF_K, 4, hT, ntok, hkT)
            nch = ntok // 64
            if isc:
                zero_state(S1w, S1b)
            else:
                P.dma("sp", S1w.ap, s1save[gi - 1], reads=["s1save%d" % (gi - 1)], writes=S1w.k())
                P.op("act", lambda e: e.activation(out=S1b.ap, in_=S1w.ap, func=AF.Copy), reads=S1w.k(), writes=S1b.k())
            if isc:
                zero_state(S2w, S2b)
            gla_run(interleave(nt), {0: (S1w, S1b), 1: (S2w, S2b)}, True,
                    first_fn=lambda d, t_: (t_ < nt // 2) if d == 0 else (t_ >= nt // 2))
            dump('m_o1', o1); dump('m_qT', qT); dump('m_S1', S1w); dump('m_S2', S2w)
            wr = [wload(wsrc(w_in, l, 0, 8, OFF_R + cb_ * 512, 512), 8, 512, slot=cb_) for cb_ in range(2)]

            def r_proj(t):
                rs_ = rs[t % 2]
                for cb_ in range(2):
                    wap, wk = wr[cb_]
                    ps, pk = psum()
                    for kc in range(8):
                        mm(ps[:, 0:512], hT.ap[:, kc, t * 128:(t + 1) * 128], wap[:, kc, :], kc == 0, kc == 7, wk + hT.k(kc), pk)
                    P.op("act", lambda e, ps=ps, cb_=cb_: e.activation(out=rs_.ap[:, cb_ * 512:(cb_ + 1) * 512], in_=ps[:, 0:512], func=AF.Silu), reads=pk,
                         writes=rs_.kb(cb_ * 1024, cb_ * 1024 + 1024))

            def ob_norm(t):
                rs_, ob_, st = rs[t % 2], ob[t % 2], sttB[t % 2]
                for h in range(4):
                    P.op("act", lambda e, h=h: e.activation(out=tmpf[h % 2].ap[:, 0:256], in_=o1.ap[:, t, h * 256:(h + 1) * 256], func=AF.Square, accum_out=st.ap[:, h:h + 1]),
                         reads=o1.k(t), writes=tmpf[h % 2].k() + st.k())
                P.op("act", lambda e: e.activation(out=st.ap[:, 4:8], in_=st.ap[:, 0:4], func=AF.Ln, bias=EPS, scale=1.0 / 256.0), reads=st.k(), writes=st.k())
                P.op("act", lambda e: e.activation(out=st.ap[:, 4:8], in_=st.ap[:, 4:8], func=AF.Exp, scale=-0.5), reads=st.k(), writes=st.k())
                for h in range(4):
                    P.op("dve", lambda e, h=h: e.scalar_tensor_tensor(out=ob_.ap[:, h * 256:(h + 1) * 256], in0=o1.ap[:, t, h * 256:(h + 1) * 256], scalar=st.ap[:, 4 + h:5 + h],
                         in1=rs_.ap[:, h * 256:(h + 1) * 256], op0=ALU.mult, op1=ALU.mult), reads=o1.k(t) + st.k() + rs_.k(), writes=ob_.k())

            def ob_transpose(t):
                ob_ = ob[t % 2]
                for j in range(8):
                    P.op("pe", lambda e, j=j: e.transpose(pst[:, j * 128:(j + 1) * 128], ob_.ap[:, j * 128:(j + 1) * 128], ident_b.ap), reads=ob_.k() + ident_b.k(), writes=["pst"])
                for j in range(8):
                    P.op("act", lambda e, j=j: e.activation(out=obT.ap[:, j, t * 128:(t + 1) * 128], in_=pst[:, j * 128:(j + 1) * 128], func=AF.Copy, scale=V(l, "gng")[:, j:j + 1]),
                         reads=["pst"] + KV, writes=obT.k(j))

            r_proj(0)
            gq = list(range(8))
            for t in range(nt):
                if t + 1 < nt:
                    r_proj(t + 1)
                ob_norm(t)
                for _ in range(per):
                    if gq:
                        gate_proj(gq.pop(0), OFF_GB, mg)
                ob_transpose(t)
            wn[0] = 0
            for b0 in range(0, 8, 4):
                wb_, wkb = wload(wsrc(w_brb, l, 0, 8, b0 * 128, 512), 8, 512)
                for j in range(4):
                    jj = b0 + j
                    ps2, pk2 = psum()
                    for kc in range(8):
                        mm(ps2[:, 0:ntok], wb_[:, kc, j * 128:(j + 1) * 128], obT.ap[:, kc, 0:ntok], kc == 0, kc == 7, wkb + obT.k(kc), pk2)
                    tf = tmpf[jj % 2]
                    P.op("dve", lambda e, ps2=ps2, tf=tf, jj=jj: e.tensor_tensor(out=tf.ap[:, 0:ntok], in0=ps2[:, 0:ntok], in1=mg.ap[:, jj, 0:ntok], op=ALU.mult),
                         reads=pk2 + mg.k(jj), writes=tf.k())
                    P.op("dve", lambda e, tf=tf, jj=jj: e.tensor_tensor(out=mg.ap[:, jj, 0:ntok], in0=tf.ap[:, 0:ntok], in1=mA.ap[:, jj, 0:ntok], op=ALU.add),
                         reads=tf.k() + mA.k(jj), writes=mg.k(jj))
            dump('m_obT', obT); dump('m_mg', mg)
            gcol = 9 if isc else 8

            def ho(j, ps, pk):
                P.op("dve", lambda e: e.scalar_tensor_tensor(out=xb.ap[:, j, 0:ntok], in0=ps[:, 0:ntok], scalar=lv.ap[:, gcol, j:j + 1], in1=xb.ap[:, j, 0:ntok],
                     op0=ALU.mult, op1=ALU.add), reads=pk + lv.k() + xb.k(j), writes=xb.k(j))
            proj_fm(w_o, l, 0, 8, mg, ntok, ho)
            dump('m_xmid', xb)

        def diag_build(l, cc, ntaps):
            d_ = dg[cc % 2]
            for j in ntaps:
                P.op("dve", lambda e, j=j: e.tensor_scalar(out=d_.ap[:, j, :], in0=ident_b.ap, scalar1=V(l, "cw")[:, cc * 9 + j:cc * 9 + j + 1], scalar2=None, op0=ALU.mult),
                     reads=ident_b.k() + KV, writes=d_.k())
            return d_

        def ffn(l, gi, slot, last):
            c0, ntok, isc = groups[gi]
            xb = xg[slot]
            lower = (not isc) and gi > 1
            nh = ntok + (64 if lower else 0)
            make_h(xb, 0, ntok, 6 if isc else 4, 7 if isc else 5, hF, 0)
            if lower:
                ob_ = xg[1 - slot]
                make_h(ob_, G - 64, 64, 4, 5, hF, G)
            wst = {}
            if isc:
                for i_ in range(2):
                    P.op("dve", lambda e, i_=i_: e.memset(apc[i_].ap, 0.0), writes=apc[i_].k())
            else:
                for i_ in range(2):
                    P.op("dve", lambda e, i_=i_: e.memset(apad[i_].ap, 0.0), writes=apad[i_].k())
            taps = [3, 4, 5] if isc else list(range(9))

            def a_proj(cc):
                ap_ = apad[cc % 2]
                if cc % 4 == 0:
                    ncb = min(4, NCC - cc)
                    wst["a"] = wload(wsrc(w_up, l, 0, 8, cc * 128, ncb * 128), 8, ncb * 128)
                wap4, wk = wst["a"]
                wap = wap4[:, :, (cc % 4) * 128:(cc % 4 + 1) * 128]
                diag_build(l, cc, taps)
                ps, pk = psum()
                for kc in range(8):
                    mm(ps[:, 0:ntok], wap[:, kc, :], hF.ap[:, kc, 0:ntok], kc == 0, kc == 7, wk + hF.k(kc), pk)
                if isc:
                    ac = apc[cc % 2]
                    P.op("act", lambda e: e.activation(out=ac.ap[:, 1:257], in_=ps[:, 0:256], func=AF.Copy), reads=pk, writes=ac.k())
                    return
                if lower:
                    psh, pkh = psum()
                    for kc in range(8):
                        mm(psh[:, 0:64], wap[:, kc, :], hF.ap[:, kc, G:G + 64], kc == 0, kc == 7, wk + hF.k(kc), pkh)
                P.op("act", lambda e: e.activation(out=ap_.ap[:, 1:9, 1:65], in_=ps[:, 0:512].rearrange("p (r c) -> p r c", c=64), func=AF.Copy), reads=pk, writes=ap_.k())
                P.op("dve", lambda e: e.tensor_copy(out=ap_.ap[:, 9, 1:65], in_=a_save.ap[:, cc, :]), reads=a_save.k(cc) + ap_.k(), writes=ap_.k())
                P.op("dve", lambda e: e.tensor_copy(out=a_save.ap[:, cc, :], in_=ap_.ap[:, 1, 1:65]), reads=ap_.k() + a_save.k(cc), writes=a_save.k(cc))
                if lower:
                    P.op("act", lambda e: e.activation(out=ap_.ap[:, 0, 1:65], in_=psh[:, 0:64], func=AF.Copy), reads=pkh + ap_.k(), writes=ap_.k())

            def val_proj(cc):
                if cc % 4 == 0:
                    ncb = min(4, NCC - cc)
                    wst["v"] = wload(wsrc(w_up, l, 0, 8, DFF + cc * 128, ncb * 128), 8, ncb * 128)
                wvp4, wvk = wst["v"]
                wvp = wvp4[:, :, (cc % 4) * 128:(cc % 4 + 1) * 128]
                psv, pkv = psum()
                for kc in range(8):
                    mm(psv[:, 0:ntok], wvp[:, kc, :], hF.ap[:, kc, 0:ntok], kc == 0, kc == 7, wvk + hF.k(kc), pkv)
                return psv, pkv

            def conv(cc, psv, pkv):
                d_ = dg[cc % 2]
                psc, pkc = psum()
                if isc:
                    ac = apc[cc % 2]
                    for i, j in enumerate(taps):
                        mm(psc[:, 0:256], d_.ap[:, j, :], ac.ap[:, j - 3:j - 3 + 256], i == 0, i == 2, d_.k() + ac.k(), pkc)
                else:
                    ap_ = apad[cc % 2]
                    for j in range(9):
                        dr, dc_ = j // 3, j % 3
                        mm(psc[:, 0:512].rearrange("p (r c) -> p r c", c=64), d_.ap[:, j, :], ap_.ap[:, dr:dr + 8, dc_:dc_ + 64], j == 0, j == 8, d_.k() + ap_.k(), pkc)
                ga_ = gact[cc % 2]
                P.op("act", lambda e: e.activation(out=ga_.ap[:, 0:ntok], in_=psc[:, 0:ntok], func=AF.Gelu, bias=V(l, "cb")[:, cc:cc + 1], scale=1.0),
                     reads=pkc + KV, writes=ga_.k())
                P.op("dve", lambda e: e.tensor_tensor(out=gv.ap[:, cc, 0:ntok], in0=psv[:, 0:ntok], in1=ga_.ap[:, 0:ntok], op=ALU.mult),
                     reads=pkv + ga_.k(), writes=gv.k(cc))

            a_proj(0)
            for cc in range(NCC):
                if cc + 1 < NCC:
                    a_proj(cc + 1)
                psv, pkv = val_proj(cc)
                conv(cc, psv, pkv)
            dump('f_gv', gv); dump('f_hF', hF)
            gcol = 11 if isc else 10
            for jb in range(2):
                banks = [psum() for _ in range(4)]
                for (k0, nk) in ((0, 8), (8, 8), (16, 6)):
                    wap, wk = wload(wsrc(w_dn, l, k0 * 128, nk, jb * 512, 512), nk, 512)
                    for j4 in range(4):
                        ps, pk = banks[j4]
                        for kk in range(nk):
                            cc = k0 + kk
                            mm(ps[:, 0:ntok], wap[:, kk, j4 * 128:(j4 + 1) * 128], gv.ap[:, cc, 0:ntok], cc == 0, cc == NCC - 1, wk + gv.k(cc), pk)
                for j4 in range(4):
                    ps, pk = banks[j4]
                    j = jb * 4 + j4
                    P.op("dve", lambda e, ps=ps, j=j: e.scalar_tensor_tensor(out=xb.ap[:, j, 0:ntok], in0=ps[:, 0:ntok], scalar=lv.ap[:, gcol, j:j + 1], in1=xb.ap[:, j, 0:ntok],
                         op0=ALU.mult, op1=ALU.add), reads=pk + lv.k() + xb.k(j), writes=xb.k(j))
            dump('f_x', xb)
            if not last:
                store_x(gi, slot)
            elif not isc:
                for kc in range(8):
                    P.op("act", lambda e, kc=kc: e.activation(out=sq.ap[:, kc, 0:ntok], in_=xb.ap[:, kc, 0:ntok], func=AF.Square), reads=xb.k(kc), writes=sq.k(kc))
                ps, pk = psum()
                for kc in range(8):
                    mm(ps[:, 0:ntok], ones_b.ap, sq.ap[:, kc, 0:ntok], kc == 0, kc == 7, ones_b.k() + sq.k(kc), pk)
                P.op("act", lambda e: e.activation(out=rstd.ap[:, 0:ntok], in_=ps[:, 0:ntok], func=AF.Ln, bias=EPS, scale=1.0), reads=pk, writes=rstd.k())
                P.op("act", lambda e: e.activation(out=rstd.ap[:, 0:ntok], in_=rstd.ap[:, 0:ntok], func=AF.Exp, scale=-0.5), reads=rstd.k(), writes=rstd.k())
                fg = vecs.ap[:, VG["fng"]:VG["fng"] + 8]
                for kc in range(8):
                    P.op("dve", lambda e, kc=kc: e.scalar_tensor_tensor(out=xb.ap[:, kc, 0:ntok], in0=xb.ap[:, kc, 0:ntok], scalar=fg[:, kc:kc + 1], in1=rstd.ap[:, 0:ntok],
                         op0=ALU.mult, op1=ALU.mult), reads=xb.k(kc) + KV + rstd.k(), writes=xb.k(kc))
                for t in range(ntok // 128):
                    for hh in range(2):
                        ps, pk = psum()
                        for j in range(4):
                            kc = hh * 4 + j
                            P.op("pe", lambda e, ps=ps, j=j, kc=kc, t=t: e.transpose(ps[:, j * 128:(j + 1) * 128], xb.ap[:, kc, t * 128:(t + 1) * 128], ident_f.ap),
                                 reads=xb.k(kc) + ident_f.k(), writes=pk)
                        fin = fin2[t % 2]
                        P.op("act", lambda e, ps=ps, hh=hh, fin=fin: e.activation(out=fin.ap[:, hh * 512:(hh + 1) * 512], in_=ps[:, 0:512], func=AF.Copy), reads=pk,
                             writes=fin.kb(hh * 2048, hh * 2048 + 2048))
                    r0 = c0 - CTX + t * 128
                    P.dma("sp", out[r0:r0 + 128, :], fin2[t % 2].ap, reads=fin2[t % 2].k(), writes=["out%d" % r0])

        ALLW = ["ada", "in", "bra", "brb", "o", "up", "dn"]
        cast_weights(0, ALLW)
        adaln(0)
        for l in range(n_layers):
            last = l == n_layers - 1
            layer_consts(l)
            if not last:
                cast_weights(l + 1, ALLW)
            pass1(l)
            P.op("act", lambda e: e.activation(out=S2b.ap, in_=S2w.ap, func=AF.Copy), reads=S2w.k(), writes=S2b.k())
            P.op("dve", lambda e: e.memset(a_save.ap, 0.0), writes=a_save.k())
            order = list(range(NLG, 0, -1))
            slot_of = {}
            for i, gi in enumerate(order):
                slot = i % 2
                slot_of[gi] = slot
                mixer(l, gi, slot)
                if i >= 1:
                    ffn(l, order[i - 1], slot_of[order[i - 1]], last)
            ffn(l, order[-1], slot_of[order[-1]], last)
            if not last:
                mixer(l, 0, 0)
                ffn(l, 0, 0, last)
                adaln(l + 1)
        P.final_wait_all("sp")
        P.emit()
    return nc


def _pack_vecs(inp, b):
    v = np.zeros((128, NV), np.float32)

    def put(col, arr):
        a = np.asarray(arr, np.float32).reshape(-1, 128).T
        v[:, col:col + a.shape[1]] = a

    for l in range(DEPTH):
        base = l * VPL
        put(base + VL["n1g"][0], inp["norm1_g"][l])
        put(base + VL["n2g"][0], inp["norm2_g"][l])
        put(base + VL["lng"][0], inp["sgu_ln_g"][l])
        put(base + VL["lnb"][0], inp["sgu_ln_b"][l])
        put(base + VL["gng"][0], inp["gla_norm_g"][l])
        put(base + VL["cb"][0], inp["ffn_conv_b"][l])
        cw = np.asarray(inp["ffn_conv_w"][l], np.float32).reshape(9, NCC, 128)
        v[:, base + VL["cw"][0]: base + VL["cw"][0] + NCC * 9] = cw.transpose(2, 1, 0).reshape(128, NCC * 9)
        put(base + VL["bada"][0], inp["b_ada"][l])
    put(VG["fng"], inp["final_norm_g"])
    put(VG["c"], inp["c"][b])
    put(VG["cctx"], inp["c_ctx"])
    return v


_NC_CACHE = {}


def kernel(**inp):
    inp = {k: np.asarray(v) for k, v in inp.items()}
    if "nc" not in _NC_CACHE:
        _NC_CACHE["nc"] = build_program()
    nc = _NC_CACHE["nc"]
    f32 = lambda a: np.ascontiguousarray(a, dtype=np.float32)
    wsT = f32(np.transpose(inp["sgu_w"], (0, 3, 1, 2)).reshape(DEPTH, 128, 1024))
    sgub = f32(inp["sgu_b"].reshape(DEPTH, 1, 1024))
    w2a = f32(np.concatenate([inp["gla_w2"], inp["gla_b2"][:, :, None, :]], axis=2))
    shared = {
        "w_ada": f32(inp["w_ada"]), "w_in": f32(inp["w_in"]), "wsT": wsT, "sgub": sgub, "w2a": w2a,
        "w_br_a": f32(inp["w_br_a"]), "w_br_b": f32(inp["w_br_b"]), "w_o": f32(inp["w_o"]),
        "ffn_w_up": f32(inp["ffn_w_up"]), "ffn_w_down": f32(inp["ffn_w_down"]),
    }
    in_maps = []
    for r in range(8):
        b = r % 4
        m = dict(shared)
        m["x_in"] = f32(inp["x"][b])
        m["ctx_in"] = f32(inp["ctx"][b])
        m["vecs"] = _pack_vecs(inp, b)
        in_maps.append(m)
    res = run_bass_kernel_spmd(nc, in_maps, core_ids=list(range(8)))
    outs = [np.asarray(res.results[b]["out"], np.float32) for b in range(4)]
    return np.stack(outs, axis=0)
```

```python
import contextlib
import types
import numpy as np
import concourse.bass as bass
import concourse.mybir as mybir
from concourse.bass_utils import run_bass_kernel_spmd

F32 = mybir.dt.float32
BF16 = mybir.dt.bfloat16
AF = mybir.ActivationFunctionType
ALU = mybir.AluOpType

D = 1024
SEQ = 4096
CTX = 256
DEPTH = 4
DIN = 7200
DFF = 2816
NCC = DFF // 128
EPS = 1e-6
OFF_Q = 2048
OFF_R = 2560
OFF_GA = 3584
OFF_GB = 4608
OFF_K = 5632
OFF_V = 6144
OFF_LR = 7168
G = 512
T = CTX + SEQ
NLG = SEQ // G

VL = {}
_o = 0
for _n, _w in (("n1g", 8), ("n2g", 8), ("lng", 8), ("lnb", 8), ("gng", 8), ("cb", NCC), ("cw", NCC * 9), ("bada", 48)):
    VL[_n] = (_o, _w)
    _o += _w
VPL = _o
VG = {"fng": DEPTH * VPL, "c": DEPTH * VPL + 8, "cctx": DEPTH * VPL + 16}
NV = DEPTH * VPL + 24


def _freeze(fn):
    if fn.__closure__ is None:
        return fn
    cells = []
    for c in fn.__closure__:
        try:
            cells.append(types.CellType(c.cell_contents))
        except ValueError:
            cells.append(c)
    return types.FunctionType(fn.__code__, fn.__globals__, fn.__name__, fn.__defaults__, tuple(cells))


class Prog:
    ENGS = ("pe", "act", "dve", "pool", "sp")

    def __init__(self, nc, es, ndma=8):
        self.nc = nc
        self.ins = {e: [] for e in self.ENGS}
        self.cnt = {e: 0 for e in self.ENGS}
        self.waited = {e: {} for e in self.ENGS}
        self.state = {}
        self.sems = {}
        for e in ("pe", "act", "dve", "pool"):
            self.sems[e] = es.enter_context(nc.semaphore("c_" + e))
        self.ndma = ndma
        self.dma_n = {}
        self.dma_cnt = {}
        for q in ("sp", "act", "pool"):
            self.dma_n[q] = 0
            for i in range(ndma):
                k = ("dma", q, i)
                self.sems[k] = es.enter_context(nc.semaphore("d_%s_%d" % (q, i)))
                self.dma_cnt[k] = 0

    def _st(self, key):
        s = self.state.get(key)
        if s is None:
            s = {"w": None, "r": {}}
            self.state[key] = s
        return s

    def _deps(self, eng, reads, writes):
        deps = {}

        def add(d):
            if d is None:
                return
            k, v = d
            if k == eng and eng == "pe":
                return
            if deps.get(k, 0) < v:
                deps[k] = v

        for r in reads:
            add(self._st(r)["w"])
        for w in writes:
            s = self._st(w)
            add(s["w"])
            for d in s["r"].values():
                add(d)
        out = []
        wd = self.waited[eng]
        for k, v in deps.items():
            if wd.get(k, 0) >= v:
                continue
            wd[k] = v
            out.append((k, v))
        return out

    def _commit(self, mydep, tag, reads, writes):
        ws = set(writes)
        for w in ws:
            s = self._st(w)
            s["w"] = mydep
            s["r"] = {}
        for r in reads:
            if r in ws:
                continue
            self._st(r)["r"][tag] = mydep

    def op(self, eng, fn, reads=(), writes=()):
        waits = self._deps(eng, reads, writes)
        self.cnt[eng] += 1
        mydep = (eng, self.cnt[eng])
        self.ins[eng].append((waits, _freeze(fn), (eng, 1)))
        self._commit(mydep, eng, reads, writes)

    def dma(self, q, out, in_, reads=(), writes=()):
        i = self.dma_n[q] % self.ndma
        self.dma_n[q] += 1
        k = ("dma", q, i)
        waits = self._deps(q, reads, writes)
        wd = self.waited[q]
        prev = self.dma_cnt[k] * 16
        if prev > 0 and wd.get(k, 0) < prev:
            wd[k] = prev
            waits.append((k, prev))
        self.dma_cnt[k] += 1
        mydep = (k, self.dma_cnt[k] * 16)
        self.ins[q].append((waits, lambda e: e.dma_start(out=out, in_=in_), (k, 16)))
        self._commit(mydep, k, reads, writes)

    def final_wait_all(self, eng="sp"):
        waits = []
        for e in ("pe", "act", "dve", "pool"):
            if self.cnt[e] > 0:
                waits.append((e, self.cnt[e]))
        for k, c in self.dma_cnt.items():
            if c > 0:
                waits.append((k, c * 16))
        self.ins[eng].append((waits, None, None))

    def emit(self):
        nc = self.nc
        with nc.Block() as block:
            def run(e, name):
                for waits, fn, inc in self.ins[name]:
                    for k, v in waits:
                        e.wait_ge(self.sems[k], v)
                    if fn is not None:
                        r = fn(e)
                        r.then_inc(self.sems[inc[0]], inc[1])

            @block.tensor
            def _(e):
                run(e, "pe")

            @block.scalar
            def _(e):
                run(e, "act")

            @block.vector
            def _(e):
                run(e, "dve")

            @block.gpsimd
            def _(e):
                run(e, "pool")

            @block.sync
            def _(e):
                run(e, "sp")


class Buf:
    GR = 512

    def __init__(self, ap, off, nbytes, sub=None):
        self.ap = ap
        self.off = off
        self.nbytes = nbytes
        self.sub = sub

    def k(self, i=None, n=1):
        if i is None:
            lo, hi = self.off, self.off + self.nbytes
        else:
            lo = self.off + i * self.sub
            hi = lo + n * self.sub
        return ["g%d" % j for j in range(lo // self.GR, (hi - 1) // self.GR + 1)]

    def kb(self, lo, hi):
        lo += self.off
        hi += self.off
        return ["g%d" % j for j in range(lo // self.GR, (hi - 1) // self.GR + 1)]


def build_program(n_layers=DEPTH, dbg=False):
    nc = bass.Bass("TRN2", target_bir_lowering=False)
    dt_in = lambda name, shape: nc.dram_tensor(name, shape, F32, kind="ExternalInput").ap()
    x_in = dt_in("x_in", [SEQ, D])
    ctx_in = dt_in("ctx_in", [CTX, D])
    vecs_in = dt_in("vecs", [128, NV])
    w_ada = dt_in("w_ada", [DEPTH, D, 6 * D])
    w_in = dt_in("w_in", [DEPTH, D, DIN])
    wsT_in = dt_in("wsT", [DEPTH, 128, 8 * 128])
    sgub_in = dt_in("sgub", [DEPTH, 1, 8 * 128])
    w2a_in = dt_in("w2a", [DEPTH, 2, 17, 512])
    w_bra = dt_in("w_br_a", [DEPTH, D, D])
    w_brb = dt_in("w_br_b", [DEPTH, D, D])
    w_o = dt_in("w_o", [DEPTH, D, D])
    w_up = dt_in("ffn_w_up", [DEPTH, D, 2 * DFF])
    w_dn = dt_in("ffn_w_down", [DEPTH, DFF, D])
    out = nc.dram_tensor("out", [SEQ, D], F32, kind="ExternalOutput").ap()
    xT = nc.dram_tensor("xT_scr", [D, T], F32).ap()
    s1save = nc.dram_tensor("s1save", [NLG, 128, 1024], F32).ap()
    xTv = xT.rearrange("(kc p) t -> p kc t", p=128)
    NGR = 1 + NLG
    ksc = nc.dram_tensor("ksc", [NGR, 128, (G // 128) * 512], BF16).ap()
    vsc = nc.dram_tensor("vsc", [NGR, 128, (G // 128) * 1024], BF16).ap()
    lsc = nc.dram_tensor("lsc", [NGR, 64, G], BF16).ap()
    WSRC = {"ada": w_ada, "in": w_in, "bra": w_bra, "brb": w_brb, "o": w_o, "up": w_up, "dn": w_dn}
    WB = {n: nc.dram_tensor("wb_" + n, list(a.shape), BF16).ap() for n, a in WSRC.items()}

    with contextlib.ExitStack() as es:
        P = Prog(nc, es)
        arena = es.enter_context(nc.sbuf_tensor("arena", [128, 200 * 256], F32))
        cur = [0]

        def alloc(shape, dt, at=None):
            esz = 4 if dt == F32 else 2
            n = 1
            for s in shape[1:]:
                n *= s
            nbytes = n * esz
            if at is None:
                off = cur[0]
                cur[0] = (off + nbytes + Buf.GR - 1) // Buf.GR * Buf.GR
            else:
                off = at
            assert off + nbytes <= 200 * 1024, (off, nbytes)
            ap = arena[0:shape[0], off // 4:(off + nbytes) // 4]
            if dt == BF16:
                ap = ap.bitcast(BF16)
            if len(shape) == 3:
                ap = ap.rearrange("p (a b) -> p a b", b=shape[2])
            elif len(shape) == 4:
                ap = ap.rearrange("p (a b c) -> p a b c", b=shape[2], c=shape[3])
            sub = nbytes // shape[1] if len(shape) >= 3 else None
            return Buf(ap, off, nbytes, sub)

        psb = [es.enter_context(nc.psum_tensor("ps%d" % i, [128, 512], F32)) for i in range(7)]
        pst = es.enter_context(nc.psum_tensor("pst", [128, 1024], BF16))
        psn = [0]

        def psum():
            i = psn[0] % 7
            psn[0] += 1
            return psb[i], ["ps%d" % i]

        dumped = set()

        def dump(name, buf, once=True):
            if not dbg or (once and name in dumped):
                return
            dumped.add(name)
            ap = buf.ap
            shp = list(ap.shape)
            n = 1
            for s_ in shp[1:]:
                n *= s_
            dten = nc.dram_tensor("dbg_" + name, [shp[0], n], F32, kind="ExternalOutput").ap()
            src = ap
            if len(shp) == 3:
                src = ap.rearrange("p a b -> p (a b)")
            elif len(shp) == 4:
                src = ap.rearrange("p a b c -> p (a b c)")
            P.dma("pool", dten, src, reads=buf.k(), writes=["dbg_" + name])

        vecs = alloc([128, NV], F32)
        mods = alloc([128, DEPTH * 2, 48], F32)
        lv = alloc([128, 12, 8], F32)
        ident_f = alloc([128, 128], F32)
        ident_b = alloc([128, 128], BF16)
        ones_b = alloc([128, 128], BF16)
        tcm2 = alloc([128, 2, 128], BF16)
        tbx2 = alloc([128, 2, 128], BF16)
        mk2 = alloc([128, 2, 4, 128], BF16)
        wsT = alloc([128, 8, 128], BF16)
        bsrow = alloc([1, 1024], BF16)
        w2a = alloc([64, 512], BF16)
        T2 = alloc([128, 8, 128], F32)
        lngB = alloc([128, 8, 128], BF16)
        scb = alloc([128, 8, 2], BF16)
        sttA = [alloc([128, 8], F32) for _ in range(2)]
        sttB = [alloc([128, 8], F32) for _ in range(2)]
        S1w = alloc([128, 1024], F32)
        S1b = alloc([128, 1024], BF16)
        S2w = alloc([128, 1024], F32)
        S2b = alloc([128, 1024], BF16)
        a_save = alloc([128, NCC, 64], BF16)
        lrT = alloc([64, G], BF16)
        wbufs = [alloc([128, 8 * 512], BF16) for _ in range(3)]
        xg = [alloc([128, 8, G], F32) for _ in range(2)]
        rstd = alloc([128, G], F32)
        tmpf = [alloc([128, G], F32) for _ in range(2)]
        hT = alloc([128, 8, G], BF16)
        hF = alloc([128, 8, G + 64], BF16)
        mA = alloc([128, 8, G], BF16)
        obT = alloc([128, 8, G], BF16)
        sq = alloc([128, 8, G], BF16, at=obT.off)
        spx = alloc([128, 512], F32)
        spb = [alloc([128, 512], BF16) for _ in range(2)]
        E1 = spx
        kdec = [alloc([128, 512], BF16) for _ in range(3)]
        EqT = alloc([128, 4, 128], F32)
        EkT = alloc([128, 4, 128], F32)
        decb = [alloc([128, 4, 2], F32) for _ in range(3)]
        qdT = [alloc([128, 4, 128], BF16) for _ in range(3)]
        kiT = [alloc([128, 4, 128], BF16) for _ in range(2)]
        scT = [alloc([128, 4, 128], BF16) for _ in range(2)]
        base = cur[0]
        NT = G // 128
        tcm = alloc([128, 2, 64], F32, at=base)
        tbx = alloc([128, 2, 65], F32, at=base + 1024)
        mk = alloc([128, 2, 4, 64], F32, at=base + 2048)
        uT = alloc([128, 8, G], BF16, at=base)
        vh = alloc([128, NT, 1024], BF16, at=base + 8192)
        gvs = [alloc([128, 1024], F32, at=base + 16384 + 4096 * i) for i in range(2)]
        aT = alloc([128, 8, G], BF16, at=base + 24576)
        qT = alloc([128, 4, G], BF16, at=base)
        kT = alloc([128, 4, G], BF16, at=base + 4096)
        ktm = alloc([128, NT, 512], BF16, at=base + 8192)
        vtm = alloc([128, NT, 1024], BF16, at=base + 12288)
        o1 = alloc([128, NT, 1024], F32, at=base + 20480)
        rs = [alloc([128, 1024], BF16, at=base + 36864 + 2048 * i) for i in range(2)]
        ob = [alloc([128, 1024], BF16, at=base + 40960 + 2048 * i) for i in range(2)]
        mg = alloc([128, 8, G], BF16, at=base)
        ktm0, vtm0, lrT0 = ktm, vtm, lrT
        ktm2 = alloc([128, NT, 512], BF16, at=base + 20480)
        vtm2 = alloc([128, NT, 1024], BF16, at=base + 24576)
        lrT2 = alloc([64, G], BF16, at=base + 32768)
        gv = alloc([128, NCC, G], BF16, at=base)
        apad = [alloc([128, 10, 66], BF16, at=base + 22528 + 1536 * i) for i in range(2)]
        dg = [alloc([128, 9, 128], BF16, at=base + 25600 + 2560 * i) for i in range(2)]
        gact = [alloc([128, G], BF16, at=base + 30720 + 1024 * i) for i in range(2)]
        apc = [alloc([128, 256 + 4], BF16, at=base + 32768 + 512 * i) for i in range(2)]
        fin2 = [alloc([128, 1024], F32, at=base + 33792 + 4096 * i) for i in range(2)]
        cur[0] = base + 45056
        assert cur[0] <= 200 * 1024, cur[0]

        V = lambda l, name: vecs.ap[:, l * VPL + VL[name][0]: l * VPL + VL[name][0] + VL[name][1]]
        KV = vecs.k()

        P.dma("sp", vecs.ap, vecs_in, writes=KV)
        P.op("pool", lambda e: e.memset(ident_f.ap, 0.0), writes=ident_f.k())
        P.op("pool", lambda e: e.affine_select(out=ident_f.ap, in_=ident_f.ap, pattern=[[-1, 128]], compare_op=ALU.not_equal,
                                               fill=1.0, base=0, channel_multiplier=1), reads=ident_f.k(), writes=ident_f.k())
        P.op("dve", lambda e: e.tensor_copy(out=ident_b.ap, in_=ident_f.ap), reads=ident_f.k(), writes=ident_b.k())
        P.op("dve", lambda e: e.memset(ones_b.ap, 1.0 / 1024.0), writes=ones_b.k())
        P.op("dve", lambda e: e.memset(lrT.ap, 1.0), writes=lrT.k())
        P.op("dve", lambda e: e.memset(a_save.ap, 0.0), writes=a_save.k())
        NEG = -1.0 / 16.0
        hs = slice(0, 64)
        for d in range(2):
            P.op("pool", lambda e, d=d: e.memset(tcm.ap[hs, d, :], NEG), writes=tcm.k())
            P.op("pool", lambda e, d=d: e.memset(tbx.ap[hs, d, :], NEG), writes=tbx.k())
            P.op("pool", lambda e, d=d: e.memset(mk.ap[hs, d, :, :], 1.0), writes=mk.k())
        P.op("pool", lambda e: e.affine_select(out=tcm.ap[hs, 0, :], in_=tcm.ap[hs, 0, :], pattern=[[-1, 64]],
             compare_op=ALU.is_gt, fill=0.0, base=0, channel_multiplier=1), reads=tcm.k(), writes=tcm.k())
        P.op("pool", lambda e: e.affine_select(out=tcm.ap[hs, 1, :], in_=tcm.ap[hs, 1, :], pattern=[[1, 64]],
             compare_op=ALU.is_gt, fill=0.0, base=0, channel_multiplier=-1), reads=tcm.k(), writes=tcm.k())
        P.op("pool", lambda e: e.affine_select(out=tbx.ap[hs, 0, 0:64], in_=tbx.ap[hs, 0, 0:64], pattern=[[1, 64]],
             compare_op=ALU.is_ge, fill=0.0, base=0, channel_multiplier=-1), reads=tbx.k(), writes=tbx.k())
        P.op("pool", lambda e: e.affine_select(out=tbx.ap[hs, 1, 0:64], in_=tbx.ap[hs, 1, 0:64], pattern=[[-1, 64]],
             compare_op=ALU.is_ge, fill=0.0, base=0, channel_multiplier=1), reads=tbx.k(), writes=tbx.k())
        for h in range(4):
            P.op("pool", lambda e, h=h: e.affine_select(out=mk.ap[hs, 0, h, :], in_=mk.ap[hs, 0, h, :], pattern=[[1, 64]],
                 compare_op=ALU.is_ge, fill=0.0, base=0, channel_multiplier=-1), reads=mk.k(), writes=mk.k())
            P.op("pool", lambda e, h=h: e.affine_select(out=mk.ap[hs, 1, h, :], in_=mk.ap[hs, 1, h, :], pattern=[[-1, 64]],
                 compare_op=ALU.is_ge, fill=0.0, base=0, channel_multiplier=1), reads=mk.k(), writes=mk.k())
        for b_ in (tcm, tbx, mk):
            P.dma("sp", b_.ap[64:128], b_.ap[0:64], reads=b_.k(), writes=b_.k())
        for b_ in (tcm2, tbx2, mk2):
            P.op("dve", lambda e, b_=b_: e.memset(b_.ap, 0.0), writes=b_.k())
        for j_ in range(2):
            hs_ = slice(j_ * 64, j_ * 64 + 64)
            P.op("dve", lambda e, hs_=hs_, j_=j_: e.tensor_copy(out=tcm2.ap[hs_, :, j_ * 64:(j_ + 1) * 64], in_=tcm.ap[hs_, :, :]), reads=tcm.k() + tcm2.k(), writes=tcm2.k())
            P.op("dve", lambda e, hs_=hs_, j_=j_: e.tensor_copy(out=tbx2.ap[hs_, :, j_ * 64:(j_ + 1) * 64], in_=tbx.ap[hs_, :, 0:64]), reads=tbx.k() + tbx2.k(), writes=tbx2.k())
            P.op("dve", lambda e, hs_=hs_, j_=j_: e.tensor_copy(out=mk2.ap[hs_, :, :, j_ * 64:(j_ + 1) * 64], in_=mk.ap[hs_, :, :, :]), reads=mk.k() + mk2.k(), writes=mk2.k())

        wn = [0]

        def cast_weights(l, names):
            for n in names:
                src, dst = WSRC[n], WB[n]
                for rc in range(src.shape[1] // 128):
                    P.dma("pool", dst[l, rc * 128:(rc + 1) * 128, :], src[l, rc * 128:(rc + 1) * 128, :], writes=["wb_%s_%d_%d" % (n, l, rc)])

        def wload(src, nkc, ncols, slot=None):
            n, l, r0, nkc_, c0, ncl = src
            if slot is None:
                i = wn[0] % 3
                wn[0] += 1
            else:
                i = slot
            b = wbufs[i]
            ap = b.ap[:, 0:nkc * ncols].rearrange("p (a b) -> p a b", b=ncols)
            sap = WB[n][l, r0:r0 + nkc * 128, c0:c0 + ncols].rearrange("(kc p) n -> p kc n", p=128)
            rk = ["wb_%s_%d_%d" % (n, l, r0 // 128 + k) for k in range(nkc)]
            P.dma("sp", ap, sap, reads=rk, writes=b.k())
            return ap, b.k()

        WNAME = {id(w_ada): "ada", id(w_in): "in", id(w_bra): "bra", id(w_brb): "brb", id(w_o): "o", id(w_up): "up", id(w_dn): "dn"}

        def wsrc(w, l, r0, nkc, c0, ncols):
            return (WNAME[id(w)], l, r0, nkc, c0, ncols)

        def mm(outap, lhsT, rhs, start, stop, reads, writes):
            P.op("pe", lambda e: e.matmul(outap, lhsT=lhsT, rhs=rhs, start=start, stop=stop), reads=reads, writes=writes)

        def proj_fm(w, l, col0, nchunks, rhs, ntok, handler, nkc=8, r0=0):
            for b0 in range(0, nchunks, 4):
                nb = min(4, nchunks - b0)
                wap, wk = wload(wsrc(w, l, r0, nkc, col0 + b0 * 128, nb * 128), nkc, nb * 128)
                for j in range(nb):
                    ps, pk = psum()
                    for kc in range(nkc):
                        mm(ps[:, 0:ntok], wap[:, kc, j * 128:(j + 1) * 128], rhs.ap[:, kc, 0:ntok], kc == 0, kc == nkc - 1,
                           wk + rhs.k(kc), pk)
                    handler(b0 + j, ps, pk)

        def proj_tm(w, l, col0, ncols, lhs, ntiles, handler):
            for c0 in range(0, ncols, 512):
                ncl = min(512, ncols - c0)
                wap, wk = wload(wsrc(w, l, 0, 8, col0 + c0, ncl), 8, ncl)
                for t in range(ntiles):
                    ps, pk = psum()
                    for kc in range(8):
                        mm(ps[:, 0:ncl], lhs.ap[:, kc, t * 128:(t + 1) * 128], wap[:, kc, :], kc == 0, kc == 7, wk + lhs.k(kc), pk)
                    handler(t, c0, ncl, ps, pk)

        def make_h(xb, x0, ntok, Acol, Bcol, hb, h0):
            for kc in range(8):
                P.op("act", lambda e, kc=kc: e.activation(out=sq.ap[:, kc, 0:ntok], in_=xb.ap[:, kc, x0:x0 + ntok], func=AF.Square),
                     reads=xb.k(kc), writes=sq.k(kc))
            ps, pk = psum()
            for kc in range(8):
                mm(ps[:, 0:ntok], ones_b.ap, sq.ap[:, kc, 0:ntok], kc == 0, kc == 7, ones_b.k() + sq.k(kc), pk)
            P.op("act", lambda e: e.activation(out=rstd.ap[:, 0:ntok], in_=ps[:, 0:ntok], func=AF.Ln, bias=EPS, scale=1.0),
                 reads=pk, writes=rstd.k())
            P.op("act", lambda e: e.activation(out=rstd.ap[:, 0:ntok], in_=rstd.ap[:, 0:ntok], func=AF.Exp, scale=-0.5),
                 reads=rstd.k(), writes=rstd.k())
            for kc in range(8):
                tf = tmpf[kc % 2]
                P.op("dve", lambda e, kc=kc, tf=tf: e.scalar_tensor_tensor(out=tf.ap[:, 0:ntok], in0=xb.ap[:, kc, x0:x0 + ntok],
                     scalar=lv.ap[:, Acol, kc:kc + 1], in1=rstd.ap[:, 0:ntok], op0=ALU.mult, op1=ALU.mult),
                     reads=xb.k(kc) + lv.k() + rstd.k(), writes=tf.k())
                P.op("act", lambda e, kc=kc, tf=tf: e.activation(out=hb.ap[:, kc, h0:h0 + ntok], in_=tf.ap[:, 0:ntok], func=AF.Identity,
                     bias=lv.ap[:, Bcol, kc:kc + 1], scale=1.0), reads=tf.k() + lv.k(), writes=hb.k(kc))

        def load_tokens(src, nrows, tcol0):
            for t in range(nrows // 128):
                xb = xg[t % 2]
                stg = gvs[t % 2]
                P.dma("sp", stg.ap, src[t * 128:(t + 1) * 128, :], writes=stg.k())
                for hh in range(2):
                    ps, pk = psum()
                    for j in range(4):
                        kc = hh * 4 + j
                        P.op("pe", lambda e, ps=ps, j=j, kc=kc, stg=stg: e.transpose(ps[:, j * 128:(j + 1) * 128], stg.ap[:, kc * 128:(kc + 1) * 128], ident_f.ap),
                             reads=stg.k() + ident_f.k(), writes=pk)
                    P.op("dve", lambda e, ps=ps, hh=hh, xb=xb: e.tensor_copy(out=xb.ap[:, hh * 4:hh * 4 + 4, 0:128],
                         in_=ps[:, :].rearrange("p (a b) -> p a b", b=128)), reads=pk, writes=xb.k())
                P.dma("sp", xTv[:, :, tcol0 + t * 128: tcol0 + (t + 1) * 128], xb.ap[:, :, 0:128], reads=xb.k(), writes=["xT%d" % (0 if tcol0 == 0 else 1 + t // (G // 128))])

        load_tokens(ctx_in, CTX, 0)
        load_tokens(x_in, SEQ, CTX)

        cs = vecs.ap[:, VG["c"]:VG["c"] + 16].rearrange("p (two kc) -> p kc two", two=2)
        P.op("act", lambda e: e.activation(out=scb.ap, in_=cs, func=AF.Silu), reads=KV, writes=scb.k())
        def adaln(l):
                ps, pk = psum()
                for blk in range(12):
                    wap, wk = wload(wsrc(w_ada, l, 0, 8, blk * 512, 512), 8, 512)
                    for j in range(4):
                        col = (blk * 4 + j) * 2
                        for kc in range(8):
                            mm(ps[:, col:col + 2], wap[:, kc, j * 128:(j + 1) * 128], scb.ap[:, kc, :], kc == 0, kc == 7, wk + scb.k(), pk)
                psv = ps[:, 0:96].rearrange("p (c two) -> p two c", two=2)
                for two in range(2):
                    P.op("dve", lambda e, two=two, l=l, psv=psv: e.tensor_tensor(out=mods.ap[:, l * 2 + two, :], in0=psv[:, two, :], in1=V(l, "bada"), op=ALU.add),
                         reads=pk + KV, writes=mods.k(l * 2 + two))

        dump('mods', mods)
        groups = [(0, CTX, True)] + [(CTX + i * G, G, False) for i in range(NLG)]
        QS = 128.0 ** -0.5

        def layer_consts(l):
            for two in range(2):
                m = mods.ap[:, l * 2 + two, :]
                rk = mods.k(l * 2 + two) + KV
                P.op("dve", lambda e, m=m, two=two: e.scalar_tensor_tensor(out=lv.ap[:, 0 + 2 * two, :], in0=m[:, 8:16], scalar=1.0, in1=V(l, "n1g"),
                     op0=ALU.add, op1=ALU.mult), reads=rk, writes=lv.k())
                P.op("dve", lambda e, m=m, two=two: e.tensor_copy(out=lv.ap[:, 1 + 2 * two, :], in_=m[:, 0:8]), reads=rk, writes=lv.k())
                P.op("dve", lambda e, m=m, two=two: e.scalar_tensor_tensor(out=lv.ap[:, 4 + 2 * two, :], in0=m[:, 32:40], scalar=1.0, in1=V(l, "n2g"),
                     op0=ALU.add, op1=ALU.mult), reads=rk, writes=lv.k())
                P.op("dve", lambda e, m=m, two=two: e.tensor_copy(out=lv.ap[:, 5 + 2 * two, :], in_=m[:, 24:32]), reads=rk, writes=lv.k())
                P.op("dve", lambda e, m=m, two=two: e.tensor_copy(out=lv.ap[:, 8 + two, :], in_=m[:, 16:24]), reads=rk, writes=lv.k())
                P.op("dve", lambda e, m=m, two=two: e.tensor_copy(out=lv.ap[:, 10 + two, :], in_=m[:, 40:48]), reads=rk, writes=lv.k())
            P.dma("pool", wsT.ap.rearrange("p a b -> p (a b)"), wsT_in[l], writes=wsT.k())
            P.dma("pool", bsrow.ap, sgub_in[l], writes=bsrow.k())
            for d in range(2):
                P.dma("pool", w2a.ap[d * 32:d * 32 + 17, :], w2a_in[l, d], writes=w2a.k())
            for g in range(8):
                ps, pk = psum()
                mm(ps[:, 0:128], ones1.ap, wsT.ap[:, g, :], True, True, ones1.k() + wsT.k(), pk)
                mm(ps[:, 128:256], ones1.ap[0:1, :], bsrow.ap[0:1, g * 128:(g + 1) * 128], True, True, ones1.k() + bsrow.k(), pk)
                tf = tmpf[g % 2]
                P.op("act", lambda e, ps=ps, tf=tf: e.activation(out=tf.ap[:, 0:128], in_=ps[:, 128:256], func=AF.Copy), reads=pk, writes=tf.k())
                P.op("dve", lambda e, ps=ps, tf=tf, g=g: e.scalar_tensor_tensor(out=T2.ap[:, g, :], in0=ps[:, 0:128], scalar=V(l, "lnb")[:, g:g + 1],
                     in1=tf.ap[:, 0:128], op0=ALU.mult, op1=ALU.add), reads=pk + tf.k() + KV, writes=T2.k(g))
                P.op("dve", lambda e, g=g: e.tensor_scalar(out=lngB.ap[:, g, :], in0=ones1.ap, scalar1=V(l, "lng")[:, g:g + 1], scalar2=None, op0=ALU.mult),
                     reads=ones1.k() + KV, writes=lngB.k(g))

        ones1 = alloc([128, 128], BF16)
        P.op("dve", lambda e: e.memset(ones1.ap, 1.0), writes=ones1.k())

        def lr_proj(l, ntok, hb=None, lrT_=None):
            hb = hb or hT
            lrT_ = lrT_ or lrT
            wap, wk = wload(wsrc(w_in, l, 0, 8, OFF_LR, 32), 8, 32)
            ps, pk = psum()
            for d in range(2):
                for kc in range(8):
                    mm(ps[d * 32:d * 32 + 16, 0:ntok], wap[:, kc, d * 16:(d + 1) * 16], hb.ap[:, kc, 0:ntok], kc == 0, kc == 7, wk + hb.k(kc), pk)
            for d in range(2):
                P.op("act", lambda e, d=d, ps=ps: e.activation(out=lrT_.ap[d * 32:d * 32 + 16, 0:ntok], in_=ps[d * 32:d * 32 + 16, 0:ntok], func=AF.Copy),
                     reads=pk, writes=lrT_.k())

        def kv_thunks(l, ntok, hb=None, ktm_=None, vtm_=None):
            hb = hb or hT
            ktm_ = ktm_ or ktm
            vtm_ = vtm_ or vtm
            nt = ntok // 128
            th = []
            wst_ = {}
            for (off, dstk) in ((OFF_K, "k"), (OFF_V, "v0"), (OFF_V + 512, "v1")):
                for t in range(nt):
                    def f(off=off, dstk=dstk, t=t):
                        if t == 0:
                            wst_[dstk] = wload(wsrc(w_in, l, 0, 8, off, 512), 8, 512)
                        wap, wk = wst_[dstk]
                        ps, pk = psum()
                        for kc in range(8):
                            mm(ps[:, 0:512], hb.ap[:, kc, t * 128:(t + 1) * 128], wap[:, kc, :], kc == 0, kc == 7, wk + hb.k(kc), pk)
                        if dstk == "k":
                            P.op("act", lambda e: e.activation(out=ktm_.ap[:, t, :], in_=ps[:, 0:512], func=AF.Copy), reads=pk, writes=ktm_.k(t))
                        else:
                            c0 = 0 if dstk == "v0" else 512
                            P.op("dve", lambda e: e.tensor_copy(out=vtm_.ap[:, t, c0:c0 + 512], in_=ps[:, 0:512]), reads=pk, writes=vtm_.kb(t * 2048 + c0 * 2, t * 2048 + c0 * 2 + 1024))
                    th.append(f)
            return th

        def kv_proj(l, ntok):
            for f in kv_thunks(l, ntok):
                f()

        ucnt = [0]

        def gla_run(units, Smap, with_out, first_fn=None, bufs=None, fillers=()):
            U = len(units)
            ub = ucnt[0]
            ucnt[0] += U
            ktm, vtm, lrT = bufs if bufs is not None else (ktm0, vtm0, lrT0)
            fillers = list(fillers)
            st_ = [dict() for _ in units]

            def geo(u):
                d, t = units[u]
                return d, t, slice(t * 128, (t + 1) * 128), (ub + u) % 2

            def b3_(u):
                return (ub + u) % 3

            def S0(u):
                d, t, tok, b = geo(u)
                sp_ = spb[b]
                ps, pk = psum()
                mm(ps[:, :], lrT.ap[d * 32:d * 32 + 17, tok], w2a.ap[d * 32:d * 32 + 17, :], True, True, lrT.k() + w2a.k(), pk)
                P.op("act", lambda e: e.activation(out=spx.ap, in_=ps[:, :], func=AF.Exp, scale=-1.0), reads=pk, writes=spx.k())
                P.op("act", lambda e: e.activation(out=sp_.ap, in_=spx.ap, func=AF.Ln, bias=1.0, scale=1.0), reads=spx.k(), writes=sp_.k())

            def S1(u):
                d, t, tok, b = geo(u)
                sp_, kd, dc = spb[b], kdec[b3_(u)], decb[b3_(u)]
                ps2, pk2 = psum()
                mm(ps2[:, :], tcm2.ap[:, d, :], sp_.ap, True, True, tcm2.k() + sp_.k(), pk2)
                P.op("act", lambda e: e.activation(out=E1.ap, in_=ps2[:, :], func=AF.Exp), reads=pk2, writes=E1.k())
                P.op("dve", lambda e: e.tensor_tensor(out=kd.ap, in0=ktm.ap[:, t, :], in1=E1.ap, op=ALU.mult), reads=ktm.k(t) + E1.k(), writes=kd.k())
                if with_out:
                    qd, ki = qdT[b3_(u)], kiT[b]
                    ps3, pk3 = psum()
                    for h in range(4):
                        mm(ps3[:, h * 128:(h + 1) * 128], sp_.ap[:, h * 128:(h + 1) * 128], tbx2.ap[:, d, :], True, True, sp_.k() + tbx2.k(), pk3)
                    p3v = ps3[:, :].rearrange("p (h j c) -> p h j c", j=2, c=64)
                    lastc = 63 if d == 0 else 0
                    P.op("act", lambda e: e.activation(out=dc.ap, in_=p3v[:, :, :, lastc], func=AF.Exp), reads=pk3, writes=dc.k())
                    P.op("act", lambda e: e.activation(out=EqT.ap, in_=ps3[:, :].rearrange("p (h c) -> p h c", c=128), func=AF.Exp), reads=pk3, writes=EqT.k())
                    P.op("act", lambda e: e.activation(out=EkT.ap, in_=ps3[:, :].rearrange("p (h c) -> p h c", c=128), func=AF.Exp, scale=-1.0), reads=pk3, writes=EkT.k())
                    P.op("dve", lambda e: e.scalar_tensor_tensor(out=qd.ap, in0=qT.ap[:, :, tok], scalar=QS, in1=EqT.ap, op0=ALU.mult, op1=ALU.mult),
                         reads=qT.k() + EqT.k(), writes=qd.k())
                    P.op("dve", lambda e: e.tensor_tensor(out=ki.ap, in0=kT.ap[:, :, tok], in1=EkT.ap, op=ALU.mult), reads=kT.k() + EkT.k(), writes=ki.k())
                else:
                    ps3, pk3 = psum()
                    for h in range(4):
                        for j in range(2):
                            lc = j * 64 + (63 if d == 0 else 0)
                            mm(ps3[:, h * 2 + j:h * 2 + j + 1], sp_.ap[:, h * 128:(h + 1) * 128], tbx2.ap[:, d, lc:lc + 1], True, True, sp_.k() + tbx2.k(), pk3)
                    P.op("act", lambda e: e.activation(out=dc.ap, in_=ps3[:, 0:8].rearrange("p (h j) -> p h j", j=2), func=AF.Exp), reads=pk3, writes=dc.k())

            def S2(u):
                if not with_out:
                    return
                d, t, tok, b = geo(u)
                qd, ki, sc = qdT[b3_(u)], kiT[b], scT[b]
                ps4, pk4 = psum()
                for h in range(4):
                    mm(ps4[:, h * 128:(h + 1) * 128], ki.ap[:, h, :], qd.ap[:, h, :], True, True, ki.k() + qd.k(), pk4)
                P.op("dve", lambda e: e.tensor_tensor(out=sc.ap, in0=ps4[:, :].rearrange("p (h c) -> p h c", c=128), in1=mk2.ap[:, d, :, :], op=ALU.mult),
                     reads=pk4 + mk2.k(), writes=sc.k())

            def S3chunk(u, jc):
                d, t, tok, b = geo(u)
                kd, dc = kdec[b3_(u)], decb[b3_(u)]
                Sw, Sb = Smap[d]
                par = slice(jc * 64, jc * 64 + 64)
                if with_out:
                    qd = qdT[b3_(u)]
                    for hh in range(2):
                        ps7, pk7 = psum()
                        for j in range(2):
                            h = hh * 2 + j
                            mm(ps7[par, j * 256:(j + 1) * 256], qd.ap[:, h, jc * 64:(jc + 1) * 64], Sb.ap[:, h * 256:(h + 1) * 256], True, True, qd.k() + Sb.k(), pk7)
                        osl = o1.ap[par, t, hh * 512:(hh + 1) * 512]
                        ok_ = o1.kb(t * 4096 + hh * 2048, t * 4096 + hh * 2048 + 2048)
                        P.op("dve", lambda e, ps7=ps7, osl=osl: e.tensor_tensor(out=osl, in0=ps7[par, :], in1=osl, op=ALU.add), reads=pk7 + ok_, writes=ok_)
                for hh in range(2):
                    ps6, pk6 = psum()
                    for j in range(2):
                        h = hh * 2 + j
                        mm(ps6[:, j * 256:(j + 1) * 256], kd.ap[par, h * 128:(h + 1) * 128], vtm.ap[par, t, h * 256:(h + 1) * 256], True, True, kd.k() + vtm.k(t), pk6)
                    for j in range(2):
                        h = hh * 2 + j
                        hk = Sw.kb(h * 1024, h * 1024 + 1024)
                        P.op("dve", lambda e, h=h, j=j, ps6=ps6: e.scalar_tensor_tensor(out=Sw.ap[:, h * 256:(h + 1) * 256], in0=Sw.ap[:, h * 256:(h + 1) * 256], scalar=dc.ap[:, h, jc:jc + 1],
                             in1=ps6[:, j * 256:(j + 1) * 256], op0=ALU.mult, op1=ALU.add), reads=hk + dc.k() + pk6, writes=hk)
                if with_out:
                    P.op("act", lambda e: e.activation(out=Sb.ap, in_=Sw.ap, func=AF.Copy), reads=Sw.k(), writes=Sb.k())

            def S3a(u):
                d, t, tok, b = geo(u)
                if with_out:
                    sc = scT[b]
                    first = first_fn(d, t)
                    for hh in range(2):
                        ps5, pk5 = psum()
                        for j in range(2):
                            h = hh * 2 + j
                            mm(ps5[:, j * 256:(j + 1) * 256], sc.ap[:, h, :], vtm.ap[:, t, h * 256:(h + 1) * 256], True, True, sc.k() + vtm.k(t), pk5)
                        osl = o1.ap[:, t, hh * 512:(hh + 1) * 512]
                        ok_ = o1.kb(t * 4096 + hh * 2048, t * 4096 + hh * 2048 + 2048)
                        if first:
                            P.op("act", lambda e, ps5=ps5, osl=osl: e.activation(out=osl, in_=ps5[:, :], func=AF.Copy), reads=pk5, writes=ok_)
                        else:
                            P.op("dve", lambda e, ps5=ps5, osl=osl: e.tensor_tensor(out=osl, in0=ps5[:, :], in1=osl, op=ALU.add), reads=pk5 + ok_, writes=ok_)
                S3chunk(u, 0 if d == 0 else 1)

            def S3b(u):
                d = units[u][0]
                S3chunk(u, 1 if d == 0 else 0)

            order = [(S3a, 3), (S3b, 4), (S2, 2), (S1, 1), (S0, 0)]
            nsteps = U + 4
            perstep = (len(fillers) + nsteps - 1) // nsteps if fillers else 0
            for step in range(nsteps):
                for fn_, s_ in order:
                    u = step - s_
                    if 0 <= u < U:
                        fn_(u)
                for _ in range(perstep):
                    if fillers:
                        fillers.pop(0)()
            while fillers:
                fillers.pop(0)()

        def interleave(n):
            units = []
            for i in range(n):
                units.append((0, i))
                units.append((1, n - 1 - i))
            return units

        def zero_state(Sw, Sb):
            P.op("dve", lambda e: e.memset(Sw.ap, 0.0), writes=Sw.k())
            P.op("dve", lambda e: e.memset(Sb.ap, 0.0), writes=Sb.k())

        def load_x(gi, slot):
            c0, ntok, isc = groups[gi]
            P.dma("sp", xg[slot].ap[:, :, 0:ntok], xTv[:, :, c0:c0 + ntok], reads=["xT%d" % gi], writes=xg[slot].k())

        def store_x(gi, slot):
            c0, ntok, isc = groups[gi]
            P.dma("sp", xTv[:, :, c0:c0 + ntok], xg[slot].ap[:, :, 0:ntok], reads=xg[slot].k(), writes=["xT%d" % gi])

        def pass1(l):
            zero_state(S1w, S1b)
            zero_state(S2w, S2b)
            P.op("dve", lambda e: e.memset(lrT2.ap, 1.0), writes=lrT2.k())
            sets = [(hT, ktm, vtm, lrT), (hF, ktm2, vtm2, lrT2)]

            def prep_thunks(gi):
                c0, ntok, isc = groups[gi]
                slot = gi % 2
                hb, k_, v_, lr_ = sets[gi % 2]
                th = []

                def f0():
                    load_x(gi, slot)
                    make_h(xg[slot], 0, ntok, 2 if isc else 0, 3 if isc else 1, hb, 0)
                th.append(f0)
                th.extend(kv_thunks(l, ntok, hb, k_, v_))
                th.append(lambda: lr_proj(l, ntok, hb, lr_))

                def fst():
                    P.dma("sp", ksc[gi], k_.ap.rearrange("p a b -> p (a b)"), reads=k_.k(), writes=["ksc%d" % gi])
                    P.dma("sp", vsc[gi], v_.ap.rearrange("p a b -> p (a b)"), reads=v_.k(), writes=["vsc%d" % gi])
                    P.dma("sp", lsc[gi], lr_.ap, reads=lr_.k(), writes=["lsc%d" % gi])
                th.append(fst)
                return th

            for f in prep_thunks(0):
                f()
            for gi in range(len(groups)):
                c0, ntok, isc = groups[gi]
                hb, k_, v_, lr_ = sets[gi % 2]
                fl = prep_thunks(gi + 1) if gi + 1 < len(groups) else []
                if not isc:
                    P.dma("sp", s1save[gi - 1], S1w.ap, reads=S1w.k(), writes=["s1save%d" % (gi - 1)])
                if isc:
                    gla_run(interleave(ntok // 128), {0: (S1w, S1b), 1: (S2w, S2b)}, False, bufs=(k_, v_, lr_), fillers=fl)
                else:
                    gla_run([(0, t_) for t_ in range(ntok // 128)], {0: (S1w, S1b)}, False, bufs=(k_, v_, lr_), fillers=fl)

        def mixer(l, gi, slot):
            c0, ntok, isc = groups[gi]
            nt = ntok // 128
            xb = xg[slot]
            load_x(gi, slot)
            make_h(xb, 0, ntok, 2 if isc else 0, 3 if isc else 1, hT, 0)

            def hu(j, ps, pk):
                P.op("act", lambda e: e.activation(out=uT.ap[:, j, 0:ntok], in_=ps[:, 0:ntok], func=AF.Gelu), reads=pk, writes=uT.k(j))
            proj_fm(w_in, l, 0, 8, hT, ntok, hu)

            wblocks = []
            for cb_ in range(2):
                wblocks.append(wload(wsrc(w_in, l, 0, 8, 1024 + cb_ * 512, 512), 8, 512, slot=cb_))
            gst = {}

            def gate_proj(jj, off, dst):
                if jj % 4 == 0:
                    gst["w"] = wload(wsrc(w_in, l, 0, 8, off + jj * 128, 512), 8, 512, slot=2)
                wa, wka = gst["w"]
                j = jj % 4
                ps, pk = psum()
                for kc in range(8):
                    mm(ps[:, 0:ntok], wa[:, kc, j * 128:(j + 1) * 128], hT.ap[:, kc, 0:ntok], kc == 0, kc == 7, wka + hT.k(kc), pk)
                P.op("act", lambda e: e.activation(out=dst.ap[:, jj, 0:ntok], in_=ps[:, 0:ntok], func=AF.Sigmoid), reads=pk, writes=dst.k(jj))

            def vs_proj(t):
                gb = gvs[t % 2]
                st = sttA[t % 2]
                for cb_ in range(2):
                    wap, wk = wblocks[cb_]
                    ps, pk = psum()
                    for kc in range(8):
                        mm(ps[:, 0:512], hT.ap[:, kc, t * 128:(t + 1) * 128], wap[:, kc, :], kc == 0, kc == 7, wk + hT.k(kc), pk)
                    c0_ = cb_ * 512
                    gk = gb.kb(c0_ * 4, c0_ * 4 + 2048)
                    P.op("act", lambda e, ps=ps, c0_=c0_, cb_=cb_: e.activation(out=gb.ap[:, c0_:c0_ + 512], in_=ps[:, 0:512], func=AF.Gelu, accum_out=st.ap[:, 2 * cb_:2 * cb_ + 1]),
                         reads=pk, writes=gk + st.k())
                    P.op("act", lambda e, c0_=c0_, cb_=cb_: e.activation(out=tmpf[cb_].ap[:, 0:512], in_=gb.ap[:, c0_:c0_ + 512], func=AF.Square, accum_out=st.ap[:, 2 * cb_ + 1:2 * cb_ + 2]),
                         reads=gk, writes=tmpf[cb_].k() + st.k())

            def ln_gate(t):
                gb = gvs[t % 2]
                st = sttA[t % 2]
                P.op("dve", lambda e: e.tensor_tensor(out=st.ap[:, 4:6], in0=st.ap[:, 0:2], in1=st.ap[:, 2:4], op=ALU.add), reads=st.k(), writes=st.k())
                P.op("dve", lambda e: e.tensor_scalar(out=st.ap[:, 4:6], in0=st.ap[:, 4:6], scalar1=1.0 / 1024.0, scalar2=None, op0=ALU.mult), reads=st.k(), writes=st.k())
                P.op("dve", lambda e: e.scalar_tensor_tensor(out=st.ap[:, 6:7], in0=st.ap[:, 4:5], scalar=st.ap[:, 4:5], in1=st.ap[:, 5:6], op0=ALU.mult, op1=ALU.subtract),
                     reads=st.k(), writes=st.k())
                P.op("act", lambda e: e.activation(out=st.ap[:, 7:8], in_=st.ap[:, 6:7], func=AF.Ln, bias=EPS, scale=-1.0), reads=st.k(), writes=st.k())
                P.op("act", lambda e: e.activation(out=st.ap[:, 7:8], in_=st.ap[:, 7:8], func=AF.Exp, scale=-0.5), reads=st.k(), writes=st.k())
                P.op("dve", lambda e: e.tensor_scalar(out=vh.ap[:, t, :], in0=gb.ap, scalar1=st.ap[:, 4:5], scalar2=st.ap[:, 7:8], op0=ALU.subtract, op1=ALU.mult),
                     reads=gb.k() + st.k(), writes=vh.k(t))
                for hh in range(2):
                    ps, pk = psum()
                    for j in range(4):
                        g = hh * 4 + j
                        mm(ps[:, j * 128:(j + 1) * 128], vh.ap[:, t, g * 128:(g + 1) * 128], wsT.ap[:, g, :], True, True, vh.k(t) + wsT.k(), pk)
                    gs = slice(hh * 4, hh * 4 + 4)
                    tv = gb.ap[:, hh * 512:(hh + 1) * 512].rearrange("p (a b) -> p a b", b=128)
                    tk = gb.kb(hh * 2048, hh * 2048 + 2048)
                    P.op("dve", lambda e, ps=ps, gs=gs, tv=tv: e.tensor_tensor(out=tv, in0=ps[:, :].rearrange("p (a b) -> p a b", b=128), in1=lngB.ap[:, gs, :], op=ALU.mult),
                         reads=pk + lngB.k(), writes=tk)
                    P.op("dve", lambda e, gs=gs, tv=tv: e.tensor_tensor(out=tv, in0=tv, in1=T2.ap[:, gs, :], op=ALU.add), reads=tk + T2.k(), writes=tk)
                    P.op("dve", lambda e, gs=gs, tv=tv: e.tensor_tensor(out=aT.ap[:, gs, t * 128:(t + 1) * 128], in0=tv, in1=uT.ap[:, gs, t * 128:(t + 1) * 128], op=ALU.mult),
                         reads=tk + uT.k(hh * 4, 4), writes=aT.k(hh * 4, 4))

            vs_proj(0)
            gq = list(range(8))
            per = (8 + nt - 1) // nt
            for t in range(nt):
                if t + 1 < nt:
                    vs_proj(t + 1)
                for _ in range(per):
                    if gq:
                        gate_proj(gq.pop(0), OFF_GA, mA)
                ln_gate(t)
            wn[0] = 0
            for b0 in range(0, 8, 4):
                wb_, wkb = wload(wsrc(w_bra, l, 0, 8, b0 * 128, 512), 8, 512)
                for j in range(4):
                    jj = b0 + j
                    ps2, pk2 = psum()
                    for kc in range(8):
                        mm(ps2[:, 0:ntok], wb_[:, kc, j * 128:(j + 1) * 128], aT.ap[:, kc, 0:ntok], kc == 0, kc == 7, wkb + aT.k(kc), pk2)
                    P.op("dve", lambda e, ps2=ps2, jj=jj: e.tensor_tensor(out=mA.ap[:, jj, 0:ntok], in0=ps2[:, 0:ntok], in1=mA.ap[:, jj, 0:ntok], op=ALU.mult),
                         reads=pk2 + mA.k(jj), writes=mA.k(jj))
            dump('m_uT', uT); dump('m_vh', vh); dump('m_aT', aT); dump('m_mA', mA); dump('m_T2', T2)
            P.dma("pool", ktm.ap.rearrange("p a b -> p (a b)"), ksc[gi], reads=["ksc%d" % gi], writes=ktm.k())
            P.dma("pool", vtm.ap.rearrange("p a b -> p (a b)"), vsc[gi], reads=["vsc%d" % gi], writes=vtm.k())
            P.dma("pool", lrT.ap, lsc[gi], reads=["lsc%d" % gi], writes=lrT.k())

            def hq(j, ps, pk):
                P.op("act", lambda e: e.activation(out=qT.ap[:, j, 0:ntok], in_=ps[:, 0:ntok], func=AF.Copy), reads=pk, writes=qT.k(j))
            proj_fm(w_in, l, OFF_Q, 4, hT, ntok, hq)

            def hkT(j, ps, pk):
                P.op("dve", lambda e: e.tensor_copy(out=kT.ap[:, j, 0:ntok], in_=ps[:, 0:ntok]), reads=pk, writes=kT.k(j))
            proj_fm(w_in, l, OFF_K, 4, hT, ntok, hkT)
            nch = ntok // 64
            if isc:
                zero_state(S1w, S1b)
            else:
                P.dma("sp", S1w.ap, s1save[gi - 1], reads=["s1save%d" % (gi - 1)], writes=S1w.k())
                P.op("act", lambda e: e.activation(out=S1b.ap, in_=S1w.ap, func=AF.Copy), reads=S1w.k(), writes=S1b.k())
            if isc:
                zero_state(S2w, S2b)
            gla_run(interleave(nt), {0: (S1w, S1b), 1: (S2w, S2b)}, True,
                    first_fn=lambda d, t_: (t_ < nt // 2) if d == 0 else (t_ >= nt // 2))
            dump('m_o1', o1); dump('m_qT', qT); dump('m_S1', S1w); dump('m_S2', S2w)
            wr = [wload(wsrc(w_in, l, 0, 8, OFF_R + cb_ * 512, 512), 8, 512, slot=cb_) for cb_ in range(2)]

            def r_proj(t):
                rs_ = rs[t % 2]
                for cb_ in range(2):
                    wap, wk = wr[cb_]
                    ps, pk = psum()
                    for kc in range(8):
                        mm(ps[:, 0:512], hT.ap[:, kc, t * 128:(t + 1) * 128], wap[:, kc, :], kc == 0, kc == 7, wk + hT.k(kc), pk)
                    P.op("act", lambda e, ps=ps, cb_=cb_: e.activation(out=rs_.ap[:, cb_ * 512:(cb_ + 1) * 512], in_=ps[:, 0:512], func=AF.Silu), reads=pk,
                         writes=rs_.kb(cb_ * 1024, cb_ * 1024 + 1024))

            def ob_norm(t):
                rs_, ob_, st = rs[t % 2], ob[t % 2], sttB[t % 2]
                for h in range(4):
                    P.op("act", lambda e, h=h: e.activation(out=tmpf[h % 2].ap[:, 0:256], in_=o1.ap[:, t, h * 256:(h + 1) * 256], func=AF.Square, accum_out=st.ap[:, h:h + 1]),
                         reads=o1.k(t), writes=tmpf[h % 2].k() + st.k())
                P.op("act", lambda e: e.activation(out=st.ap[:, 4:8], in_=st.ap[:, 0:4], func=AF.Ln, bias=EPS, scale=1.0 / 256.0), reads=st.k(), writes=st.k())
                P.op("act", lambda e: e.activation(out=st.ap[:, 4:8], in_=st.ap[:, 4:8], func=AF.Exp, scale=-0.5), reads=st.k(), writes=st.k())
                for h in range(4):
                    P.op("dve", lambda e, h=h: e.scalar_tensor_tensor(out=ob_.ap[:, h * 256:(h + 1) * 256], in0=o1.ap[:, t, h * 256:(h + 1) * 256], scalar=st.ap[:, 4 + h:5 + h],
                         in1=rs_.ap[:, h * 256:(h + 1) * 256], op0=ALU.mult, op1=ALU.mult), reads=o1.k(t) + st.k() + rs_.k(), writes=ob_.k())

            def ob_transpose(t):
                ob_ = ob[t % 2]
                for j in range(8):
                    P.op("pe", lambda e, j=j: e.transpose(pst[:, j * 128:(j + 1) * 128], ob_.ap[:, j * 128:(j + 1) * 128], ident_b.ap), reads=ob_.k() + ident_b.k(), writes=["pst"])
                for j in range(8):
                    P.op("act", lambda e, j=j: e.activation(out=obT.ap[:, j, t * 128:(t + 1) * 128], in_=pst[:, j * 128:(j + 1) * 128], func=AF.Copy, scale=V(l, "gng")[:, j:j + 1]),
                         reads=["pst"] + KV, writes=obT.k(j))

            r_proj(0)
            gq = list(range(8))
            for t in range(nt):
                if t + 1 < nt:
                    r_proj(t + 1)
                ob_norm(t)
                for _ in range(per):
                    if gq:
                        gate_proj(gq.pop(0), OFF_GB, mg)
                ob_transpose(t)
            wn[0] = 0
            for b0 in range(0, 8, 4):
                wb_, wkb = wload(wsrc(w_brb, l, 0, 8, b0 * 128, 512), 8, 512)
                for j in range(4):
                    jj = b0 + j
                    ps2, pk2 = psum()
                    for kc in range(8):
                        mm(ps2[:, 0:ntok], wb_[:, kc, j * 128:(j + 1) * 128], obT.ap[:, kc, 0:ntok], kc == 0, kc == 7, wkb + obT.k(kc), pk2)
                    tf = tmpf[jj % 2]
                    P.op("dve", lambda e, ps2=ps2, tf=tf, jj=jj: e.tensor_tensor(out=tf.ap[:, 0:ntok], in0=ps2[:, 0:ntok], in1=mg.ap[:, jj, 0:ntok], op=ALU.mult),
                         reads=pk2 + mg.k(jj), writes=tf.k())
                    P.op("dve", lambda e, tf=tf, jj=jj: e.tensor_tensor(out=mg.ap[:, jj, 0:ntok], in0=tf.ap[:, 0:ntok], in1=mA.ap[:, jj, 0:ntok], op=ALU.add),
                         reads=tf.k() + mA.k(jj), writes=mg.k(jj))
            dump('m_obT', obT); dump('m_mg', mg)
            gcol = 9 if isc else 8

            def ho(j, ps, pk):
                P.op("dve", lambda e: e.scalar_tensor_tensor(out=xb.ap[:, j, 0:ntok], in0=ps[:, 0:ntok], scalar=lv.ap[:, gcol, j:j + 1], in1=xb.ap[:, j, 0:ntok],
                     op0=ALU.mult, op1=ALU.add), reads=pk + lv.k() + xb.k(j), writes=xb.k(j))
            proj_fm(w_o, l, 0, 8, mg, ntok, ho)
            dump('m_xmid', xb)

        def diag_build(l, cc, ntaps):
            d_ = dg[cc % 2]
            for j in ntaps:
                P.op("dve", lambda e, j=j: e.tensor_scalar(out=d_.ap[:, j, :], in0=ident_b.ap, scalar1=V(l, "cw")[:, cc * 9 + j:cc * 9 + j + 1], scalar2=None, op0=ALU.mult),
                     reads=ident_b.k() + KV, writes=d_.k())
            return d_

        def ffn(l, gi, slot, last):
            c0, ntok, isc = groups[gi]
            xb = xg[slot]
            lower = (not isc) and gi > 1
            nh = ntok + (64 if lower else 0)
            make_h(xb, 0, ntok, 6 if isc else 4, 7 if isc else 5, hF, 0)
            if lower:
                ob_ = xg[1 - slot]
                make_h(ob_, G - 64, 64, 4, 5, hF, G)
            wst = {}
            if isc:
                for i_ in range(2):
                    P.op("dve", lambda e, i_=i_: e.memset(apc[i_].ap, 0.0), writes=apc[i_].k())
            else:
                for i_ in range(2):
                    P.op("dve", lambda e, i_=i_: e.memset(apad[i_].ap, 0.0), writes=apad[i_].k())
            taps = [3, 4, 5] if isc else list(range(9))

            def a_proj(cc):
                ap_ = apad[cc % 2]
                if cc % 4 == 0:
                    ncb = min(4, NCC - cc)
                    wst["a"] = wload(wsrc(w_up, l, 0, 8, cc * 128, ncb * 128), 8, ncb * 128)
                wap4, wk = wst["a"]
                wap = wap4[:, :, (cc % 4) * 128:(cc % 4 + 1) * 128]
                diag_build(l, cc, taps)
                ps, pk = psum()
                for kc in range(8):
                    mm(ps[:, 0:ntok], wap[:, kc, :], hF.ap[:, kc, 0:ntok], kc == 0, kc == 7, wk + hF.k(kc), pk)
                if isc:
                    ac = apc[cc % 2]
                    P.op("act", lambda e: e.activation(out=ac.ap[:, 1:257], in_=ps[:, 0:256], func=AF.Copy), reads=pk, writes=ac.k())
                    return
                if lower:
                    psh, pkh = psum()
                    for kc in range(8):
                        mm(psh[:, 0:64], wap[:, kc, :], hF.ap[:, kc, G:G + 64], kc == 0, kc == 7, wk + hF.k(kc), pkh)
                P.op("act", lambda e: e.activation(out=ap_.ap[:, 1:9, 1:65], in_=ps[:, 0:512].rearrange("p (r c) -> p r c", c=64), func=AF.Copy), reads=pk, writes=ap_.k())
                P.op("dve", lambda e: e.tensor_copy(out=ap_.ap[:, 9, 1:65], in_=a_save.ap[:, cc, :]), reads=a_save.k(cc) + ap_.k(), writes=ap_.k())
                P.op("dve", lambda e: e.tensor_copy(out=a_save.ap[:, cc, :], in_=ap_.ap[:, 1, 1:65]), reads=ap_.k() + a_save.k(cc), writes=a_save.k(cc))
                if lower:
                    P.op("act", lambda e: e.activation(out=ap_.ap[:, 0, 1:65], in_=psh[:, 0:64], func=AF.Copy), reads=pkh + ap_.k(), writes=ap_.k())

            def val_proj(cc):
                if cc % 4 == 0:
                    ncb = min(4, NCC - cc)
                    wst["v"] = wload(wsrc(w_up, l, 0, 8, DFF + cc * 128, ncb * 128), 8, ncb * 128)
                wvp4, wvk = wst["v"]
                wvp = wvp4[:, :, (cc % 4) * 128:(cc % 4 + 1) * 128]
                psv, pkv = psum()
                for kc in range(8):
                    mm(psv[:, 0:ntok], wvp[:, kc, :], hF.ap[:, kc, 0:ntok], kc == 0, kc == 7, wvk + hF.k(kc), pkv)
                return psv, pkv

            def conv(cc, psv, pkv):
                d_ = dg[cc % 2]
                psc, pkc = psum()
                if isc:
                    ac = apc[cc % 2]
                    for i, j in enumerate(taps):
                        mm(psc[:, 0:256], d_.ap[:, j, :], ac.ap[:, j - 3:j - 3 + 256], i == 0, i == 2, d_.k() + ac.k(), pkc)
                else:
                    ap_ = apad[cc % 2]
                    for j in range(9):
                        dr, dc_ = j // 3, j % 3
                        mm(psc[:, 0:512].rearrange("p (r c) -> p r c", c=64), d_.ap[:, j, :], ap_.ap[:, dr:dr + 8, dc_:dc_ + 64], j == 0, j == 8, d_.k() + ap_.k(), pkc)
                ga_ = gact[cc % 2]
                P.op("act", lambda e: e.activation(out=ga_.ap[:, 0:ntok], in_=psc[:, 0:ntok], func=AF.Gelu, bias=V(l, "cb")[:, cc:cc + 1], scale=1.0),
                     reads=pkc + KV, writes=ga_.k())
                P.op("dve", lambda e: e.tensor_tensor(out=gv.ap[:, cc, 0:ntok], in0=psv[:, 0:ntok], in1=ga_.ap[:, 0:ntok], op=ALU.mult),
                     reads=pkv + ga_.k(), writes=gv.k(cc))

            a_proj(0)
            for cc in range(NCC):
                if cc + 1 < NCC:
                    a_proj(cc + 1)
                psv, pkv = val_proj(cc)
                conv(cc, psv, pkv)
            dump('f_gv', gv); dump('f_hF', hF)
            gcol = 11 if isc else 10
            for jb in range(2):
                banks = [psum() for _ in range(4)]
                for (k0, nk) in ((0, 8), (8, 8), (16, 6)):
                    wap, wk = wload(wsrc(w_dn, l, k0 * 128, nk, jb * 512, 512), nk, 512)
                    for j4 in range(4):
                        ps, pk = banks[j4]
                        for kk in range(nk):
                            cc = k0 + kk
                            mm(ps[:, 0:ntok], wap[:, kk, j4 * 128:(j4 + 1) * 128], gv.ap[:, cc, 0:ntok], cc == 0, cc == NCC - 1, wk + gv.k(cc), pk)
                for j4 in range(4):
                    ps, pk = banks[j4]
                    j = jb * 4 + j4
                    P.op("dve", lambda e, ps=ps, j=j: e.scalar_tensor_tensor(out=xb.ap[:, j, 0:ntok], in0=ps[:, 0:ntok], scalar=lv.ap[:, gcol, j:j + 1], in1=xb.ap[:, j, 0:ntok],
                         op0=ALU.mult, op1=ALU.add), reads=pk + lv.k() + xb.k(j), writes=xb.k(j))
            dump('f_x', xb)
            if not last:
                store_x(gi, slot)
            elif not isc:
                for kc in range(8):
                    P.op("act", lambda e, kc=kc: e.activation(out=sq.ap[:, kc, 0:ntok], in_=xb.ap[:, kc, 0:ntok], func=AF.Square), reads=xb.k(kc), writes=sq.k(kc))
                ps, pk = psum()
                for kc in range(8):
                    mm(ps[:, 0:ntok], ones_b.ap, sq.ap[:, kc, 0:ntok], kc == 0, kc == 7, ones_b.k() + sq.k(kc), pk)
                P.op("act", lambda e: e.activation(out=rstd.ap[:, 0:ntok], in_=ps[:, 0:ntok], func=AF.Ln, bias=EPS, scale=1.0), reads=pk, writes=rstd.k())
                P.op("act", lambda e: e.activation(out=rstd.ap[:, 0:ntok], in_=rstd.ap[:, 0:ntok], func=AF.Exp, scale=-0.5), reads=rstd.k(), writes=rstd.k())
                fg = vecs.ap[:, VG["fng"]:VG["fng"] + 8]
                for kc in range(8):
                    P.op("dve", lambda e, kc=kc: e.scalar_tensor_tensor(out=xb.ap[:, kc, 0:ntok], in0=xb.ap[:, kc, 0:ntok], scalar=fg[:, kc:kc + 1], in1=rstd.ap[:, 0:ntok],
                         op0=ALU.mult, op1=ALU.mult), reads=xb.k(kc) + KV + rstd.k(), writes=xb.k(kc))
                for t in range(ntok // 128):
                    for hh in range(2):
                        ps, pk = psum()
                        for j in range(4):
                            kc = hh * 4 + j
                            P.op("pe", lambda e, ps=ps, j=j, kc=kc, t=t: e.transpose(ps[:, j * 128:(j + 1) * 128], xb.ap[:, kc, t * 128:(t + 1) * 128], ident_f.ap),
                                 reads=xb.k(kc) + ident_f.k(), writes=pk)
                        fin = fin2[t % 2]
                        P.op("act", lambda e, ps=ps, hh=hh, fin=fin: e.activation(out=fin.ap[:, hh * 512:(hh + 1) * 512], in_=ps[:, 0:512], func=AF.Copy), reads=pk,
                             writes=fin.kb(hh * 2048, hh * 2048 + 2048))
                    r0 = c0 - CTX + t * 128
                    P.dma("sp", out[r0:r0 + 128, :], fin2[t % 2].ap, reads=fin2[t % 2].k(), writes=["out%d" % r0])

        ALLW = ["ada", "in", "bra", "brb", "o", "up", "dn"]
        cast_weights(0, ALLW)
        adaln(0)
        for l in range(n_layers):
            last = l == n_layers - 1
            layer_consts(l)
            if not last:
                cast_weights(l + 1, ALLW)
            pass1(l)
            P.op("act", lambda e: e.activation(out=S2b.ap, in_=S2w.ap, func=AF.Copy), reads=S2w.k(), writes=S2b.k())
            P.op("dve", lambda e: e.memset(a_save.ap, 0.0), writes=a_save.k())
            order = list(range(NLG, 0, -1))
            slot_of = {}
            for i, gi in enumerate(order):
                slot = i % 2
                slot_of[gi] = slot
                mixer(l, gi, slot)
                if i >= 1:
                    ffn(l, order[i - 1], slot_of[order[i - 1]], last)
            ffn(l, order[-1], slot_of[order[-1]], last)
            if not last:
                mixer(l, 0, 0)
                ffn(l, 0, 0, last)
                adaln(l + 1)
        P.final_wait_all("sp")
        P.emit()
    return nc


def _pack_vecs(inp, b):
    v = np.zeros((128, NV), np.float32)

    def put(col, arr):
        a = np.asarray(arr, np.float32).reshape(-1, 128).T
        v[:, col:col + a.shape[1]] = a

    for l in range(DEPTH):
        base = l * VPL
        put(base + VL["n1g"][0], inp["norm1_g"][l])
        put(base + VL["n2g"][0], inp["norm2_g"][l])
        put(base + VL["lng"][0], inp["sgu_ln_g"][l])
        put(base + VL["lnb"][0], inp["sgu_ln_b"][l])
        put(base + VL["gng"][0], inp["gla_norm_g"][l])
        put(base + VL["cb"][0], inp["ffn_conv_b"][l])
        cw = np.asarray(inp["ffn_conv_w"][l], np.float32).reshape(9, NCC, 128)
        v[:, base + VL["cw"][0]: base + VL["cw"][0] + NCC * 9] = cw.transpose(2, 1, 0).reshape(128, NCC * 9)
        put(base + VL["bada"][0], inp["b_ada"][l])
    put(VG["fng"], inp["final_norm_g"])
    put(VG["c"], inp["c"][b])
    put(VG["cctx"], inp["c_ctx"])
    return v


_NC_CACHE = {}


def kernel(**inp):
    inp = {k: np.asarray(v) for k, v in inp.items()}
    if "nc" not in _NC_CACHE:
        _NC_CACHE["nc"] = build_program()
    nc = _NC_CACHE["nc"]
    f32 = lambda a: np.ascontiguousarray(a, dtype=np.float32)
    wsT = f32(np.transpose(inp["sgu_w"], (0, 3, 1, 2)).reshape(DEPTH, 128, 1024))
    sgub = f32(inp["sgu_b"].reshape(DEPTH, 1, 1024))
    w2a = f32(np.concatenate([inp["gla_w2"], inp["gla_b2"][:, :, None, :]], axis=2))
    shared = {
        "w_ada": f32(inp["w_ada"]), "w_in": f32(inp["w_in"]), "wsT": wsT, "sgub": sgub, "w2a": w2a,
        "w_br_a": f32(inp["w_br_a"]), "w_br_b": f32(inp["w_br_b"]), "w_o": f32(inp["w_o"]),
        "ffn_w_up": f32(inp["ffn_w_up"]), "ffn_w_down": f32(inp["ffn_w_down"]),
    }
    in_maps = []
    for r in range(8):
        b = r % 4
        m = dict(shared)
        m["x_in"] = f32(inp["x"][b])
        m["ctx_in"] = f32(inp["ctx"][b])
        m["vecs"] = _pack_vecs(inp, b)
        in_maps.append(m)
    res = run_bass_kernel_spmd(nc, in_maps, core_ids=list(range(8)))
    outs = [np.asarray(res.results[b]["out"], np.float32) for b in range(4)]
    return np.stack(outs, axis=0)
```

```python
import contextlib
import types
import numpy as np
import concourse.bass as bass
import concourse.mybir as mybir
from concourse.bass_utils import run_bass_kernel_spmd

F32 = mybir.dt.float32
BF16 = mybir.dt.bfloat16
AF = mybir.ActivationFunctionType
ALU = mybir.AluOpType

D = 1024
SEQ = 4096
CTX = 256
DEPTH = 4
DIN = 7200
DFF = 2816
NCC = DFF // 128
EPS = 1e-6
OFF_Q = 2048
OFF_R = 2560
OFF_GA = 3584
OFF_GB = 4608
OFF_K = 5632
OFF_V = 6144
OFF_LR = 7168
G = 512
T = CTX + SEQ
NLG = SEQ // G

VL = {}
_o = 0
for _n, _w in (("n1g", 8), ("n2g", 8), ("lng", 8), ("lnb", 8), ("gng", 8), ("cb", NCC), ("cw", NCC * 9), ("bada", 48)):
    VL[_n] = (_o, _w)
    _o += _w
VPL = _o
VG = {"fng": DEPTH * VPL, "c": DEPTH * VPL + 8, "cctx": DEPTH * VPL + 16}
NV = DEPTH * VPL + 24


def _freeze(fn):
    if fn.__closure__ is None:
        return fn
    cells = []
    for c in fn.__closure__:
        try:
            cells.append(types.CellType(c.cell_contents))
        except ValueError:
            cells.append(c)
    return types.FunctionType(fn.__code__, fn.__globals__, fn.__name__, fn.__defaults__, tuple(cells))


class Prog:
    ENGS = ("pe", "act", "dve", "pool", "sp")

    def __init__(self, nc, es, ndma=8):
        self.nc = nc
        self.ins = {e: [] for e in self.ENGS}
        self.cnt = {e: 0 for e in self.ENGS}
        self.waited = {e: {} for e in self.ENGS}
        self.state = {}
        self.sems = {}
        for e in ("pe", "act", "dve", "pool"):
            self.sems[e] = es.enter_context(nc.semaphore("c_" + e))
        self.ndma = ndma
        self.dma_n = {}
        self.dma_cnt = {}
        for q in ("sp", "act", "pool"):
            self.dma_n[q] = 0
            for i in range(ndma):
                k = ("dma", q, i)
                self.sems[k] = es.enter_context(nc.semaphore("d_%s_%d" % (q, i)))
                self.dma_cnt[k] = 0

    def _st(self, key):
        s = self.state.get(key)
        if s is None:
            s = {"w": None, "r": {}}
            self.state[key] = s
        return s

    def _deps(self, eng, reads, writes):
        deps = {}

        def add(d):
            if d is None:
                return
            k, v = d
            if k == eng and eng == "pe":
                return
            if deps.get(k, 0) < v:
                deps[k] = v

        for r in reads:
            add(self._st(r)["w"])
        for w in writes:
            s = self._st(w)
            add(s["w"])
            for d in s["r"].values():
                add(d)
        out = []
        wd = self.waited[eng]
        for k, v in deps.items():
            if wd.get(k, 0) >= v:
                continue
            wd[k] = v
            out.append((k, v))
        return out

    def _commit(self, mydep, tag, reads, writes):
        ws = set(writes)
        for w in ws:
            s = self._st(w)
            s["w"] = mydep
            s["r"] = {}
        for r in reads:
            if r in ws:
                continue
            self._st(r)["r"][tag] = mydep

    def op(self, eng, fn, reads=(), writes=()):
        waits = self._deps(eng, reads, writes)
        self.cnt[eng] += 1
        mydep = (eng, self.cnt[eng])
        self.ins[eng].append((waits, _freeze(fn), (eng, 1)))
        self._commit(mydep, eng, reads, writes)

    def dma(self, q, out, in_, reads=(), writes=()):
        i = self.dma_n[q] % self.ndma
        self.dma_n[q] += 1
        k = ("dma", q, i)
        waits = self._deps(q, reads, writes)
        wd = self.waited[q]
        prev = self.dma_cnt[k] * 16
        if prev > 0 and wd.get(k, 0) < prev:
            wd[k] = prev
            waits.append((k, prev))
        self.dma_cnt[k] += 1
        mydep = (k, self.dma_cnt[k] * 16)
        self.ins[q].append((waits, lambda e: e.dma_start(out=out, in_=in_), (k, 16)))
        self._commit(mydep, k, reads, writes)

    def final_wait_all(self, eng="sp"):
        waits = []
        for e in ("pe", "act", "dve", "pool"):
            if self.cnt[e] > 0:
                waits.append((e, self.cnt[e]))
        for k, c in self.dma_cnt.items():
            if c > 0:
                waits.append((k, c * 16))
        self.ins[eng].append((waits, None, None))

    def emit(self):
        nc = self.nc
        with nc.Block() as block:
            def run(e, name):
                for waits, fn, inc in self.ins[name]:
                    for k, v in waits:
                        e.wait_ge(self.sems[k], v)
                    if fn is not None:
                        r = fn(e)
                        r.then_inc(self.sems[inc[0]], inc[1])

            @block.tensor
            def _(e):
                run(e, "pe")

            @block.scalar
            def _(e):
                run(e, "act")

            @block.vector
            def _(e):
                run(e, "dve")

            @block.gpsimd
            def _(e):
                run(e, "pool")

            @block.sync
            def _(e):
                run(e, "sp")


class Buf:
    GR = 512

    def __init__(self, ap, off, nbytes, sub=None):
        self.ap = ap
        self.off = off
        self.nbytes = nbytes
        self.sub = sub

    def k(self, i=None, n=1):
        if i is None:
            lo, hi = self.off, self.off + self.nbytes
        else:
            lo = self.off + i * self.sub
            hi = lo + n * self.sub
        return ["g%d" % j for j in range(lo // self.GR, (hi - 1) // self.GR + 1)]

    def kb(self, lo, hi):
        lo += self.off
        hi += self.off
        return ["g%d" % j for j in range(lo // self.GR, (hi - 1) // self.GR + 1)]


def build_program(n_layers=DEPTH, dbg=False):
    nc = bass.Bass("TRN2", target_bir_lowering=False)
    dt_in = lambda name, shape: nc.dram_tensor(name, shape, F32, kind="ExternalInput").ap()
    x_in = dt_in("x_in", [SEQ, D])
    ctx_in = dt_in("ctx_in", [CTX, D])
    vecs_in = dt_in("vecs", [128, NV])
    w_ada = dt_in("w_ada", [DEPTH, D, 6 * D])
    w_in = dt_in("w_in", [DEPTH, D, DIN])
    wsT_in = dt_in("wsT", [DEPTH, 128, 8 * 128])
    sgub_in = dt_in("sgub", [DEPTH, 1, 8 * 128])
    w2a_in = dt_in("w2a", [DEPTH, 2, 17, 512])
    w_bra = dt_in("w_br_a", [DEPTH, D, D])
    w_brb = dt_in("w_br_b", [DEPTH, D, D])
    w_o = dt_in("w_o", [DEPTH, D, D])
    w_up = dt_in("ffn_w_up", [DEPTH, D, 2 * DFF])
    w_dn = dt_in("ffn_w_down", [DEPTH, DFF, D])
    out = nc.dram_tensor("out", [SEQ, D], F32, kind="ExternalOutput").ap()
    xT = nc.dram_tensor("xT_scr", [D, T], F32).ap()
    s1save = nc.dram_tensor("s1save", [NLG, 128, 1024], F32).ap()
    xTv = xT.rearrange("(kc p) t -> p kc t", p=128)
    NGR = 1 + NLG
    ksc = nc.dram_tensor("ksc", [NGR, 128, (G // 128) * 512], BF16).ap()
    vsc = nc.dram_tensor("vsc", [NGR, 128, (G // 128) * 1024], BF16).ap()
    lsc = nc.dram_tensor("lsc", [NGR, 64, G], BF16).ap()
    WSRC = {"ada": w_ada, "in": w_in, "bra": w_bra, "brb": w_brb, "o": w_o, "up": w_up, "dn": w_dn}
    WB = {n: nc.dram_tensor("wb_" + n, list(a.shape), BF16).ap() for n, a in WSRC.items()}

    with contextlib.ExitStack() as es:
        P = Prog(nc, es)
        arena = es.enter_context(nc.sbuf_tensor("arena", [128, 200 * 256], F32))
        cur = [0]

        def alloc(shape, dt, at=None):
            esz = 4 if dt == F32 else 2
            n = 1
            for s in shape[1:]:
                n *= s
            nbytes = n * esz
            if at is None:
                off = cur[0]
                cur[0] = (off + nbytes + Buf.GR - 1) // Buf.GR * Buf.GR
            else:
                off = at
            assert off + nbytes <= 200 * 1024, (off, nbytes)
            ap = arena[0:shape[0], off // 4:(off + nbytes) // 4]
            if dt == BF16:
                ap = ap.bitcast(BF16)
            if len(shape) == 3:
                ap = ap.rearrange("p (a b) -> p a b", b=shape[2])
            elif len(shape) == 4:
                ap = ap.rearrange("p (a b c) -> p a b c", b=shape[2], c=shape[3])
            sub = nbytes // shape[1] if len(shape) >= 3 else None
            return Buf(ap, off, nbytes, sub)

        psb = [es.enter_context(nc.psum_tensor("ps%d" % i, [128, 512], F32)) for i in range(7)]
        pst = es.enter_context(nc.psum_tensor("pst", [128, 1024], BF16))
        psn = [0]

        def psum():
            i = psn[0] % 7
            psn[0] += 1
            return psb[i], ["ps%d" % i]

        dumped = set()

        def dump(name, buf, once=True):
            if not dbg or (once and name in dumped):
                return
            dumped.add(name)
            ap = buf.ap
            shp = list(ap.shape)
            n = 1
            for s_ in shp[1:]:
                n *= s_
            dten = nc.dram_tensor("dbg_" + name, [shp[0], n], F32, kind="ExternalOutput").ap()
            src = ap
            if len(shp) == 3:
                src = ap.rearrange("p a b -> p (a b)")
            elif len(shp) == 4:
                src = ap.rearrange("p a b c -> p (a b c)")
            P.dma("pool", dten, src, reads=buf.k(), writes=["dbg_" + name])

        vecs = alloc([128, NV], F32)
        mods = alloc([128, DEPTH * 2, 48], F32)
        lv = alloc([128, 12, 8], F32)
        ident_f = alloc([128, 128], F32)
        ident_b = alloc([128, 128], BF16)
        ones_b = alloc([128, 128], BF16)
        tcm2 = alloc([128, 2, 128], BF16)
        tbx2 = alloc([128, 2, 128], BF16)
        mk2 = alloc([128, 2, 4, 128], BF16)
        wsT = alloc([128, 8, 128], BF16)
        bsrow = alloc([1, 1024], BF16)
        w2a = alloc([64, 512], BF16)
        T2 = alloc([128, 8, 128], F32)
        lngB = alloc([128, 8, 128], BF16)
        scb = alloc([128, 8, 2], BF16)
        sttA = [alloc([128, 8], F32) for _ in range(2)]
        sttB = [alloc([128, 8], F32) for _ in range(2)]
        S1w = alloc([128, 1024], F32)
        S1b = alloc([128, 1024], BF16)
        S2w = alloc([128, 1024], F32)
        S2b = alloc([128, 1024], BF16)
        a_save = alloc([128, NCC, 64], BF16)
        lrT = alloc([64, G], BF16)
        wbufs = [alloc([128, 8 * 512], BF16) for _ in range(3)]
        xg = [alloc([128, 8, G], F32) for _ in range(2)]
        rstd = alloc([128, G], F32)
        tmpf = [alloc([128, G], F32) for _ in range(2)]
        hT = alloc([128, 8, G], BF16)
        hF = alloc([128, 8, G + 64], BF16)
        mA = alloc([128, 8, G], BF16)
        obT = alloc([128, 8, G], BF16)
        sq = alloc([128, 8, G], BF16, at=obT.off)
        spx = alloc([128, 512], F32)
        spb = [alloc([128, 512], BF16) for _ in range(2)]
        E1 = spx
        kdec = [alloc([128, 512], BF16) for _ in range(3)]
        EqT = alloc([128, 4, 128], F32)
        EkT = alloc([128, 4, 128], F32)
        decb = [alloc([128, 4, 2], F32) for _ in range(3)]
        qdT = [alloc([128, 4, 128], BF16) for _ in range(3)]
        kiT = [alloc([128, 4, 128], BF16) for _ in range(2)]
        scT = [alloc([128, 4, 128], BF16) for _ in range(2)]
        base = cur[0]
        NT = G // 128
        tcm = alloc([128, 2, 64], F32, at=base)
        tbx = alloc([128, 2, 65], F32, at=base + 1024)
        mk = alloc([128, 2, 4, 64], F32, at=base + 2048)
        uT = alloc([128, 8, G], BF16, at=base)
        vh = alloc([128, NT, 1024], BF16, at=base + 8192)
        gvs = [alloc([128, 1024], F32, at=base + 16384 + 4096 * i) for i in range(2)]
        aT = alloc([128, 8, G], BF16, at=base + 24576)
        qT = alloc([128, 4, G], BF16, at=base)
        kT = alloc([128, 4, G], BF16, at=base + 4096)
        ktm = alloc([128, NT, 512], BF16, at=base + 8192)
        vtm = alloc([128, NT, 1024], BF16, at=base + 12288)
        o1 = alloc([128, NT, 1024], F32, at=base + 20480)
        rs = [alloc([128, 1024], BF16, at=base + 36864 + 2048 * i) for i in range(2)]
        ob = [alloc([128, 1024], BF16, at=base + 40960 + 2048 * i) for i in range(2)]
        mg = alloc([128, 8, G], BF16, at=base)
        ktm0, vtm0, lrT0 = ktm, vtm, lrT
        ktm2 = alloc([128, NT, 512], BF16, at=base + 20480)
        vtm2 = alloc([128, NT, 1024], BF16, at=base + 24576)
        lrT2 = alloc([64, G], BF16, at=base + 32768)
        gv = alloc([128, NCC, G], BF16, at=base)
        apad = [alloc([128, 10, 66], BF16, at=base + 22528 + 1536 * i) for i in range(2)]
        dg = [alloc([128, 9, 128], BF16, at=base + 25600 + 2560 * i) for i in range(2)]
        gact = [alloc([128, G], BF16, at=base + 30720 + 1024 * i) for i in range(2)]
        apc = [alloc([128, 256 + 4], BF16, at=base + 32768 + 512 * i) for i in range(2)]
        fin2 = [alloc([128, 1024], F32, at=base + 33792 + 4096 * i) for i in range(2)]
        cur[0] = base + 45056
        assert cur[0] <= 200 * 1024, cur[0]

        V = lambda l, name: vecs.ap[:, l * VPL + VL[name][0]: l * VPL + VL[name][0] + VL[name][1]]
        KV = vecs.k()

        P.dma("sp", vecs.ap, vecs_in, writes=KV)
        P.op("pool", lambda e: e.memset(ident_f.ap, 0.0), writes=ident_f.k())
        P.op("pool", lambda e: e.affine_select(out=ident_f.ap, in_=ident_f.ap, pattern=[[-1, 128]], compare_op=ALU.not_equal,
                                               fill=1.0, base=0, channel_multiplier=1), reads=ident_f.k(), writes=ident_f.k())
        P.op("dve", lambda e: e.tensor_copy(out=ident_b.ap, in_=ident_f.ap), reads=ident_f.k(), writes=ident_b.k())
        P.op("dve", lambda e: e.memset(ones_b.ap, 1.0 / 1024.0), writes=ones_b.k())
        P.op("dve", lambda e: e.memset(lrT.ap, 1.0), writes=lrT.k())
        P.op("dve", lambda e: e.memset(a_save.ap, 0.0), writes=a_save.k())
        NEG = -1.0 / 16.0
        hs = slice(0, 64)
        for d in range(2):
            P.op("pool", lambda e, d=d: e.memset(tcm.ap[hs, d, :], NEG), writes=tcm.k())
            P.op("pool", lambda e, d=d: e.memset(tbx.ap[hs, d, :], NEG), writes=tbx.k())
            P.op("pool", lambda e, d=d: e.memset(mk.ap[hs, d, :, :], 1.0), writes=mk.k())
        P.op("pool", lambda e: e.affine_select(out=tcm.ap[hs, 0, :], in_=tcm.ap[hs, 0, :], pattern=[[-1, 64]],
             compare_op=ALU.is_gt, fill=0.0, base=0, channel_multiplier=1), reads=tcm.k(), writes=tcm.k())
        P.op("pool", lambda e: e.affine_select(out=tcm.ap[hs, 1, :], in_=tcm.ap[hs, 1, :], pattern=[[1, 64]],
             compare_op=ALU.is_gt, fill=0.0, base=0, channel_multiplier=-1), reads=tcm.k(), writes=tcm.k())
        P.op("pool", lambda e: e.affine_select(out=tbx.ap[hs, 0, 0:64], in_=tbx.ap[hs, 0, 0:64], pattern=[[1, 64]],
             compare_op=ALU.is_ge, fill=0.0, base=0, channel_multiplier=-1), reads=tbx.k(), writes=tbx.k())
        P.op("pool", lambda e: e.affine_select(out=tbx.ap[hs, 1, 0:64], in_=tbx.ap[hs, 1, 0:64], pattern=[[-1, 64]],
             compare_op=ALU.is_ge, fill=0.0, base=0, channel_multiplier=1), reads=tbx.k(), writes=tbx.k())
        for h in range(4):
            P.op("pool", lambda e, h=h: e.affine_select(out=mk.ap[hs, 0, h, :], in_=mk.ap[hs, 0, h, :], pattern=[[1, 64]],
                 compare_op=ALU.is_ge, fill=0.0, base=0, channel_multiplier=-1), reads=mk.k(), writes=mk.k())
            P.op("pool", lambda e, h=h: e.affine_select(out=mk.ap[hs, 1, h, :], in_=mk.ap[hs, 1, h, :], pattern=[[-1, 64]],
                 compare_op=ALU.is_ge, fill=0.0, base=0, channel_multiplier=1), reads=mk.k(), writes=mk.k())
        for b_ in (tcm, tbx, mk):
            P.dma("sp", b_.ap[64:128], b_.ap[0:64], reads=b_.k(), writes=b_.k())
        for b_ in (tcm2, tbx2, mk2):
            P.op("dve", lambda e, b_=b_: e.memset(b_.ap, 0.0), writes=b_.k())
        for j_ in range(2):
            hs_ = slice(j_ * 64, j_ * 64 + 64)
            P.op("dve", lambda e, hs_=hs_, j_=j_: e.tensor_copy(out=tcm2.ap[hs_, :, j_ * 64:(j_ + 1) * 64], in_=tcm.ap[hs_, :, :]), reads=tcm.k() + tcm2.k(), writes=tcm2.k())
            P.op("dve", lambda e, hs_=hs_, j_=j_: e.tensor_copy(out=tbx2.ap[hs_, :, j_ * 64:(j_ + 1) * 64], in_=tbx.ap[hs_, :, 0:64]), reads=tbx.k() + tbx2.k(), writes=tbx2.k())
            P.op("dve", lambda e, hs_=hs_, j_=j_: e.tensor_copy(out=mk2.ap[hs_, :, :, j_ * 64:(j_ + 1) * 64], in_=mk.ap[hs_, :, :, :]), reads=mk.k() + mk2.k(), writes=mk2.k())

        wn = [0]

        def cast_weights(l, names):
            for n in names:
                src, dst = WSRC[n], WB[n]
                for rc in range(src.shape[1] // 128):
                    P.dma("pool", dst[l, rc * 128:(rc + 1) * 128, :], src[l, rc * 128:(rc + 1) * 128, :], writes=["wb_%s_%d_%d" % (n, l, rc)])

        def wload(src, nkc, ncols, slot=None):
            n, l, r0, nkc_, c0, ncl = src
            if slot is None:
                i = wn[0] % 3
                wn[0] += 1
            else:
                i = slot
            b = wbufs[i]
            ap = b.ap[:, 0:nkc * ncols].rearrange("p (a b) -> p a b", b=ncols)
            sap = WB[n][l, r0:r0 + nkc * 128, c0:c0 + ncols].rearrange("(kc p) n -> p kc n", p=128)
            rk = ["wb_%s_%d_%d" % (n, l, r0 // 128 + k) for k in range(nkc)]
            P.dma("sp", ap, sap, reads=rk, writes=b.k())
            return ap, b.k()

        WNAME = {id(w_ada): "ada", id(w_in): "in", id(w_bra): "bra", id(w_brb): "brb", id(w_o): "o", id(w_up): "up", id(w_dn): "dn"}

        def wsrc(w, l, r0, nkc, c0, ncols):
            return (WNAME[id(w)], l, r0, nkc, c0, ncols)

        def mm(outap, lhsT, rhs, start, stop, reads, writes):
            P.op("pe", lambda e: e.matmul(outap, lhsT=lhsT, rhs=rhs, start=start, stop=stop), reads=reads, writes=writes)

        def proj_fm(w, l, col0, nchunks, rhs, ntok, handler, nkc=8, r0=0):
            for b0 in range(0, nchunks, 4):
                nb = min(4, nchunks - b0)
                wap, wk = wload(wsrc(w, l, r0, nkc, col0 + b0 * 128, nb * 128), nkc, nb * 128)
                for j in range(nb):
                    ps, pk = psum()
                    for kc in range(nkc):
                        mm(ps[:, 0:ntok], wap[:, kc, j * 128:(j + 1) * 128], rhs.ap[:, kc, 0:ntok], kc == 0, kc == nkc - 1,
                           wk + rhs.k(kc), pk)
                    handler(b0 + j, ps, pk)

        def proj_tm(w, l, col0, ncols, lhs, ntiles, handler):
            for c0 in range(0, ncols, 512):
                ncl = min(512, ncols - c0)
                wap, wk = wload(wsrc(w, l, 0, 8, col0 + c0, ncl), 8, ncl)
                for t in range(ntiles):
                    ps, pk = psum()
                    for kc in range(8):
                        mm(ps[:, 0:ncl], lhs.ap[:, kc, t * 128:(t + 1) * 128], wap[:, kc, :], kc == 0, kc == 7, wk + lhs.k(kc), pk)
                    handler(t, c0, ncl, ps, pk)

        def make_h(xb, x0, ntok, Acol, Bcol, hb, h0):
            for kc in range(8):
                P.op("act", lambda e, kc=kc: e.activation(out=sq.ap[:, kc, 0:ntok], in_=xb.ap[:, kc, x0:x0 + ntok], func=AF.Square),
                     reads=xb.k(kc), writes=sq.k(kc))
            ps, pk = psum()
            for kc in range(8):
                mm(ps[:, 0:ntok], ones_b.ap, sq.ap[:, kc, 0:ntok], kc == 0, kc == 7, ones_b.k() + sq.k(kc), pk)
            P.op("act", lambda e: e.activation(out=rstd.ap[:, 0:ntok], in_=ps[:, 0:ntok], func=AF.Ln, bias=EPS, scale=1.0),
                 reads=pk, writes=rstd.k())
            P.op("act", lambda e: e.activation(out=rstd.ap[:, 0:ntok], in_=rstd.ap[:, 0:ntok], func=AF.Exp, scale=-0.5),
                 reads=rstd.k(), writes=rstd.k())
            for kc in range(8):
                tf = tmpf[kc % 2]
                P.op("dve", lambda e, kc=kc, tf=tf: e.scalar_tensor_tensor(out=tf.ap[:, 0:ntok], in0=xb.ap[:, kc, x0:x0 + ntok],
                     scalar=lv.ap[:, Acol, kc:kc + 1], in1=rstd.ap[:, 0:ntok], op0=ALU.mult, op1=ALU.mult),
                     reads=xb.k(kc) + lv.k() + rstd.k(), writes=tf.k())
                P.op("act", lambda e, kc=kc, tf=tf: e.activation(out=hb.ap[:, kc, h0:h0 + ntok], in_=tf.ap[:, 0:ntok], func=AF.Identity,
                     bias=lv.ap[:, Bcol, kc:kc + 1], scale=1.0), reads=tf.k() + lv.k(), writes=hb.k(kc))

        def load_tokens(src, nrows, tcol0):
            for t in range(nrows // 128):
                xb = xg[t % 2]
                stg = gvs[t % 2]
                P.dma("sp", stg.ap, src[t * 128:(t + 1) * 128, :], writes=stg.k())
                for hh in range(2):
                    ps, pk = psum()
                    for j in range(4):
                        kc = hh * 4 + j
                        P.op("pe", lambda e, ps=ps, j=j, kc=kc, stg=stg: e.transpose(ps[:, j * 128:(j + 1) * 128], stg.ap[:, kc * 128:(kc + 1) * 128], ident_f.ap),
                             reads=stg.k() + ident_f.k(), writes=pk)
                    P.op("dve", lambda e, ps=ps, hh=hh, xb=xb: e.tensor_copy(out=xb.ap[:, hh * 4:hh * 4 + 4, 0:128],
                         in_=ps[:, :].rearrange("p (a b) -> p a b", b=128)), reads=pk, writes=xb.k())
                P.dma("sp", xTv[:, :, tcol0 + t * 128: tcol0 + (t + 1) * 128], xb.ap[:, :, 0:128], reads=xb.k(), writes=["xT%d" % (0 if tcol0 == 0 else 1 + t // (G // 128))])

        load_tokens(ctx_in, CTX, 0)
        load_tokens(x_in, SEQ, CTX)

        cs = vecs.ap[:, VG["c"]:VG["c"] + 16].rearrange("p (two kc) -> p kc two", two=2)
        P.op("act", lambda e: e.activation(out=scb.ap, in_=cs, func=AF.Silu), reads=KV, writes=scb.k())
        def adaln(l):
                ps, pk = psum()
                for blk in range(12):
                    wap, wk = wload(wsrc(w_ada, l, 0, 8, blk * 512, 512), 8, 512)
                    for j in range(4):
                        col = (blk * 4 + j) * 2
                        for kc in range(8):
                            mm(ps[:, col:col + 2], wap[:, kc, j * 128:(j + 1) * 128], scb.ap[:, kc, :], kc == 0, kc == 7, wk + scb.k(), pk)
                psv = ps[:, 0:96].rearrange("p (c two) -> p two c", two=2)
                for two in range(2):
                    P.op("dve", lambda e, two=two, l=l, psv=psv: e.tensor_tensor(out=mods.ap[:, l * 2 + two, :], in0=psv[:, two, :], in1=V(l, "bada"), op=ALU.add),
                         reads=pk + KV, writes=mods.k(l * 2 + two))

        dump('mods', mods)
        groups = [(0, CTX, True)] + [(CTX + i * G, G, False) for i in range(NLG)]
        QS = 128.0 ** -0.5

        def layer_consts(l):
            for two in range(2):
                m = mods.ap[:, l * 2 + two, :]
                rk = mods.k(l * 2 + two) + KV
                P.op("dve", lambda e, m=m, two=two: e.scalar_tensor_tensor(out=lv.ap[:, 0 + 2 * two, :], in0=m[:, 8:16], scalar=1.0, in1=V(l, "n1g"),
                     op0=ALU.add, op1=ALU.mult), reads=rk, writes=lv.k())
                P.op("dve", lambda e, m=m, two=two: e.tensor_copy(out=lv.ap[:, 1 + 2 * two, :], in_=m[:, 0:8]), reads=rk, writes=lv.k())
                P.op("dve", lambda e, m=m, two=two: e.scalar_tensor_tensor(out=lv.ap[:, 4 + 2 * two, :], in0=m[:, 32:40], scalar=1.0, in1=V(l, "n2g"),
                     op0=ALU.add, op1=ALU.mult), reads=rk, writes=lv.k())
                P.op("dve", lambda e, m=m, two=two: e.tensor_copy(out=lv.ap[:, 5 + 2 * two, :], in_=m[:, 24:32]), reads=rk, writes=lv.k())
                P.op("dve", lambda e, m=m, two=two: e.tensor_copy(out=lv.ap[:, 8 + two, :], in_=m[:, 16:24]), reads=rk, writes=lv.k())
                P.op("dve", lambda e, m=m, two=two: e.tensor_copy(out=lv.ap[:, 10 + two, :], in_=m[:, 40:48]), reads=rk, writes=lv.k())
            P.dma("pool", wsT.ap.rearrange("p a b -> p (a b)"), wsT_in[l], writes=wsT.k())
            P.dma("pool", bsrow.ap, sgub_in[l], writes=bsrow.k())
            for d in range(2):
                P.dma("pool", w2a.ap[d * 32:d * 32 + 17, :], w2a_in[l, d], writes=w2a.k())
            for g in range(8):
                ps, pk = psum()
                mm(ps[:, 0:128], ones1.ap, wsT.ap[:, g, :], True, True, ones1.k() + wsT.k(), pk)
                mm(ps[:, 128:256], ones1.ap[0:1, :], bsrow.ap[0:1, g * 128:(g + 1) * 128], True, True, ones1.k() + bsrow.k(), pk)
                tf = tmpf[g % 2]
                P.op("act", lambda e, ps=ps, tf=tf: e.activation(out=tf.ap[:, 0:128], in_=ps[:, 128:256], func=AF.Copy), reads=pk, writes=tf.k())
                P.op("dve", lambda e, ps=ps, tf=tf, g=g: e.scalar_tensor_tensor(out=T2.ap[:, g, :], in0=ps[:, 0:128], scalar=V(l, "lnb")[:, g:g + 1],
                     in1=tf.ap[:, 0:128], op0=ALU.mult, op1=ALU.add), reads=pk + tf.k() + KV, writes=T2.k(g))
                P.op("dve", lambda e, g=g: e.tensor_scalar(out=lngB.ap[:, g, :], in0=ones1.ap, scalar1=V(l, "lng")[:, g:g + 1], scalar2=None, op0=ALU.mult),
                     reads=ones1.k() + KV, writes=lngB.k(g))

        ones1 = alloc([128, 128], BF16)
        P.op("dve", lambda e: e.memset(ones1.ap, 1.0), writes=ones1.k())

        def lr_proj(l, ntok, hb=None, lrT_=None):
            hb = hb or hT
            lrT_ = lrT_ or lrT
            wap, wk = wload(wsrc(w_in, l, 0, 8, OFF_LR, 32), 8, 32)
            ps, pk = psum()
            for d in range(2):
                for kc in range(8):
                    mm(ps[d * 32:d * 32 + 16, 0:ntok], wap[:, kc, d * 16:(d + 1) * 16], hb.ap[:, kc, 0:ntok], kc == 0, kc == 7, wk + hb.k(kc), pk)
            for d in range(2):
                P.op("act", lambda e, d=d, ps=ps: e.activation(out=lrT_.ap[d * 32:d * 32 + 16, 0:ntok], in_=ps[d * 32:d * 32 + 16, 0:ntok], func=AF.Copy),
                     reads=pk, writes=lrT_.k())

        def kv_thunks(l, ntok, hb=None, ktm_=None, vtm_=None):
            hb = hb or hT
            ktm_ = ktm_ or ktm
            vtm_ = vtm_ or vtm
            nt = ntok // 128
            th = []
            wst_ = {}
            for (off, dstk) in ((OFF_K, "k"), (OFF_V, "v0"), (OFF_V + 512, "v1")):
                for t in range(nt):
                    def f(off=off, dstk=dstk, t=t):
                        if t == 0:
                            wst_[dstk] = wload(wsrc(w_in, l, 0, 8, off, 512), 8, 512)
                        wap, wk = wst_[dstk]
                        ps, pk = psum()
                        for kc in range(8):
                            mm(ps[:, 0:512], hb.ap[:, kc, t * 128:(t + 1) * 128], wap[:, kc, :], kc == 0, kc == 7, wk + hb.k(kc), pk)
                        if dstk == "k":
                            P.op("act", lambda e: e.activation(out=ktm_.ap[:, t, :], in_=ps[:, 0:512], func=AF.Copy), reads=pk, writes=ktm_.k(t))
                        else:
                            c0 = 0 if dstk == "v0" else 512
                            P.op("dve", lambda e: e.tensor_copy(out=vtm_.ap[:, t, c0:c0 + 512], in_=ps[:, 0:512]), reads=pk, writes=vtm_.kb(t * 2048 + c0 * 2, t * 2048 + c0 * 2 + 1024))
                    th.append(f)
            return th

        def kv_proj(l, ntok):
            for f in kv_thunks(l, ntok):
                f()

        ucnt = [0]

        def gla_run(units, Smap, with_out, first_fn=None, bufs=None, fillers=()):
            U = len(units)
            ub = ucnt[0]
            ucnt[0] += U
            ktm, vtm, lrT = bufs if bufs is not None else (ktm0, vtm0, lrT0)
            fillers = list(fillers)
            st_ = [dict() for _ in units]

            def geo(u):
                d, t = units[u]
                return d, t, slice(t * 128, (t + 1) * 128), (ub + u) % 2

            def b3_(u):
                return (ub + u) % 3

            def S0(u):
                d, t, tok, b = geo(u)
                sp_ = spb[b]
                ps, pk = psum()
                mm(ps[:, :], lrT.ap[d * 32:d * 32 + 17, tok], w2a.ap[d * 32:d * 32 + 17, :], True, True, lrT.k() + w2a.k(), pk)
                P.op("act", lambda e: e.activation(out=spx.ap, in_=ps[:, :], func=AF.Exp, scale=-1.0), reads=pk, writes=spx.k())
                P.op("act", lambda e: e.activation(out=sp_.ap, in_=spx.ap, func=AF.Ln, bias=1.0, scale=1.0), reads=spx.k(), writes=sp_.k())

            def S1(u):
                d, t, tok, b = geo(u)
                sp_, kd, dc = spb[b], kdec[b3_(u)], decb[b3_(u)]
                ps2, pk2 = psum()
                mm(ps2[:, :], tcm2.ap[:, d, :], sp_.ap, True, True, tcm2.k() + sp_.k(), pk2)
                P.op("act", lambda e: e.activation(out=E1.ap, in_=ps2[:, :], func=AF.Exp), reads=pk2, writes=E1.k())
                P.op("dve", lambda e: e.tensor_tensor(out=kd.ap, in0=ktm.ap[:, t, :], in1=E1.ap, op=ALU.mult), reads=ktm.k(t) + E1.k(), writes=kd.k())
                if with_out:
                    qd, ki = qdT[b3_(u)], kiT[b]
                    ps3, pk3 = psum()
                    for h in range(4):
                        mm(ps3[:, h * 128:(h + 1) * 128], sp_.ap[:, h * 128:(h + 1) * 128], tbx2.ap[:, d, :], True, True, sp_.k() + tbx2.k(), pk3)
                    p3v = ps3[:, :].rearrange("p (h j c) -> p h j c", j=2, c=64)
                    lastc = 63 if d == 0 else 0
                    P.op("act", lambda e: e.activation(out=dc.ap, in_=p3v[:, :, :, lastc], func=AF.Exp), reads=pk3, writes=dc.k())
                    P.op("act", lambda e: e.activation(out=EqT.ap, in_=ps3[:, :].rearrange("p (h c) -> p h c", c=128), func=AF.Exp), reads=pk3, writes=EqT.k())
                    P.op("act", lambda e: e.activation(out=EkT.ap, in_=ps3[:, :].rearrange("p (h c) -> p h c", c=128), func=AF.Exp, scale=-1.0), reads=pk3, writes=EkT.k())
                    P.op("dve", lambda e: e.scalar_tensor_tensor(out=qd.ap, in0=qT.ap[:, :, tok], scalar=QS, in1=EqT.ap, op0=ALU.mult, op1=ALU.mult),
                         reads=qT.k() + EqT.k(), writes=qd.k())
                    P.op("dve", lambda e: e.tensor_tensor(out=ki.ap, in0=kT.ap[:, :, tok], in1=EkT.ap, op=ALU.mult), reads=kT.k() + EkT.k(), writes=ki.k())
                else:
                    ps3, pk3 = psum()
                    for h in range(4):
                        for j in range(2):
                            lc = j * 64 + (63 if d == 0 else 0)
                            mm(ps3[:, h * 2 + j:h * 2 + j + 1], sp_.ap[:, h * 128:(h + 1) * 128], tbx2.ap[:, d, lc:lc + 1], True, True, sp_.k() + tbx2.k(), pk3)
                    P.op("act", lambda e: e.activation(out=dc.ap, in_=ps3[:, 0:8].rearrange("p (h j) -> p h j", j=2), func=AF.Exp), reads=pk3, writes=dc.k())

            def S2(u):
                if not with_out:
                    return
                d, t, tok, b = geo(u)
                qd, ki, sc = qdT[b3_(u)], kiT[b], scT[b]
                ps4, pk4 = psum()
                for h in range(4):
                    mm(ps4[:, h * 128:(h + 1) * 128], ki.ap[:, h, :], qd.ap[:, h, :], True, True, ki.k() + qd.k(), pk4)
                P.op("dve", lambda e: e.tensor_tensor(out=sc.ap, in0=ps4[:, :].rearrange("p (h c) -> p h c", c=128), in1=mk2.ap[:, d, :, :], op=ALU.mult),
                     reads=pk4 + mk2.k(), writes=sc.k())

            def S3chunk(u, jc):
                d, t, tok, b = geo(u)
                kd, dc = kdec[b3_(u)], decb[b3_(u)]
                Sw, Sb = Smap[d]
                par = slice(jc * 64, jc * 64 + 64)
                if with_out:
                    qd = qdT[b3_(u)]
                    for hh in range(2):
                        ps7, pk7 = psum()
                        for j in range(2):
                            h = hh * 2 + j
                            mm(ps7[par, j * 256:(j + 1) * 256], qd.ap[:, h, jc * 64:(jc + 1) * 64], Sb.ap[:, h * 256:(h + 1) * 256], True, True, qd.k() + Sb.k(), pk7)
                        osl = o1.ap[par, t, hh * 512:(hh + 1) * 512]
                        ok_ = o1.kb(t * 4096 + hh * 2048, t * 4096 + hh * 2048 + 2048)
                        P.op("dve", lambda e, ps7=ps7, osl=osl: e.tensor_tensor(out=osl, in0=ps7[par, :], in1=osl, op=ALU.add), reads=pk7 + ok_, writes=ok_)
                for hh in range(2):
                    ps6, pk6 = psum()
                    for j in range(2):
                        h = hh * 2 + j
                        mm(ps6[:, j * 256:(j + 1) * 256], kd.ap[par, h * 128:(h + 1) * 128], vtm.ap[par, t, h * 256:(h + 1) * 256], True, True, kd.k() + vtm.k(t), pk6)
                    for j in range(2):
                        h = hh * 2 + j
                        hk = Sw.kb(h * 1024, h * 1024 + 1024)
                        P.op("dve", lambda e, h=h, j=j, ps6=ps6: e.scalar_tensor_tensor(out=Sw.ap[:, h * 256:(h + 1) * 256], in0=Sw.ap[:, h * 256:(h + 1) * 256], scalar=dc.ap[:, h, jc:jc + 1],
                             in1=ps6[:, j * 256:(j + 1) * 256], op0=ALU.mult, op1=ALU.add), reads=hk + dc.k() + pk6, writes=hk)
                if with_out:
                    P.op("act", lambda e: e.activation(out=Sb.ap, in_=Sw.ap, func=AF.Copy), reads=Sw.k(), writes=Sb.k())

            def S3a(u):
                d, t, tok, b = geo(u)
                if with_out:
                    sc = scT[b]
                    first = first_fn(d, t)
                    for hh in range(2):
                        ps5, pk5 = psum()
                        for j in range(2):
                            h = hh * 2 + j
                            mm(ps5[:, j * 256:(j + 1) * 256], sc.ap[:, h, :], vtm.ap[:, t, h * 256:(h + 1) * 256], True, True, sc.k() + vtm.k(t), pk5)
                        osl = o1.ap[:, t, hh * 512:(hh + 1) * 512]
                        ok_ = o1.kb(t * 4096 + hh * 2048, t * 4096 + hh * 2048 + 2048)
                        if first:
                            P.op("act", lambda e, ps5=ps5, osl=osl: e.activation(out=osl, in_=ps5[:, :], func=AF.Copy), reads=pk5, writes=ok_)
                        else:
                            P.op("dve", lambda e, ps5=ps5, osl=osl: e.tensor_tensor(out=osl, in0=ps5[:, :], in1=osl, op=ALU.add), reads=pk5 + ok_, writes=ok_)
                S3chunk(u, 0 if d == 0 else 1)

            def S3b(u):
                d = units[u][0]
                S3chunk(u, 1 if d == 0 else 0)

            nsteps = U + 4
            perstep = (len(fillers) + nsteps - 1) // nsteps if fillers else 0
            for step in range(nsteps):
                ua, ub_ = step - 3, step - 4
                same = (0 <= ua < U) and (0 <= ub_ < U) and units[ua][0] == units[ub_][0]
                order = [(S3b, 4), (S3a, 3)] if same else [(S3a, 3), (S3b, 4)]
                order += [(S2, 2), (S1, 1), (S0, 0)]
                for fn_, s_ in order:
                    u = step - s_
                    if 0 <= u < U:
                        fn_(u)
                for _ in range(perstep):
                    if fillers:
                        fillers.pop(0)()
            while fillers:
                fillers.pop(0)()

        def interleave(n):
            units = []
            for i in range(n):
                units.append((0, i))
                units.append((1, n - 1 - i))
            return units

        def zero_state(Sw, Sb):
            P.op("dve", lambda e: e.memset(Sw.ap, 0.0), writes=Sw.k())
            P.op("dve", lambda e: e.memset(Sb.ap, 0.0), writes=Sb.k())

        def load_x(gi, slot):
            c0, ntok, isc = groups[gi]
            P.dma("sp", xg[slot].ap[:, :, 0:ntok], xTv[:, :, c0:c0 + ntok], reads=["xT%d" % gi], writes=xg[slot].k())

        def store_x(gi, slot):
            c0, ntok, isc = groups[gi]
            P.dma("sp", xTv[:, :, c0:c0 + ntok], xg[slot].ap[:, :, 0:ntok], reads=xg[slot].k(), writes=["xT%d" % gi])

        def pass1(l):
            zero_state(S1w, S1b)
            zero_state(S2w, S2b)
            P.op("dve", lambda e: e.memset(lrT2.ap, 1.0), writes=lrT2.k())
            sets = [(hT, ktm, vtm, lrT), (hF, ktm2, vtm2, lrT2)]

            def prep_thunks(gi):
                c0, ntok, isc = groups[gi]
                slot = gi % 2
                hb, k_, v_, lr_ = sets[gi % 2]
                th = []

                def f0():
                    load_x(gi, slot)
                    make_h(xg[slot], 0, ntok, 2 if isc else 0, 3 if isc else 1, hb, 0)
                th.append(f0)
                th.extend(kv_thunks(l, ntok, hb, k_, v_))
                th.append(lambda: lr_proj(l, ntok, hb, lr_))

                def fst():
                    P.dma("sp", ksc[gi], k_.ap.rearrange("p a b -> p (a b)"), reads=k_.k(), writes=["ksc%d" % gi])
                    P.dma("sp", vsc[gi], v_.ap.rearrange("p a b -> p (a b)"), reads=v_.k(), writes=["vsc%d" % gi])
                    P.dma("sp", lsc[gi], lr_.ap, reads=lr_.k(), writes=["lsc%d" % gi])
                th.append(fst)
                return th

            for f in prep_thunks(0):
                f()
            for gi in range(len(groups)):
                c0, ntok, isc = groups[gi]
                hb, k_, v_, lr_ = sets[gi % 2]
                fl = prep_thunks(gi + 1) if gi + 1 < len(groups) else []
                if not isc:
                    P.dma("sp", s1save[gi - 1], S1w.ap, reads=S1w.k(), writes=["s1save%d" % (gi - 1)])
                if isc:
                    gla_run(interleave(ntok // 128), {0: (S1w, S1b), 1: (S2w, S2b)}, False, bufs=(k_, v_, lr_), fillers=fl)
                else:
                    gla_run([(0, t_) for t_ in range(ntok // 128)], {0: (S1w, S1b)}, False, bufs=(k_, v_, lr_), fillers=fl)

        def mixer(l, gi, slot):
            c0, ntok, isc = groups[gi]
            nt = ntok // 128
            xb = xg[slot]
            load_x(gi, slot)
            make_h(xb, 0, ntok, 2 if isc else 0, 3 if isc else 1, hT, 0)

            def hu(j, ps, pk):
                P.op("act", lambda e: e.activation(out=uT.ap[:, j, 0:ntok], in_=ps[:, 0:ntok], func=AF.Gelu), reads=pk, writes=uT.k(j))
            proj_fm(w_in, l, 0, 8, hT, ntok, hu)

            wblocks = []
            for cb_ in range(2):
                wblocks.append(wload(wsrc(w_in, l, 0, 8, 1024 + cb_ * 512, 512), 8, 512, slot=cb_))
            gst = {}

            def gate_proj(jj, off, dst):
                if jj % 4 == 0:
                    gst["w"] = wload(wsrc(w_in, l, 0, 8, off + jj * 128, 512), 8, 512, slot=2)
                wa, wka = gst["w"]
                j = jj % 4
                ps, pk = psum()
                for kc in range(8):
                    mm(ps[:, 0:ntok], wa[:, kc, j * 128:(j + 1) * 128], hT.ap[:, kc, 0:ntok], kc == 0, kc == 7, wka + hT.k(kc), pk)
                P.op("act", lambda e: e.activation(out=dst.ap[:, jj, 0:ntok], in_=ps[:, 0:ntok], func=AF.Sigmoid), reads=pk, writes=dst.k(jj))

            def vs_proj(t):
                gb = gvs[t % 2]
                st = sttA[t % 2]
                for cb_ in range(2):
                    wap, wk = wblocks[cb_]
                    ps, pk = psum()
                    for kc in range(8):
                        mm(ps[:, 0:512], hT.ap[:, kc, t * 128:(t + 1) * 128], wap[:, kc, :], kc == 0, kc == 7, wk + hT.k(kc), pk)
                    c0_ = cb_ * 512
                    gk = gb.kb(c0_ * 4, c0_ * 4 + 2048)
                    P.op("act", lambda e, ps=ps, c0_=c0_, cb_=cb_: e.activation(out=gb.ap[:, c0_:c0_ + 512], in_=ps[:, 0:512], func=AF.Gelu, accum_out=st.ap[:, 2 * cb_:2 * cb_ + 1]),
                         reads=pk, writes=gk + st.k())
                    P.op("act", lambda e, c0_=c0_, cb_=cb_: e.activation(out=tmpf[cb_].ap[:, 0:512], in_=gb.ap[:, c0_:c0_ + 512], func=AF.Square, accum_out=st.ap[:, 2 * cb_ + 1:2 * cb_ + 2]),
                         reads=gk, writes=tmpf[cb_].k() + st.k())

            def ln_gate(t):
                gb = gvs[t % 2]
                st = sttA[t % 2]
                P.op("dve", lambda e: e.tensor_tensor(out=st.ap[:, 4:6], in0=st.ap[:, 0:2], in1=st.ap[:, 2:4], op=ALU.add), reads=st.k(), writes=st.k())
                P.op("dve", lambda e: e.tensor_scalar(out=st.ap[:, 4:6], in0=st.ap[:, 4:6], scalar1=1.0 / 1024.0, scalar2=None, op0=ALU.mult), reads=st.k(), writes=st.k())
                P.op("dve", lambda e: e.scalar_tensor_tensor(out=st.ap[:, 6:7], in0=st.ap[:, 4:5], scalar=st.ap[:, 4:5], in1=st.ap[:, 5:6], op0=ALU.mult, op1=ALU.subtract),
                     reads=st.k(), writes=st.k())
                P.op("act", lambda e: e.activation(out=st.ap[:, 7:8], in_=st.ap[:, 6:7], func=AF.Ln, bias=EPS, scale=-1.0), reads=st.k(), writes=st.k())
                P.op("act", lambda e: e.activation(out=st.ap[:, 7:8], in_=st.ap[:, 7:8], func=AF.Exp, scale=-0.5), reads=st.k(), writes=st.k())
                P.op("dve", lambda e: e.tensor_scalar(out=vh.ap[:, t, :], in0=gb.ap, scalar1=st.ap[:, 4:5], scalar2=st.ap[:, 7:8], op0=ALU.subtract, op1=ALU.mult),
                     reads=gb.k() + st.k(), writes=vh.k(t))
                for hh in range(2):
                    ps, pk = psum()
                    for j in range(4):
                        g = hh * 4 + j
                        mm(ps[:, j * 128:(j + 1) * 128], vh.ap[:, t, g * 128:(g + 1) * 128], wsT.ap[:, g, :], True, True, vh.k(t) + wsT.k(), pk)
                    gs = slice(hh * 4, hh * 4 + 4)
                    tv = gb.ap[:, hh * 512:(hh + 1) * 512].rearrange("p (a b) -> p a b", b=128)
                    tk = gb.kb(hh * 2048, hh * 2048 + 2048)
                    P.op("dve", lambda e, ps=ps, gs=gs, tv=tv: e.tensor_tensor(out=tv, in0=ps[:, :].rearrange("p (a b) -> p a b", b=128), in1=lngB.ap[:, gs, :], op=ALU.mult),
                         reads=pk + lngB.k(), writes=tk)
                    P.op("dve", lambda e, gs=gs, tv=tv: e.tensor_tensor(out=tv, in0=tv, in1=T2.ap[:, gs, :], op=ALU.add), reads=tk + T2.k(), writes=tk)
                    P.op("dve", lambda e, gs=gs, tv=tv: e.tensor_tensor(out=aT.ap[:, gs, t * 128:(t + 1) * 128], in0=tv, in1=uT.ap[:, gs, t * 128:(t + 1) * 128], op=ALU.mult),
                         reads=tk + uT.k(hh * 4, 4), writes=aT.k(hh * 4, 4))

            vs_proj(0)
            gq = list(range(8))
            per = (8 + nt - 1) // nt
            for t in range(nt):
                if t + 1 < nt:
                    vs_proj(t + 1)
                for _ in range(per):
                    if gq:
                        gate_proj(gq.pop(0), OFF_GA, mA)
                ln_gate(t)
            wn[0] = 0
            for b0 in range(0, 8, 4):
                wb_, wkb = wload(wsrc(w_bra, l, 0, 8, b0 * 128, 512), 8, 512)
                for j in range(4):
                    jj = b0 + j
                    ps2, pk2 = psum()
                    for kc in range(8):
                        mm(ps2[:, 0:ntok], wb_[:, kc, j * 128:(j + 1) * 128], aT.ap[:, kc, 0:ntok], kc == 0, kc == 7, wkb + aT.k(kc), pk2)
                    P.op("dve", lambda e, ps2=ps2, jj=jj: e.tensor_tensor(out=mA.ap[:, jj, 0:ntok], in0=ps2[:, 0:ntok], in1=mA.ap[:, jj, 0:ntok], op=ALU.mult),
                         reads=pk2 + mA.k(jj), writes=mA.k(jj))
            dump('m_uT', uT); dump('m_vh', vh); dump('m_aT', aT); dump('m_mA', mA); dump('m_T2', T2)
            P.dma("pool", ktm.ap.rearrange("p a b -> p (a b)"), ksc[gi], reads=["ksc%d" % gi], writes=ktm.k())
            P.dma("pool", vtm.ap.rearrange("p a b -> p (a b)"), vsc[gi], reads=["vsc%d" % gi], writes=vtm.k())
            P.dma("pool", lrT.ap, lsc[gi], reads=["lsc%d" % gi], writes=lrT.k())

            def hq(j, ps, pk):
                P.op("act", lambda e: e.activation(out=qT.ap[:, j, 0:ntok], in_=ps[:, 0:ntok], func=AF.Copy), reads=pk, writes=qT.k(j))
            proj_fm(w_in, l, OFF_Q, 4, hT, ntok, hq)

            def hkT(j, ps, pk):
                P.op("dve", lambda e: e.tensor_copy(out=kT.ap[:, j, 0:ntok], in_=ps[:, 0:ntok]), reads=pk, writes=kT.k(j))
            proj_fm(w_in, l, OFF_K, 4, hT, ntok, hkT)
            nch = ntok // 64
            if isc:
                zero_state(S1w, S1b)
            else:
                P.dma("sp", S1w.ap, s1save[gi - 1], reads=["s1save%d" % (gi - 1)], writes=S1w.k())
                P.op("act", lambda e: e.activation(out=S1b.ap, in_=S1w.ap, func=AF.Copy), reads=S1w.k(), writes=S1b.k())
            if isc:
                zero_state(S2w, S2b)
            gla_run(interleave(nt), {0: (S1w, S1b), 1: (S2w, S2b)}, True,
                    first_fn=lambda d, t_: (t_ < nt // 2) if d == 0 else (t_ >= nt // 2))
            dump('m_o1', o1); dump('m_qT', qT); dump('m_S1', S1w); dump('m_S2', S2w)
            wr = [wload(wsrc(w_in, l, 0, 8, OFF_R + cb_ * 512, 512), 8, 512, slot=cb_) for cb_ in range(2)]

            def r_proj(t):
                rs_ = rs[t % 2]
                for cb_ in range(2):
                    wap, wk = wr[cb_]
                    ps, pk = psum()
                    for kc in range(8):
                        mm(ps[:, 0:512], hT.ap[:, kc, t * 128:(t + 1) * 128], wap[:, kc, :], kc == 0, kc == 7, wk + hT.k(kc), pk)
                    P.op("act", lambda e, ps=ps, cb_=cb_: e.activation(out=rs_.ap[:, cb_ * 512:(cb_ + 1) * 512], in_=ps[:, 0:512], func=AF.Silu), reads=pk,
                         writes=rs_.kb(cb_ * 1024, cb_ * 1024 + 1024))

            def ob_norm(t):
                rs_, ob_, st = rs[t % 2], ob[t % 2], sttB[t % 2]
                for h in range(4):
                    P.op("act", lambda e, h=h: e.activation(out=tmpf[h % 2].ap[:, 0:256], in_=o1.ap[:, t, h * 256:(h + 1) * 256], func=AF.Square, accum_out=st.ap[:, h:h + 1]),
                         reads=o1.k(t), writes=tmpf[h % 2].k() + st.k())
                P.op("act", lambda e: e.activation(out=st.ap[:, 4:8], in_=st.ap[:, 0:4], func=AF.Ln, bias=EPS, scale=1.0 / 256.0), reads=st.k(), writes=st.k())
                P.op("act", lambda e: e.activation(out=st.ap[:, 4:8], in_=st.ap[:, 4:8], func=AF.Exp, scale=-0.5), reads=st.k(), writes=st.k())
                for h in range(4):
                    P.op("dve", lambda e, h=h: e.scalar_tensor_tensor(out=ob_.ap[:, h * 256:(h + 1) * 256], in0=o1.ap[:, t, h * 256:(h + 1) * 256], scalar=st.ap[:, 4 + h:5 + h],
                         in1=rs_.ap[:, h * 256:(h + 1) * 256], op0=ALU.mult, op1=ALU.mult), reads=o1.k(t) + st.k() + rs_.k(), writes=ob_.k())

            def ob_transpose(t):
                ob_ = ob[t % 2]
                for j in range(8):
                    P.op("pe", lambda e, j=j: e.transpose(pst[:, j * 128:(j + 1) * 128], ob_.ap[:, j * 128:(j + 1) * 128], ident_b.ap), reads=ob_.k() + ident_b.k(), writes=["pst"])
                for j in range(8):
                    P.op("act", lambda e, j=j: e.activation(out=obT.ap[:, j, t * 128:(t + 1) * 128], in_=pst[:, j * 128:(j + 1) * 128], func=AF.Copy, scale=V(l, "gng")[:, j:j + 1]),
                         reads=["pst"] + KV, writes=obT.k(j))

            r_proj(0)
            gq = list(range(8))
            for t in range(nt):
                if t + 1 < nt:
                    r_proj(t + 1)
                ob_norm(t)
                for _ in range(per):
                    if gq:
                        gate_proj(gq.pop(0), OFF_GB, mg)
                ob_transpose(t)
            wn[0] = 0
            for b0 in range(0, 8, 4):
                wb_, wkb = wload(wsrc(w_brb, l, 0, 8, b0 * 128, 512), 8, 512)
                for j in range(4):
                    jj = b0 + j
                    ps2, pk2 = psum()
                    for kc in range(8):
                        mm(ps2[:, 0:ntok], wb_[:, kc, j * 128:(j + 1) * 128], obT.ap[:, kc, 0:ntok], kc == 0, kc == 7, wkb + obT.k(kc), pk2)
                    tf = tmpf[jj % 2]
                    P.op("dve", lambda e, ps2=ps2, tf=tf, jj=jj: e.tensor_tensor(out=tf.ap[:, 0:ntok], in0=ps2[:, 0:ntok], in1=mg.ap[:, jj, 0:ntok], op=ALU.mult),
                         reads=pk2 + mg.k(jj), writes=tf.k())
                    P.op("dve", lambda e, tf=tf, jj=jj: e.tensor_tensor(out=mg.ap[:, jj, 0:ntok], in0=tf.ap[:, 0:ntok], in1=mA.ap[:, jj, 0:ntok], op=ALU.add),
                         reads=tf.k() + mA.k(jj), writes=mg.k(jj))
            dump('m_obT', obT); dump('m_mg', mg)
            gcol = 9 if isc else 8

            def ho(j, ps, pk):
                P.op("dve", lambda e: e.scalar_tensor_tensor(out=xb.ap[:, j, 0:ntok], in0=ps[:, 0:ntok], scalar=lv.ap[:, gcol, j:j + 1], in1=xb.ap[:, j, 0:ntok],
                     op0=ALU.mult, op1=ALU.add), reads=pk + lv.k() + xb.k(j), writes=xb.k(j))
            proj_fm(w_o, l, 0, 8, mg, ntok, ho)
            dump('m_xmid', xb)

        def diag_build(l, cc, ntaps):
            d_ = dg[cc % 2]
            for j in ntaps:
                P.op("dve", lambda e, j=j: e.tensor_scalar(out=d_.ap[:, j, :], in0=ident_b.ap, scalar1=V(l, "cw")[:, cc * 9 + j:cc * 9 + j + 1], scalar2=None, op0=ALU.mult),
                     reads=ident_b.k() + KV, writes=d_.k())
            return d_

        def ffn(l, gi, slot, last):
            c0, ntok, isc = groups[gi]
            xb = xg[slot]
            lower = (not isc) and gi > 1
            nh = ntok + (64 if lower else 0)
            make_h(xb, 0, ntok, 6 if isc else 4, 7 if isc else 5, hF, 0)
            if lower:
                ob_ = xg[1 - slot]
                make_h(ob_, G - 64, 64, 4, 5, hF, G)
            wst = {}
            if isc:
                for i_ in range(2):
                    P.op("dve", lambda e, i_=i_: e.memset(apc[i_].ap, 0.0), writes=apc[i_].k())
            else:
                for i_ in range(2):
                    P.op("dve", lambda e, i_=i_: e.memset(apad[i_].ap, 0.0), writes=apad[i_].k())
            taps = [3, 4, 5] if isc else list(range(9))

            def a_proj(cc):
                ap_ = apad[cc % 2]
                if cc % 4 == 0:
                    ncb = min(4, NCC - cc)
                    wst["a"] = wload(wsrc(w_up, l, 0, 8, cc * 128, ncb * 128), 8, ncb * 128)
                wap4, wk = wst["a"]
                wap = wap4[:, :, (cc % 4) * 128:(cc % 4 + 1) * 128]
                diag_build(l, cc, taps)
                ps, pk = psum()
                for kc in range(8):
                    mm(ps[:, 0:ntok], wap[:, kc, :], hF.ap[:, kc, 0:ntok], kc == 0, kc == 7, wk + hF.k(kc), pk)
                if isc:
                    ac = apc[cc % 2]
                    P.op("act", lambda e: e.activation(out=ac.ap[:, 1:257], in_=ps[:, 0:256], func=AF.Copy), reads=pk, writes=ac.k())
                    return
                if lower:
                    psh, pkh = psum()
                    for kc in range(8):
                        mm(psh[:, 0:64], wap[:, kc, :], hF.ap[:, kc, G:G + 64], kc == 0, kc == 7, wk + hF.k(kc), pkh)
                P.op("act", lambda e: e.activation(out=ap_.ap[:, 1:9, 1:65], in_=ps[:, 0:512].rearrange("p (r c) -> p r c", c=64), func=AF.Copy), reads=pk, writes=ap_.k())
                P.op("dve", lambda e: e.tensor_copy(out=ap_.ap[:, 9, 1:65], in_=a_save.ap[:, cc, :]), reads=a_save.k(cc) + ap_.k(), writes=ap_.k())
                P.op("dve", lambda e: e.tensor_copy(out=a_save.ap[:, cc, :], in_=ap_.ap[:, 1, 1:65]), reads=ap_.k() + a_save.k(cc), writes=a_save.k(cc))
                if lower:
                    P.op("act", lambda e: e.activation(out=ap_.ap[:, 0, 1:65], in_=psh[:, 0:64], func=AF.Copy), reads=pkh + ap_.k(), writes=ap_.k())

            def val_proj(cc):
                if cc % 4 == 0:
                    ncb = min(4, NCC - cc)
                    wst["v"] = wload(wsrc(w_up, l, 0, 8, DFF + cc * 128, ncb * 128), 8, ncb * 128)
                wvp4, wvk = wst["v"]
                wvp = wvp4[:, :, (cc % 4) * 128:(cc % 4 + 1) * 128]
                psv, pkv = psum()
                for kc in range(8):
                    mm(psv[:, 0:ntok], wvp[:, kc, :], hF.ap[:, kc, 0:ntok], kc == 0, kc == 7, wvk + hF.k(kc), pkv)
                return psv, pkv

            def conv(cc, psv, pkv):
                d_ = dg[cc % 2]
                psc, pkc = psum()
                if isc:
                    ac = apc[cc % 2]
                    for i, j in enumerate(taps):
                        mm(psc[:, 0:256], d_.ap[:, j, :], ac.ap[:, j - 3:j - 3 + 256], i == 0, i == 2, d_.k() + ac.k(), pkc)
                else:
                    ap_ = apad[cc % 2]
                    for j in range(9):
                        dr, dc_ = j // 3, j % 3
                        mm(psc[:, 0:512].rearrange("p (r c) -> p r c", c=64), d_.ap[:, j, :], ap_.ap[:, dr:dr + 8, dc_:dc_ + 64], j == 0, j == 8, d_.k() + ap_.k(), pkc)
                ga_ = gact[cc % 2]
                P.op("act", lambda e: e.activation(out=ga_.ap[:, 0:ntok], in_=psc[:, 0:ntok], func=AF.Gelu, bias=V(l, "cb")[:, cc:cc + 1], scale=1.0),
                     reads=pkc + KV, writes=ga_.k())
                P.op("dve", lambda e: e.tensor_tensor(out=gv.ap[:, cc, 0:ntok], in0=psv[:, 0:ntok], in1=ga_.ap[:, 0:ntok], op=ALU.mult),
                     reads=pkv + ga_.k(), writes=gv.k(cc))

            a_proj(0)
            for cc in range(NCC):
                if cc + 1 < NCC:
                    a_proj(cc + 1)
                psv, pkv = val_proj(cc)
                conv(cc, psv, pkv)
            dump('f_gv', gv); dump('f_hF', hF)
            gcol = 11 if isc else 10
            for jb in range(2):
                banks = [psum() for _ in range(4)]
                for (k0, nk) in ((0, 8), (8, 8), (16, 6)):
                    wap, wk = wload(wsrc(w_dn, l, k0 * 128, nk, jb * 512, 512), nk, 512)
                    for j4 in range(4):
                        ps, pk = banks[j4]
                        for kk in range(nk):
                            cc = k0 + kk
                            mm(ps[:, 0:ntok], wap[:, kk, j4 * 128:(j4 + 1) * 128], gv.ap[:, cc, 0:ntok], cc == 0, cc == NCC - 1, wk + gv.k(cc), pk)
                for j4 in range(4):
                    ps, pk = banks[j4]
                    j = jb * 4 + j4
                    P.op("dve", lambda e, ps=ps, j=j: e.scalar_tensor_tensor(out=xb.ap[:, j, 0:ntok], in0=ps[:, 0:ntok], scalar=lv.ap[:, gcol, j:j + 1], in1=xb.ap[:, j, 0:ntok],
                         op0=ALU.mult, op1=ALU.add), reads=pk + lv.k() + xb.k(j), writes=xb.k(j))
            dump('f_x', xb)
            if not last:
                store_x(gi, slot)
            elif not isc:
                for kc in range(8):
                    P.op("act", lambda e, kc=kc: e.activation(out=sq.ap[:, kc, 0:ntok], in_=xb.ap[:, kc, 0:ntok], func=AF.Square), reads=xb.k(kc), writes=sq.k(kc))
                ps, pk = psum()
                for kc in range(8):
                    mm(ps[:, 0:ntok], ones_b.ap, sq.ap[:, kc, 0:ntok], kc == 0, kc == 7, ones_b.k() + sq.k(kc), pk)
                P.op("act", lambda e: e.activation(out=rstd.ap[:, 0:ntok], in_=ps[:, 0:ntok], func=AF.Ln, bias=EPS, scale=1.0), reads=pk, writes=rstd.k())
                P.op("act", lambda e: e.activation(out=rstd.ap[:, 0:ntok], in_=rstd.ap[:, 0:ntok], func=AF.Exp, scale=-0.5), reads=rstd.k(), writes=rstd.k())
                fg = vecs.ap[:, VG["fng"]:VG["fng"] + 8]
                for kc in range(8):
                    P.op("dve", lambda e, kc=kc: e.scalar_tensor_tensor(out=xb.ap[:, kc, 0:ntok], in0=xb.ap[:, kc, 0:ntok], scalar=fg[:, kc:kc + 1], in1=rstd.ap[:, 0:ntok],
                         op0=ALU.mult, op1=ALU.mult), reads=xb.k(kc) + KV + rstd.k(), writes=xb.k(kc))
                for t in range(ntok // 128):
                    for hh in range(2):
                        ps, pk = psum()
                        for j in range(4):
                            kc = hh * 4 + j
                            P.op("pe", lambda e, ps=ps, j=j, kc=kc, t=t: e.transpose(ps[:, j * 128:(j + 1) * 128], xb.ap[:, kc, t * 128:(t + 1) * 128], ident_f.ap),
                                 reads=xb.k(kc) + ident_f.k(), writes=pk)
                        fin = fin2[t % 2]
                        P.op("act", lambda e, ps=ps, hh=hh, fin=fin: e.activation(out=fin.ap[:, hh * 512:(hh + 1) * 512], in_=ps[:, 0:512], func=AF.Copy), reads=pk,
                             writes=fin.kb(hh * 2048, hh * 2048 + 2048))
                    r0 = c0 - CTX + t * 128
                    P.dma("sp", out[r0:r0 + 128, :], fin2[t % 2].ap, reads=fin2[t % 2].k(), writes=["out%d" % r0])

        ALLW = ["ada", "in", "bra", "brb", "o", "up", "dn"]
        cast_weights(0, ALLW)
        adaln(0)
        for l in range(n_layers):
            last = l == n_layers - 1
            layer_consts(l)
            if not last:
                cast_weights(l + 1, ALLW)
            pass1(l)
            P.op("act", lambda e: e.activation(out=S2b.ap, in_=S2w.ap, func=AF.Copy), reads=S2w.k(), writes=S2b.k())
            P.op("dve", lambda e: e.memset(a_save.ap, 0.0), writes=a_save.k())
            order = list(range(NLG, 0, -1))
            slot_of = {}
            for i, gi in enumerate(order):
                slot = i % 2
                slot_of[gi] = slot
                mixer(l, gi, slot)
                if i >= 1:
                    ffn(l, order[i - 1], slot_of[order[i - 1]], last)
            ffn(l, order[-1], slot_of[order[-1]], last)
            if not last:
                mixer(l, 0, 0)
                ffn(l, 0, 0, last)
                adaln(l + 1)
        P.final_wait_all("sp")
        P.emit()
    return nc


def _pack_vecs(inp, b):
    v = np.zeros((128, NV), np.float32)

    def put(col, arr):
        a = np.asarray(arr, np.float32).reshape(-1, 128).T
        v[:, col:col + a.shape[1]] = a

    for l in range(DEPTH):
        base = l * VPL
        put(base + VL["n1g"][0], inp["norm1_g"][l])
        put(base + VL["n2g"][0], inp["norm2_g"][l])
        put(base + VL["lng"][0], inp["sgu_ln_g"][l])
        put(base + VL["lnb"][0], inp["sgu_ln_b"][l])
        put(base + VL["gng"][0], inp["gla_norm_g"][l])
        put(base + VL["cb"][0], inp["ffn_conv_b"][l])
        cw = np.asarray(inp["ffn_conv_w"][l], np.float32).reshape(9, NCC, 128)
        v[:, base + VL["cw"][0]: base + VL["cw"][0] + NCC * 9] = cw.transpose(2, 1, 0).reshape(128, NCC * 9)
        put(base + VL["bada"][0], inp["b_ada"][l])
    put(VG["fng"], inp["final_norm_g"])
    put(VG["c"], inp["c"][b])
    put(VG["cctx"], inp["c_ctx"])
    return v


_NC_CACHE = {}


def kernel(**inp):
    inp = {k: np.asarray(v) for k, v in inp.items()}
    if "nc" not in _NC_CACHE:
        _NC_CACHE["nc"] = build_program()
    nc = _NC_CACHE["nc"]
    f32 = lambda a: np.ascontiguousarray(a, dtype=np.float32)
    wsT = f32(np.transpose(inp["sgu_w"], (0, 3, 1, 2)).reshape(DEPTH, 128, 1024))
    sgub = f32(inp["sgu_b"].reshape(DEPTH, 1, 1024))
    w2a = f32(np.concatenate([inp["gla_w2"], inp["gla_b2"][:, :, None, :]], axis=2))
    shared = {
        "w_ada": f32(inp["w_ada"]), "w_in": f32(inp["w_in"]), "wsT": wsT, "sgub": sgub, "w2a": w2a,
        "w_br_a": f32(inp["w_br_a"]), "w_br_b": f32(inp["w_br_b"]), "w_o": f32(inp["w_o"]),
        "ffn_w_up": f32(inp["ffn_w_up"]), "ffn_w_down": f32(inp["ffn_w_down"]),
    }
    in_maps = []
    for r in range(8):
        b = r % 4
        m = dict(shared)
        m["x_in"] = f32(inp["x"][b])
        m["ctx_in"] = f32(inp["ctx"][b])
        m["vecs"] = _pack_vecs(inp, b)
        in_maps.append(m)
    res = run_bass_kernel_spmd(nc, in_maps, core_ids=list(range(8)))
    outs = [np.asarray(res.results[b]["out"], np.float32) for b in range(4)]
    return np.stack(outs, axis=0)
```
